# Optimizing a Trainium2 kernel written in Bass

```python
import jax, jax.numpy as jnp
from jax import lax
import numpy as np

D_MODEL = 2048
BATCH = 16
SEQ = 2048
DEPTH = 2

GRID_W = 64
CTX_LEN = 256
N_MOD = 9
FFN_DIM = 5632
HEAD_DIM = 128
HALF_ROT = HEAD_DIM // 2
ATTN_HEADS = 8
ATTN_KV_HEADS = 2
ATTN_GROUP = ATTN_HEADS // ATTN_KV_HEADS
ATTN_WIDTH = ATTN_HEADS * HEAD_DIM
KV_WIDTH = ATTN_KV_HEADS * HEAD_DIM
RET_HEADS = 4
RET_DK = 128
RET_DV = 128
RET_WIDTH = RET_HEADS * RET_DK
FOURIER_GROUPS = 4
FOURIER_GROUP_DIM = 128
FOURIER_WIDTH = FOURIER_GROUPS * FOURIER_GROUP_DIM
IN_WIDTH = ATTN_WIDTH + 2 * KV_WIDTH + 4 * RET_WIDTH + FOURIER_WIDTH
IN_SPLITS = (
    ATTN_WIDTH,
    ATTN_WIDTH + KV_WIDTH,
    ATTN_WIDTH + 2 * KV_WIDTH,
    ATTN_WIDTH + 2 * KV_WIDTH + RET_WIDTH,
    ATTN_WIDTH + 2 * KV_WIDTH + 2 * RET_WIDTH,
    ATTN_WIDTH + 2 * KV_WIDTH + 3 * RET_WIDTH,
    ATTN_WIDTH + 2 * KV_WIDTH + 4 * RET_WIDTH,
)
Q_BLOCK = 128
RET_CHUNK = 128
ROPE_THETA = 10000.0
RET_DECAY_BASE = 5
EPS = 1e-6

kernel_name = "hybrid_fourier_gqa_retention_macaron_dit"


def rmsnorm(x, gain=None):
    xf = x.astype(jnp.float32)
    y = xf * lax.rsqrt(jnp.mean(xf * xf, axis=-1, keepdims=True) + EPS)
    if gain is not None:
        y = y * gain.astype(jnp.float32)
    return y.astype(x.dtype)


def modulate(x, shift, scale):
    return x * (1 + scale) + shift


def adaln(c_act, w_ada, b_ada):
    m = (c_act @ w_ada + b_ada).reshape(c_act.shape[0], N_MOD, D_MODEL)
    return [m[:, i, None, :] for i in range(N_MOD)]


def swiglu(x, w_gate, w_up, w_down):
    return (jax.nn.silu(x @ w_gate) * (x @ w_up)) @ w_down


def axial_rope_angles(rows):
    n = rows * GRID_W
    t = jnp.arange(n)
    row = jnp.repeat(jnp.arange(rows), GRID_W).astype(jnp.float32)
    col = (t % GRID_W).astype(jnp.float32)
    inv = ROPE_THETA ** (-jnp.arange(0, HALF_ROT, 2, dtype=jnp.float32) / HALF_ROT)
    return row[:, None] * inv[None], col[:, None] * inv[None]


def _rotate(xh, ang):
    half = xh.shape[-1] // 2
    x1, x2 = xh[..., :half], xh[..., half:]
    cos = jnp.cos(ang)[None, :, None, :].astype(xh.dtype)
    sin = jnp.sin(ang)[None, :, None, :].astype(xh.dtype)
    return jnp.concatenate([x1 * cos - x2 * sin, x1 * sin + x2 * cos], axis=-1)


def apply_axial_rope(x, ang_row, ang_col):
    return jnp.concatenate([_rotate(x[..., :HALF_ROT], ang_row),
                            _rotate(x[..., HALF_ROT:], ang_col)], axis=-1)


def gqa_attend(q, k, v):
    B, L, H, hd = q.shape
    nblk = L // Q_BLOCK
    qb = q.reshape(B, nblk, Q_BLOCK, ATTN_KV_HEADS, ATTN_GROUP, hd).transpose(1, 0, 2, 3, 4, 5)
    scale = hd ** -0.5

    def block(qi):
        s = jnp.einsum('bqkgd,bskd->bkgqs', qi, k, preferred_element_type=jnp.float32) * scale
        p = jax.nn.softmax(s, axis=-1).astype(v.dtype)
        return jnp.einsum('bkgqs,bskd->bqkgd', p, v)

    o = lax.map(block, qb)
    return o.transpose(1, 0, 2, 3, 4, 5).reshape(B, L, H * hd)


def retention_chunkwise(q, k, v, log_gamma, state0):
    B, H, L, dk = q.shape
    dv = v.shape[-1]
    C = RET_CHUNK
    n = L // C
    idx = jnp.arange(C, dtype=jnp.float32)
    lg = log_gamma[:, None]
    rel = idx[:, None] - idx[None, :]
    decay_in = jnp.where(rel[None] >= 0, jnp.exp(lg[:, :, None] * jnp.maximum(rel, 0.0)[None]), 0.0)
    decay_q = jnp.exp(lg * (idx + 1.0))[None, :, :, None]
    decay_k = jnp.exp(lg * (C - 1.0 - idx))[None, :, :, None]
    decay_c = jnp.exp(log_gamma * C)[None, :, None, None]
    qs = jnp.moveaxis(q.reshape(B, H, n, C, dk), 2, 0)
    ks = jnp.moveaxis(k.reshape(B, H, n, C, dk), 2, 0)
    vs = jnp.moveaxis(v.reshape(B, H, n, C, dv), 2, 0)

    def step(state, qkv):
        qc, kc, vc = qkv
        inner = jnp.einsum('bhid,bhjd->bhij', qc, kc) * decay_in
        o = jnp.einsum('bhij,bhje->bhie', inner, vc) + jnp.einsum('bhid,bhde->bhie', qc, state) * decay_q
        state = state * decay_c + jnp.einsum('bhjd,bhje->bhde', kc * decay_k, vc)
        return state, o

    state, o = lax.scan(step, state0, (qs, ks, vs))
    return jnp.moveaxis(o, 0, 2).reshape(B, H, L, dv), state


def retention_final_state(k, v, log_gamma):
    L = k.shape[2]
    w = jnp.exp(log_gamma[:, None] * (L - 1.0 - jnp.arange(L, dtype=jnp.float32))[None])
    return jnp.einsum('bhsd,bhse,hs->bhde', k, v, w)


def _ret_heads(t):
    B, L, _ = t.shape
    return t.reshape(B, L, RET_HEADS, -1).transpose(0, 2, 1, 3).astype(jnp.float32)


def _ret_output(o, zg):
    B, H, L, dv = o.shape
    o = o * lax.rsqrt(jnp.mean(o * o, axis=-1, keepdims=True) + EPS)
    o = o.transpose(0, 2, 1, 3).reshape(B, L, H * dv).astype(zg.dtype)
    return jax.nn.silu(zg) * o


def retention_mixer(rq, rk, rv, rg, rqc, rkc, rvc, rgc, ret_decay, need_ctx):
    log_g = -jnp.exp(ret_decay.astype(jnp.float32))
    q, k, v = _ret_heads(rq), _ret_heads(rk) * RET_DK ** -0.5, _ret_heads(rv)
    qc, kc, vc = _ret_heads(rqc), _ret_heads(rkc) * RET_DK ** -0.5, _ret_heads(rvc)
    flip = lambda t: t[:, :, ::-1]
    if need_ctx:
        zeros = jnp.zeros((kc.shape[0], RET_HEADS, RET_DK, RET_DV), jnp.float32)
        oc_f, s_f = retention_chunkwise(qc, kc, vc, log_g[0], zeros)
        oc_b, s_b = retention_chunkwise(flip(qc), flip(kc), flip(vc), log_g[1], zeros)
        y_ctx = _ret_output(oc_f + flip(oc_b), rgc)
    else:
        s_f = retention_final_state(kc, vc, log_g[0])
        s_b = retention_final_state(flip(kc), flip(vc), log_g[1])
        y_ctx = None
    o_f, _ = retention_chunkwise(q, k, v, log_g[0], s_f)
    o_b, _ = retention_chunkwise(flip(q), flip(k), flip(v), log_g[1], s_b)
    return _ret_output(o_f + flip(o_b), rg), y_ctx


def fourier_mix(z):
    B, L, _ = z.shape
    zf = z.reshape(B, L, FOURIER_GROUPS, FOURIER_GROUP_DIM).astype(jnp.float32)
    y = jnp.fft.fftn(zf, axes=(1, 3), norm='ortho').real
    return y.reshape(B, L, FOURIER_WIDTH).astype(z.dtype)


def merge_branches(u, y_f, y_a, y_r, w_bf, w_ba, w_br, w_mg, b_mg, w_out):
    g_f, g_a, g_r = jnp.split(jax.nn.sigmoid(u @ w_mg + b_mg), 3, axis=-1)
    m = g_f * (y_f @ w_bf) + g_a * (y_a @ w_ba) + g_r * (y_r @ w_br)
    return m @ w_out


def token_mixer(u, uc, ang_row, ang_col, w_in, q_norm, k_norm, ret_decay,
                w_bf, w_ba, w_br, w_mg, b_mg, w_out, need_ctx):
    B, L, _ = u.shape
    Lc = uc.shape[1]
    aq, ak, av, rq, rk, rv, rg, fz = jnp.split(u @ w_in, IN_SPLITS, axis=-1)
    aqc, akc, avc, rqc, rkc, rvc, rgc, fzc = jnp.split(uc @ w_in, IN_SPLITS, axis=-1)
    q = apply_axial_rope(rmsnorm(aq.reshape(B, L, ATTN_HEADS, HEAD_DIM), q_norm), ang_row, ang_col)
    k = apply_axial_rope(rmsnorm(ak.reshape(B, L, ATTN_KV_HEADS, HEAD_DIM), k_norm), ang_row, ang_col)
    v = av.reshape(B, L, ATTN_KV_HEADS, HEAD_DIM)
    kc = rmsnorm(akc.reshape(B, Lc, ATTN_KV_HEADS, HEAD_DIM), k_norm)
    vc = avc.reshape(B, Lc, ATTN_KV_HEADS, HEAD_DIM)
    y_attn = gqa_attend(q, jnp.concatenate([kc, k], axis=1), jnp.concatenate([vc, v], axis=1))
    y_ret, y_ret_c = retention_mixer(rq, rk, rv, rg, rqc, rkc, rvc, rgc, ret_decay, need_ctx)
    y_four = fourier_mix(fz)
    out = merge_branches(u, y_four, y_attn, y_ret, w_bf, w_ba, w_br, w_mg, b_mg, w_out)
    if need_ctx:
        qc = rmsnorm(aqc.reshape(B, Lc, ATTN_HEADS, HEAD_DIM), q_norm)
        y_attn_c = gqa_attend(qc, kc, vc)
        out_c = merge_branches(uc, fourier_mix(fzc), y_attn_c, y_ret_c,
                               w_bf, w_ba, w_br, w_mg, b_mg, w_out)
    else:
        out_c = None
    return out, out_c


def setup_inputs(seed: int = 0) -> dict:
    key = jax.random.key(seed)
    ks = jax.random.split(key, 32)
    f32 = jnp.float32
    nrm = lambda k, shape, s: jax.random.normal(k, shape, f32) * s
    gain = lambda k, shape: 1.0 + 0.02 * jax.random.normal(k, shape, f32)
    L = DEPTH
    gam = 1.0 - 2.0 ** (-(RET_DECAY_BASE + jnp.arange(RET_HEADS, dtype=f32)))
    decay_base = jnp.log(-jnp.log(gam))
    return {
        'x': nrm(ks[0], (BATCH, SEQ, D_MODEL), 1.0),
        'c': nrm(ks[1], (BATCH, D_MODEL), 1.0),
        'ctx': nrm(ks[2], (BATCH, CTX_LEN, D_MODEL), 1.0),
        'c_ctx': nrm(ks[3], (D_MODEL,), 1.0),
        'w_ada': nrm(ks[4], (L, D_MODEL, N_MOD * D_MODEL), 0.5 * D_MODEL ** -0.5),
        'b_ada': nrm(ks[5], (L, N_MOD * D_MODEL), 0.01),
        'ffn1_norm': gain(ks[6], (L, D_MODEL)),
        'ffn1_w_gate': nrm(ks[7], (L, D_MODEL, FFN_DIM), D_MODEL ** -0.5),
        'ffn1_w_up': nrm(ks[8], (L, D_MODEL, FFN_DIM), D_MODEL ** -0.5),
        'ffn1_w_down': nrm(ks[9], (L, FFN_DIM, D_MODEL), FFN_DIM ** -0.5),
        'mix_norm': gain(ks[10], (L, D_MODEL)),
        'w_in': nrm(ks[11], (L, D_MODEL, IN_WIDTH), D_MODEL ** -0.5),
        'q_norm': gain(ks[12], (L, HEAD_DIM)),
        'k_norm': gain(ks[13], (L, HEAD_DIM)),
        'ret_decay': decay_base[None, None, :] + nrm(ks[14], (L, 2, RET_HEADS), 0.01),
        'w_branch_fourier': nrm(ks[15], (L, FOURIER_WIDTH, D_MODEL), FOURIER_WIDTH ** -0.5),
        'w_branch_attn': nrm(ks[16], (L, ATTN_WIDTH, D_MODEL), ATTN_WIDTH ** -0.5),
        'w_branch_ret': nrm(ks[17], (L, RET_WIDTH, D_MODEL), RET_WIDTH ** -0.5),
        'w_merge_gate': nrm(ks[18], (L, D_MODEL, 3 * D_MODEL), D_MODEL ** -0.5),
        'b_merge_gate': nrm(ks[19], (L, 3 * D_MODEL), 0.01),
        'w_out': nrm(ks[20], (L, D_MODEL, D_MODEL), D_MODEL ** -0.5),
        'ffn2_norm': gain(ks[21], (L, D_MODEL)),
        'ffn2_w_gate': nrm(ks[22], (L, D_MODEL, FFN_DIM), D_MODEL ** -0.5),
        'ffn2_w_up': nrm(ks[23], (L, D_MODEL, FFN_DIM), D_MODEL ** -0.5),
        'ffn2_w_down': nrm(ks[24], (L, FFN_DIM, D_MODEL), FFN_DIM ** -0.5),
        'final_norm': gain(ks[25], (D_MODEL,)),
    }


def reference(x, c, ctx, c_ctx, w_ada, b_ada, ffn1_norm, ffn1_w_gate, ffn1_w_up, ffn1_w_down,
              mix_norm, w_in, q_norm, k_norm, ret_decay, w_branch_fourier, w_branch_attn,
              w_branch_ret, w_merge_gate, b_merge_gate, w_out, ffn2_norm, ffn2_w_gate,
              ffn2_w_up, ffn2_w_down, final_norm):
    n_lat = x.shape[1]
    rows = n_lat // GRID_W
    ang_row, ang_col = axial_rope_angles(rows)
    c_act = jax.nn.silu(c)
    cc_act = jax.nn.silu(c_ctx)[None]
    h, hc = x, ctx
    for l in range(DEPTH):
        need_ctx = l < DEPTH - 1
        m = adaln(c_act, w_ada[l], b_ada[l])
        mc = adaln(cc_act, w_ada[l], b_ada[l])
        h = h + 0.5 * m[2] * swiglu(modulate(rmsnorm(h, ffn1_norm[l]), m[0], m[1]),
                                    ffn1_w_gate[l], ffn1_w_up[l], ffn1_w_down[l])
        hc = hc + 0.5 * mc[2] * swiglu(modulate(rmsnorm(hc, ffn1_norm[l]), mc[0], mc[1]),
                                       ffn1_w_gate[l], ffn1_w_up[l], ffn1_w_down[l])
        u = modulate(rmsnorm(h, mix_norm[l]), m[3], m[4])
        uc = modulate(rmsnorm(hc, mix_norm[l]), mc[3], mc[4])
        out, out_c = token_mixer(u, uc, ang_row, ang_col, w_in[l], q_norm[l], k_norm[l], ret_decay[l],
                                 w_branch_fourier[l], w_branch_attn[l], w_branch_ret[l],
                                 w_merge_gate[l], b_merge_gate[l], w_out[l], need_ctx)
        h = h + m[5] * out
        h = h + 0.5 * m[8] * swiglu(modulate(rmsnorm(h, ffn2_norm[l]), m[6], m[7]),
                                    ffn2_w_gate[l], ffn2_w_up[l], ffn2_w_down[l])
        if need_ctx:
            hc = hc + mc[5] * out_c
            hc = hc + 0.5 * mc[8] * swiglu(modulate(rmsnorm(hc, ffn2_norm[l]), mc[6], mc[7]),
                                           ffn2_w_gate[l], ffn2_w_up[l], ffn2_w_down[l])
    return rmsnorm(h, final_norm)
```

```python
import contextlib
import numpy as np
import ml_dtypes
import concourse.bass as bass
import concourse.mybir as mybir
from concourse.bass_utils import run_bass_kernel_spmd

F32 = mybir.dt.float32
BF16 = mybir.dt.bfloat16
AF = mybir.ActivationFunctionType
ALU = mybir.AluOpType

NCORES = 8
D = 2048
KC = 16
FF = 5632
FKC = 44
NT = 4608
NBLK = 9
EPS = 1e-6
DEPTH = 2
NS_ENG = 4
NS_DMA = 16


def blk_j(b):
    return 2 if b == 0 else (0 if b < 5 else 1)


class Res:
    __slots__ = ("w", "r")

    def __init__(self):
        self.w = []
        self.r = {}


class Sem:
    def __init__(self, h, idx, owner):
        self.h = h
        self.idx = idx
        self.owner = owner
        self.count = 0


class Eng:
    def __init__(self, name, be):
        self.name = name
        self.be = be
        self.sems = []
        self.n = 0
        self.waited = {}


class K:
    def __init__(self, nc, stack):
        self.nc = nc
        self.nsem = 0
        self.pe = Eng("pe", nc.tensor)
        self.act = Eng("act", nc.scalar)
        self.dve = Eng("dve", nc.vector)
        self.pool = Eng("pool", nc.gpsimd)
        self.sp = Eng("sp", nc.sync)
        self.engs = [self.pe, self.act, self.dve, self.pool, self.sp]
        for e in self.engs[:4]:
            for i in range(NS_ENG):
                e.sems.append(self._mk(stack, f"s_{e.name}{i}", e))
        self.dsem = {}
        self.dn = {}
        for q, e in (("sp", self.sp), ("pool", self.pool), ("act", self.act)):
            self.dsem[q] = [self._mk(stack, f"d_{q}{i}", None) for i in range(NS_DMA)]
            self.dn[q] = 0
        self.dq_eng = {"sp": self.sp, "pool": self.pool, "act": self.act}

    def _mk(self, stack, name, owner):
        h = stack.enter_context(self.nc.semaphore(name))
        s = Sem(h, self.nsem, owner)
        self.nsem += 1
        return s

    def _wait(self, eng, deps, skip_self=False):
        best = {}
        for (sem, val) in deps:
            if skip_self and sem.owner is eng:
                continue
            if eng.waited.get(sem.idx, 0) >= val:
                continue
            if sem.idx not in best or best[sem.idx][1] < val:
                best[sem.idx] = (sem, val)
        for sem, val in best.values():
            eng.be.wait_ge(sem.h, val)
            eng.waited[sem.idx] = val

    @staticmethod
    def _deps(reads, writes):
        deps = []
        for r in reads:
            deps.extend(r.w)
        for w in writes:
            deps.extend(w.w)
            deps.extend(w.r.values())
        return deps

    @staticmethod
    def _commit(evs, reads, writes):
        for r in reads:
            for ev in evs:
                old = r.r.get(ev[0].idx)
                if old is None or old[1] < ev[1]:
                    r.r[ev[0].idx] = ev
        for w in writes:
            w.w = list(evs)
            w.r = {}

    def op(self, eng, fn, reads=(), writes=(), skip_self=False):
        self._wait(eng, self._deps(reads, writes), skip_self)
        ins = fn(eng.be)
        k = eng.n
        eng.n += 1
        sem = eng.sems[k % NS_ENG]
        val = k // NS_ENG + 1
        ins.then_inc(sem.h, 1)
        ev = (sem, val)
        self._commit([ev], reads, writes)
        return ev

    def dma(self, q, pairs, reads=(), writes=(), acc_writes=()):
        eng = self.dq_eng[q]
        self._wait(eng, self._deps(reads, writes))
        evs = []
        for (o, i) in pairs:
            k = self.dn[q]
            self.dn[q] += 1
            sem = self.dsem[q][k % NS_DMA]
            if sem.count > 0:
                self._wait(eng, [(sem, 16 * sem.count)])
            eng.be.dma_start(out=o, in_=i).then_inc(sem.h, 16)
            sem.count += 1
            evs.append((sem, 16 * sem.count))
        self._commit(evs, reads, writes)
        for w in acc_writes:
            best = {ev[0].idx: ev for ev in w.w}
            for ev in evs:
                if ev[0].idx not in best or best[ev[0].idx][1] < ev[1]:
                    best[ev[0].idx] = ev
            w.w = list(best.values())
        return evs

    def all_events(self):
        evs = []
        for e in self.engs[:4]:
            if e.n > 0:
                k = e.n - 1
                evs.append((e.sems[k % NS_ENG], k // NS_ENG + 1))
        for q in self.dsem:
            for sem in self.dsem[q]:
                if sem.count > 0:
                    evs.append((sem, 16 * sem.count))
        return evs

    def barrier(self):
        evs = self.all_events()
        for e in self.engs:
            self._wait(e, evs)

    def mm(self, out_ap, pairs, reads, writes):
        n = len(pairs)

        def fn(pe):
            ins = None
            for i, (l, r) in enumerate(pairs):
                ins = pe.matmul(out_ap, lhsT=l, rhs=r, start=(i == 0), stop=(i == n - 1))
            return ins

        return self.op(self.pe, fn, reads, writes, skip_self=True)


class Tile:
    def __init__(self, t, n=1):
        self.t = t
        self.res = [Res() for _ in range(n)]


def build_program(dbg=(), stages=None):
    nc = bass.Bass("TRN2", target_bir_lowering=False)
    stack = contextlib.ExitStack()
    with stack:
        _emit(nc, stack, dbg, stages)
    return nc


def _emit(nc, gstack, dbg, stages):
    def din(name, shape, dt=F32):
        return nc.dram_tensor(name, list(shape), dt, kind="ExternalInput").ap()

    def dscr(name, shape, dt):
        kind = "ExternalOutput" if name in dbg else "Internal"
        return nc.dram_tensor(name, list(shape), dt, kind=kind).ap()

    xin = din("xin", [2, 2048, D])
    ctxin = din("ctxin", [2, 256, D])
    ccat = din("ccat", [128, KC, 3])
    w_ada = din("w_ada", [DEPTH, D, 9 * D])
    b_ada = din("b_ada_l", [128, DEPTH, 144])
    gains = din("gains", [128, 7, KC])
    qkn = din("qkn", [128, DEPTH, 2])
    rdec = din("rdec", [128, DEPTH * 8])
    bmg = din("bmg", [128, DEPTH, 48])
    Wd = {}
    for nm, shp in (("ffn1_w_gate", [D, FF]), ("ffn1_w_up", [D, FF]), ("ffn1_w_down", [FF, D]),
                    ("ffn2_w_gate", [D, FF]), ("ffn2_w_up", [D, FF]), ("ffn2_w_down", [FF, D]),
                    ("w_in", [D, 4096]), ("w_branch_fourier", [512, D]), ("w_branch_attn", [1024, D]),
                    ("w_branch_ret", [512, D]), ("w_merge_gate", [D, 3 * D]), ("w_out", [D, D])):
        Wd[nm] = din(nm, [DEPTH] + shp)
    ident = din("ident", [128, 128])
    cos_t = din("cos_t", [128, 2048])
    sin_t = din("sin_t", [128, 2048])
    pswap = din("pswap", [128, 128], BF16)
    rtab = din("rtab", [128, 8, 128])
    dftc = din("dftc", [128, 256], BF16)
    dftL = din("dftL", [2, 2048, 2048], BF16)
    dftS = din("dftS", [2, 256, 256], BF16)
    out = nc.dram_tensor("out", [2, 2048, D], F32, kind="ExternalOutput").ap()

    hT = dscr("hT", [D, NT], F32)
    xnT = dscr("xnT", [D, NT], BF16)
    hidT = dscr("hidT", [FF, NT], BF16)
    qkT = dscr("qkT", [1280, NT], BF16)
    rT = dscr("rT", [2048, NT], BF16)
    vtok = dscr("vtok", [NT, 1280], BF16)
    yT = dscr("yT", [2048, NT], BF16)
    mT = dscr("mT", [D, NT], BF16)
    moddbg = dscr("moddbg", [128, DEPTH * 144 * 3], F32)

    hT_v = hT.rearrange("(c p) t -> p c t", p=128)
    xnT_v = xnT.rearrange("(c p) t -> p c t", p=128)
    hidT_v = hidT.rearrange("(c p) t -> p c t", p=128)
    qkT_v = qkT.rearrange("(c p) t -> p c t", p=128)
    rT_v = rT.rearrange("(c p) t -> p c t", p=128)
    yT_v = yT.rearrange("(c p) t -> p c t", p=128)
    mT_v = mT.rearrange("(c p) t -> p c t", p=128)

    k = K(nc, gstack)
    want = (lambda s: True) if stages is None else (lambda s: s in stages)

    sbn = [0]

    def sb(stack, name, shape, dt, n=1):
        sbn[0] += 1
        t = stack.enter_context(nc.sbuf_tensor(f"{name}_{sbn[0]}", list(shape), dt))
        return Tile(t, n)

    PS = [Tile(gstack.enter_context(nc.psum_tensor(f"ps{i}", [128, 512], F32))) for i in range(8)]
    MOD = sb(gstack, "MOD", [128, DEPTH, 144, 3], F32)
    DER = sb(gstack, "DER", [128, DEPTH, 6, KC, 3], F32)
    GN = sb(gstack, "GN", [128, 7, KC], F32)
    QKN = sb(gstack, "QKN", [128, DEPTH, 2], F32)
    ONES_D = sb(gstack, "ONES_D", [128, 128], BF16)
    ONES_H = sb(gstack, "ONES_H", [128, 128], BF16)
    ONES_1 = sb(gstack, "ONES_1", [128, 128], BF16)
    IDN = sb(gstack, "IDN", [128, 128], F32)
    BMG = sb(gstack, "BMG", [128, DEPTH, 48], F32)
    RDEC = sb(gstack, "RDEC", [128, DEPTH * 8], F32)
    EPS_T = sb(gstack, "EPS_T", [128, 1], F32)
    psn = [0]

    def rsqrt(out_ap, out_res, in_ap, in_res):
        k.op(k.act, lambda e: e.activation(out=out_ap, in_=in_ap, func=AF.Sqrt, bias=EPS_T.t[:, 0:1], scale=1.0),
             reads=[in_res, EPS_T.res[0]], writes=[out_res])
        k.op(k.dve, lambda e: e.reciprocal(out_ap, out_ap), reads=[out_res], writes=[out_res])

    def ps_next():
        p = PS[psn[0] % 8]
        psn[0] += 1
        return p

    k.dma("sp", [(GN.t[:], gains), (QKN.t[:], qkn), (IDN.t[:], ident), (BMG.t[:], bmg), (RDEC.t[:], rdec)],
          writes=[GN.res[0], QKN.res[0], IDN.res[0], BMG.res[0], RDEC.res[0]])
    k.op(k.dve, lambda e: e.memset(EPS_T.t[:], EPS), writes=[EPS_T.res[0]])
    k.op(k.dve, lambda e: e.memset(ONES_D.t[:], 1.0 / D), writes=[ONES_D.res[0]])
    k.op(k.dve, lambda e: e.memset(ONES_H.t[:], 1.0 / 128), writes=[ONES_H.res[0]])
    k.op(k.dve, lambda e: e.memset(ONES_1.t[:], 1.0), writes=[ONES_1.res[0]])

    def stage_in():
        with contextlib.ExitStack() as st:
            XT = sb(st, "XT", [128, 2, D], F32, 2)
            HT = sb(st, "HTt", [128, 2, KC, 128], F32, 2)
            for i in range(36):
                b = i % 2
                if i < 2:
                    src = ctxin[0, i * 128:(i + 1) * 128, :]
                elif i < 4:
                    src = ctxin[1, (i - 2) * 128:(i - 1) * 128, :]
                elif i < 20:
                    src = xin[0, (i - 4) * 128:(i - 3) * 128, :]
                else:
                    src = xin[1, (i - 20) * 128:(i - 19) * 128, :]
                k.dma("sp", [(XT.t[:, b, :], src)], writes=[XT.res[b]])
                for g in range(4):
                    p = ps_next()

                    def fn(pe, g=g, p=p, b=b):
                        ins = None
                        for jj in range(4):
                            kc = g * 4 + jj
                            ins = pe.transpose(p.t[:, jj * 128:(jj + 1) * 128], XT.t[:, b, kc * 128:(kc + 1) * 128], IDN.t[:])
                        return ins
                    k.op(k.pe, fn, reads=[XT.res[b], IDN.res[0]], writes=[p.res[0]], skip_self=True)
                    eng = k.act if g % 2 == 0 else k.dve
                    dst = HT.t[:, b, g * 4:(g + 1) * 4, :]
                    srcp = p.t[:].rearrange("p (c t) -> p c t", c=4)
                    if g % 2 == 0:
                        k.op(k.act, lambda e, dst=dst, srcp=srcp: e.copy(dst, srcp), reads=[p.res[0]], writes=[HT.res[b]])
                    else:
                        k.op(k.dve, lambda e, dst=dst, srcp=srcp: e.tensor_copy(dst, srcp), reads=[p.res[0]], writes=[HT.res[b]])
                k.dma("sp", [(hT_v[:, :, i * 128:(i + 1) * 128], HT.t[:, b, :, :])], reads=[HT.res[b]])
            k.barrier()

    WSCR_N = 800000
    wscrs = [dscr(f"wscr{i}", [128, WSCR_N], BF16) for i in range(DEPTH)]
    wcache = {}
    wscr_offs = [0] * DEPTH
    pend_wb = {}

    def load_slab_c(key, tile, bi, parts, kcn, ncols):
        if key in wcache:
            off, n, res = wcache[key]
            src = wscrs[key[1]][:, off:off + n].rearrange("p (c n) -> p c n", c=kcn)
            h = kcn // 2
            k.dma("sp", [(tile.t[:, bi, 0:h, 0:ncols], src[:, 0:h, :]), (tile.t[:, bi, h:kcn, 0:ncols], src[:, h:kcn, :])],
                  reads=[res], writes=[tile.res[bi]])
            return
        pairs = []
        for (a, b2, src2d) in parts:
            v = src2d.rearrange("(c p) n -> p c n", p=128)
            for a2 in range(a, b2, 8):
                b3 = min(b2, a2 + 8)
                pairs.append((tile.t[:, bi, a2:b3, 0:ncols], v[:, a2 - a:b3 - a, :]))
        k.dma("pool", pairs, writes=[tile.res[bi]])
        n = kcn * ncols
        off = wscr_offs[key[1]]
        wscr_offs[key[1]] += n
        assert wscr_offs[key[1]] <= WSCR_N
        wcache[key] = (off, n, Res())
        pend_wb[(id(tile), bi)] = key

    def writeback(tile, bi, kcn, ncols):
        key = pend_wb.pop((id(tile), bi), None)
        if key is None:
            return
        off, n, res = wcache[key]
        dst = wscrs[key[1]][:, off:off + n].rearrange("p (c n) -> p c n", c=kcn)
        k.dma("sp", [(dst, tile.t[:, bi, 0:kcn, 0:ncols])], reads=[tile.res[bi]], writes=[res])

    class Stream:
        def __init__(self, loaders):
            self.loaders = loaders
            self.done = 0

        def need(self, i):
            while self.done <= min(i + 1, len(self.loaders) - 1):
                self.loaders[self.done](self.done % 2)
                self.done += 1
            return i % 2

    CA = sb(gstack, "CA", [128, KC, 3], BF16)
    BA = sb(gstack, "BA", [128, DEPTH, 144], F32)

    def der_a(l, n_):
        sc_i, g_i = ((1, l * 3 + 0), (4, l * 3 + 1), (7, l * 3 + 2))[n_]
        gb = GN.t[:, g_i, :].unsqueeze(2).to_broadcast([128, KC, 3])
        k.op(k.dve, lambda e: e.scalar_tensor_tensor(
            out=DER.t[:, l, n_, :, :], in0=MOD.t[:, l, sc_i * 16:(sc_i + 1) * 16, :], scalar=1.0, in1=gb,
            op0=ALU.add, op1=ALU.mult), reads=[MOD.res[0], GN.res[0]], writes=[DER.res[0]])

    def der_g(l, n_):
        mi, f = ((2, 0.5), (5, 1.0), (8, 0.5))[n_]
        k.op(k.dve, lambda e: e.tensor_scalar(
            out=DER.t[:, l, 3 + n_, :, :], in0=MOD.t[:, l, mi * 16:(mi + 1) * 16, :], scalar1=f, scalar2=None,
            op0=ALU.mult), reads=[MOD.res[0]], writes=[DER.res[0]])

    def adaln_task(st, slabs):
        WF = sb(st, "WFb", [128, 2, KC, 256], F32, 2)
        WA = sb(st, "WAb", [128, 2, KC, 256], BF16, 2)

        def mk(l, c0):
            def f(b):
                v = w_ada[l, :, c0:c0 + 256].rearrange("(c p) n -> p c n", p=128)
                k.dma("sp", [(WF.t[:, b, 0:8, :], v[:, 0:8, :]), (WF.t[:, b, 8:16, :], v[:, 8:16, :])], writes=[WF.res[b]])
            return f
        strm = Stream([mk(l, c0) for (l, c0) in slabs])

        def gen():
            N = len(slabs)
            for n in range(N + 1):
                if n < N:
                    bi = strm.need(n)
                    k.op(k.dve, lambda e: e.tensor_copy(WA.t[:, bi, :, :], WF.t[:, bi, :, :]), reads=[WF.res[bi]], writes=[WA.res[bi]])
                if n >= 1:
                    l, c0 = slabs[n - 1]
                    bj = (n - 1) % 2
                    for sub in range(2):
                        p = ps_next()
                        ch = c0 // 128 + sub
                        k.mm(p.t[:, 0:3], [(WA.t[:, bj, kc, sub * 128:(sub + 1) * 128], CA.t[:, kc, :]) for kc in range(KC)],
                             reads=[WA.res[bj], CA.res[0]], writes=[p.res[0]])
                        k.op(k.act, lambda e, p=p, ch=ch: e.activation(out=MOD.t[:, l, ch, :], in_=p.t[:, 0:3], func=AF.Identity,
                                                                   bias=BA.t[:, l, ch:ch + 1], scale=1.0),
                             reads=[p.res[0], BA.res[0]], writes=[MOD.res[0]])
                yield
        return gen()

    def stage_adaln_pre():
        with contextlib.ExitStack() as st:
            CC = sb(st, "CC", [128, KC, 3], F32)
            k.dma("sp", [(CC.t[:], ccat), (BA.t[:], b_ada)], writes=[CC.res[0], BA.res[0]])
            k.op(k.act, lambda e: e.activation(out=CA.t[:], in_=CC.t[:], func=AF.Silu), reads=[CC.res[0]], writes=[CA.res[0]])
            g_ = adaln_task(st, [(0, c0) for c0 in range(0, 3 * D, 256)])
            for _ in g_:
                pass
            der_a(0, 0)
            der_g(0, 0)
            k.barrier()

    bg_slabs = {"gu0": [(0, c0) for c0 in range(3 * D, 9 * D, 256)], "attn0": [(1, c0) for c0 in range(0, 9 * D, 256)]}

    def bg_finish(which):
        if which == "gu0":
            der_a(0, 1); der_a(0, 2); der_g(0, 1); der_g(0, 2)
        else:
            for n_ in range(3):
                der_a(1, n_); der_g(1, n_)
        if which == "attn0" and "moddbg" in dbg:
            k.dma("sp", [(moddbg, MOD.t[:].rearrange("p a b c -> p (a b c)"))], reads=[MOD.res[0]])

    def stage_norm(l, n_, blocks):
        sh_i = (0, 3, 6)[n_]
        with contextlib.ExitStack() as st:
            X = sb(st, "X", [128, 3, KC, 512], F32, 3)
            SQ = sb(st, "SQ", [128, 2, KC, 512], BF16, 2)
            RS = sb(st, "RS", [128, 2, 512], F32, 2)
            XN = sb(st, "XN", [128, 2, KC, 512], BF16, 4)

            def ldx(it):
                ts_ = slice(blocks[it] * 512, (blocks[it] + 1) * 512)
                k.dma("sp", [(X.t[:, it % 3, 0:8, :], hT_v[:, 0:8, ts_]), (X.t[:, it % 3, 8:16, :], hT_v[:, 8:16, ts_])], writes=[X.res[it % 3]])
            ldx(0)
            if len(blocks) > 1:
                ldx(1)
            for it, blk in enumerate(blocks):
                b = it % 2
                xb = it % 3
                j = blk_j(blk)
                ts = slice(blk * 512, (blk + 1) * 512)
                if it + 2 < len(blocks):
                    ldx(it + 2)
                k.op(k.act, lambda e: e.activation(out=SQ.t[:, b, :, :], in_=X.t[:, xb, :, :], func=AF.Square),
                     reads=[X.res[xb]], writes=[SQ.res[b]])
                p = ps_next()
                k.mm(p.t[:], [(ONES_D.t[:], SQ.t[:, b, kc, :]) for kc in range(KC)], reads=[SQ.res[b], ONES_D.res[0]], writes=[p.res[0]])
                rsqrt(RS.t[:, b, :], RS.res[b], p.t[:], p.res[0])
                rb = RS.t[:, b, :].unsqueeze(1).to_broadcast([128, KC, 512])
                k.op(k.dve, lambda e: e.tensor_tensor(out=X.t[:, xb, :, :], in0=X.t[:, xb, :, :], in1=rb, op=ALU.mult),
                     reads=[RS.res[b], X.res[xb]], writes=[X.res[xb]])
                def fn(e):
                    ins = None
                    for kc in range(0, 10):
                        ins = e.activation(out=XN.t[:, b, kc, :], in_=X.t[:, xb, kc, :], func=AF.Identity,
                                           bias=MOD.t[:, l, sh_i * 16 + kc, j:j + 1], scale=DER.t[:, l, n_, kc, j:j + 1])
                    return ins

                def fnp(e):
                    ins = None
                    for kc in range(10, KC):
                        ins = e.tensor_scalar(out=XN.t[:, b, kc, :], in0=X.t[:, xb, kc, :], scalar1=DER.t[:, l, n_, kc, j:j + 1],
                                              scalar2=MOD.t[:, l, sh_i * 16 + kc, j:j + 1], op0=ALU.mult, op1=ALU.add)
                    return ins
                k.op(k.act, fn, reads=[X.res[xb], MOD.res[0], DER.res[0]], writes=[XN.res[2 * b]])
                k.op(k.pool, fnp, reads=[X.res[xb], MOD.res[0], DER.res[0]], writes=[XN.res[2 * b + 1]])
                k.dma("sp", [(xnT_v[:, :, ts], XN.t[:, b, :, :])], reads=[XN.res[2 * b], XN.res[2 * b + 1]])
            k.barrier()

    def stage_ffn_gu(l, wg, wu, blocks, tag, bg=None):
        passes = [blocks[i:i + 3] for i in range(0, len(blocks), 3)]
        nxb = 3 if bg else 6
        with contextlib.ExitStack() as st:
            XNr = sb(st, "XNr", [128, nxb, KC, 512], BF16, nxb)
            WGU = sb(st, "WGU", [128, 2, 2 * KC, 512], BF16, 2)
            SG = sb(st, "SG", [128, 3, 512], F32, 3)
            HB = sb(st, "HB", [128, 2, 4, 512], BF16, 2)
            task = adaln_task(st, bg_slabs[bg]) if bg else None
            nsg = 0
            nhb = 0
            order = [(pi, jc) for pi in range(len(passes)) for jc in range(11)]

            def mk(jc):
                def f(b):
                    load_slab_c((tag, l, "gu", jc), WGU, b, [(0, KC, wg[l, :, jc * 512:(jc + 1) * 512]), (KC, 2 * KC, wu[l, :, jc * 512:(jc + 1) * 512])], 2 * KC, 512)
                return f
            strm = Stream([mk(jc) for (_, jc) in order])
            xloaded = set()

            def ldpass(pi):
                if pi >= len(passes) or pi in xloaded:
                    return
                xloaded.add(pi)
                for bi, blk in enumerate(passes[pi]):
                    xb = (pi * 3 + bi) % nxb
                    k.dma("sp", [(XNr.t[:, xb, :, :], xnT_v[:, :, blk * 512:(blk + 1) * 512])], writes=[XNr.res[xb]])
            n = 0
            it = 0
            for pi, ps_ in enumerate(passes):
                ldpass(pi)
                if nxb == 6:
                    ldpass(pi + 1)
                for jc in range(11):
                    wb = strm.need(n)
                    n += 1
                    writeback(WGU, wb, 2 * KC, 512)
                    for bi, blk in enumerate(ps_):
                        xb = (pi * 3 + bi) % nxb
                        hb = nhb % 2
                        nhb += 1
                        for sub in range(4):
                            pg = ps_next()
                            pu = ps_next()
                            k.mm(pg.t[:], [(WGU.t[:, wb, kc, sub * 128:(sub + 1) * 128], XNr.t[:, xb, kc, :]) for kc in range(KC)],
                                 reads=[WGU.res[wb], XNr.res[xb]], writes=[pg.res[0]])
                            k.mm(pu.t[:], [(WGU.t[:, wb, KC + kc, sub * 128:(sub + 1) * 128], XNr.t[:, xb, kc, :]) for kc in range(KC)],
                                 reads=[WGU.res[wb], XNr.res[xb]], writes=[pu.res[0]])
                            sg = nsg % 3
                            nsg += 1
                            k.op(k.act, lambda e, sg=sg, pg=pg: e.activation(out=SG.t[:, sg, :], in_=pg.t[:], func=AF.Silu),
                                 reads=[pg.res[0]], writes=[SG.res[sg]])
                            k.op(k.dve, lambda e, sg=sg, pu=pu, hb=hb, sub=sub: e.tensor_tensor(
                                out=HB.t[:, hb, sub, :], in0=SG.t[:, sg, :], in1=pu.t[:], op=ALU.mult),
                                reads=[SG.res[sg], pu.res[0]], writes=[HB.res[hb]])
                        k.dma("sp", [(hidT_v[:, jc * 4:(jc + 1) * 4, blk * 512:(blk + 1) * 512], HB.t[:, hb, :, :])], reads=[HB.res[hb]])
                        it += 1
                        if task is not None and it % 2 == 0:
                            next(task, None)
            if task is not None:
                for _ in task:
                    pass
                bg_finish(bg)
            k.barrier()

    hT_res = [Res() for _ in range(NBLK)]

    def norm_task(st, l, n_, ready, state):
        sh_i = (0, 3, 6)[n_]
        X = sb(st, "Xb", [128, 2, KC, 256], F32, 2)
        XN = sb(st, "XNb", [128, 2, KC, 256], BF16, 2)
        RS = sb(st, "RSb", [128, 256], F32)
        nh = 0
        while True:
            if not ready:
                if state["done"]:
                    return
                yield
                continue
            blk = ready.pop(0)
            j = blk_j(blk)
            for half in range(2):
                xb = nh % 2
                nh += 1
                t0 = blk * 512 + half * 256
                k.dma("sp", [(X.t[:, xb, 0:8, :], hT_v[:, 0:8, t0:t0 + 256]), (X.t[:, xb, 8:16, :], hT_v[:, 8:16, t0:t0 + 256])],
                      reads=[hT_res[blk]], writes=[X.res[xb]])
                yield
                k.op(k.act, lambda e: e.activation(out=XN.t[:, xb, :, :], in_=X.t[:, xb, :, :], func=AF.Square),
                     reads=[X.res[xb]], writes=[XN.res[xb]])
                yield
                p = ps_next()
                k.mm(p.t[:, 0:256], [(ONES_D.t[:], XN.t[:, xb, kc, :]) for kc in range(KC)], reads=[XN.res[xb], ONES_D.res[0]], writes=[p.res[0]])
                rsqrt(RS.t[:], RS.res[0], p.t[:, 0:256], p.res[0])
                rb = RS.t[:].unsqueeze(1).to_broadcast([128, KC, 256])
                k.op(k.dve, lambda e: e.tensor_tensor(out=X.t[:, xb, :, :], in0=X.t[:, xb, :, :], in1=rb, op=ALU.mult),
                     reads=[RS.res[0], X.res[xb]], writes=[X.res[xb]])

                def fn(e):
                    ins = None
                    for kc in range(KC):
                        ins = e.activation(out=XN.t[:, xb, kc, :], in_=X.t[:, xb, kc, :], func=AF.Identity,
                                           bias=MOD.t[:, l, sh_i * 16 + kc, j:j + 1], scale=DER.t[:, l, n_, kc, j:j + 1])
                    return ins
                k.op(k.act, fn, reads=[X.res[xb], MOD.res[0], DER.res[0]], writes=[XN.res[xb]])
                k.dma("sp", [(xnT_v[:, :, t0:t0 + 256], XN.t[:, xb, :, :])], reads=[XN.res[xb]])
                yield

    def stage_gemm_resid(l, inT_v, kcn, w2d, g_i, blocks, name, norm_after=None):
        passes = [blocks[i:i + 2] for i in range(0, len(blocks), 2)]
        nib = (2 if norm_after else 3) if kcn > 16 else 4
        with contextlib.ExitStack() as st:
            IN = sb(st, "IN" + name, [128, nib, kcn, 512], BF16, nib)
            WS = sb(st, "WS" + name, [128, 2, kcn, 256], BF16, 2)
            HO = sb(st, "HO" + name, [128, 4, 512], F32, 4)
            ready = []
            state = {"done": False}
            task = None
            if norm_after is not None:
                task = norm_task(st, norm_after[0], norm_after[1], ready, state)
                nblocks = norm_after[2]
            for blk in blocks:
                hT_res[blk] = Res()
            nho = 0
            order = [(pi, s_) for pi in range(len(passes)) for s_ in range(8)]

            def mk(s_):
                def f(b):
                    load_slab_c((name, l, "w", s_), WS, b, [(0, kcn, w2d[:, s_ * 256:(s_ + 1) * 256])], kcn, 256)
                return f
            strm = Stream([mk(s_) for (_, s_) in order])
            seq = list(blocks)
            loaded = [0]

            def ensure(upto):
                while loaded[0] < min(upto, len(seq)):
                    c = loaded[0]
                    blk = seq[c]
                    ts_ = slice(blk * 512, (blk + 1) * 512)
                    pairs = []
                    for a_ in range(0, kcn, 8):
                        b2 = min(kcn, a_ + 8)
                        pairs.append((IN.t[:, c % nib, a_:b2, :], inT_v[:, a_:b2, ts_]))
                    k.dma("sp", pairs, writes=[IN.res[c % nib]])
                    loaded[0] += 1
            n = 0
            c0 = 0
            for ps_ in passes:
                ensure(c0 + len(ps_))
                ensure(c0 + nib)
                for s in range(8):
                    wb = strm.need(n)
                    n += 1
                    writeback(WS, wb, kcn, 256)
                    for bi, blk in enumerate(ps_):
                        ib = (c0 + bi) % nib
                        j = blk_j(blk)
                        ts = slice(blk * 512, (blk + 1) * 512)
                        for sub in range(2):
                            dc = s * 2 + sub
                            p = ps_next()
                            ho = nho % 4
                            nho += 1
                            k.dma("sp", [(HO.t[:, ho, :], hT_v[:, dc, ts])], writes=[HO.res[ho]])
                            k.mm(p.t[:], [(WS.t[:, wb, kc, sub * 128:(sub + 1) * 128], IN.t[:, ib, kc, :]) for kc in range(kcn)],
                                 reads=[WS.res[wb], IN.res[ib]], writes=[p.res[0]])
                            k.op(k.dve, lambda e, p=p, ho=ho, dc=dc, j=j: e.scalar_tensor_tensor(
                                out=HO.t[:, ho, :], in0=p.t[:], scalar=DER.t[:, l, g_i, dc, j:j + 1], in1=HO.t[:, ho, :],
                                op0=ALU.mult, op1=ALU.add), reads=[p.res[0], HO.res[ho], DER.res[0]], writes=[HO.res[ho]])
                            k.dma("sp", [(hT_v[:, dc, ts], HO.t[:, ho, :])], reads=[HO.res[ho]], acc_writes=[hT_res[blk]])
                            if task is not None:
                                next(task, None)
                c0 += len(ps_)
                if task is not None:
                    ready.extend(b_ for b_ in ps_ if b_ in nblocks)
            if task is not None:
                state["done"] = True
                for _ in task:
                    pass
            k.barrier()

    def stage_final():
        with contextlib.ExitStack() as st:
            X = sb(st, "Xf", [128, 2, KC, 512], F32, 2)
            SQ = sb(st, "SQf", [128, 2, KC, 512], BF16, 2)
            RS = sb(st, "RSf", [128, 2, 512], F32, 2)
            OT = sb(st, "OTf", [128, 2, D], F32, 2)
            no = 0
            def ldx(it):
                ts_ = slice((it + 1) * 512, (it + 2) * 512)
                k.dma("sp", [(X.t[:, it % 2, 0:8, :], hT_v[:, 0:8, ts_]), (X.t[:, it % 2, 8:16, :], hT_v[:, 8:16, ts_])], writes=[X.res[it % 2]])
            ldx(0)
            for it, blk in enumerate(range(1, 9)):
                b = it % 2
                ts = slice(blk * 512, (blk + 1) * 512)
                if it + 1 < 8:
                    ldx(it + 1)
                k.op(k.act, lambda e, b=b: e.activation(out=SQ.t[:, b, :, :], in_=X.t[:, b, :, :], func=AF.Square),
                     reads=[X.res[b]], writes=[SQ.res[b]])
                p = ps_next()
                k.mm(p.t[:], [(ONES_D.t[:], SQ.t[:, b, kc, :]) for kc in range(KC)], reads=[SQ.res[b], ONES_D.res[0]], writes=[p.res[0]])
                rsqrt(RS.t[:, b, :], RS.res[b], p.t[:], p.res[0])
                rb = RS.t[:, b, :].unsqueeze(1).to_broadcast([128, KC, 512])
                k.op(k.dve, lambda e, b=b, rb=rb: e.tensor_tensor(out=X.t[:, b, :, :], in0=X.t[:, b, :, :], in1=rb, op=ALU.mult),
                     reads=[RS.res[b], X.res[b]], writes=[X.res[b]])

                def fn(e, b=b):
                    ins = None
                    for kc in range(KC):
                        ins = e.activation(out=X.t[:, b, kc, :], in_=X.t[:, b, kc, :], func=AF.Identity, scale=GN.t[:, 6, kc:kc + 1])
                    return ins
                k.op(k.act, fn, reads=[X.res[b], GN.res[0]], writes=[X.res[b]])
                for tt in range(4):
                    ob = no % 2
                    no += 1
                    for g in range(4):
                        p = ps_next()

                        def fnt(pe, g=g, p=p, b=b, tt=tt):
                            ins = None
                            for jj in range(4):
                                kc = g * 4 + jj
                                ins = pe.transpose(p.t[:, jj * 128:(jj + 1) * 128], X.t[:, b, kc, tt * 128:(tt + 1) * 128], IDN.t[:])
                            return ins
                        k.op(k.pe, fnt, reads=[X.res[b], IDN.res[0]], writes=[p.res[0]], skip_self=True)
                        dst = OT.t[:, ob, g * 512:(g + 1) * 512]
                        if g % 2 == 0:
                            k.op(k.act, lambda e, dst=dst, p=p: e.copy(dst, p.t[:]), reads=[p.res[0]], writes=[OT.res[ob]])
                        else:
                            k.op(k.dve, lambda e, dst=dst, p=p: e.tensor_copy(dst, p.t[:]), reads=[p.res[0]], writes=[OT.res[ob]])
                    s = 0 if blk < 5 else 1
                    t0 = ((blk - 1) % 4) * 512 + tt * 128
                    k.dma("sp", [(out[s, t0:t0 + 128, :], OT.t[:, ob, :])], reads=[OT.res[ob]])
            k.barrier()

    def mm1(out_ap, l_ap, r_ap, start, stop, reads, writes):
        return k.op(k.pe, lambda pe: pe.matmul(out_ap, lhsT=l_ap, rhs=r_ap, start=start, stop=stop), reads, writes, skip_self=True)

    def ctx_tok(s):
        return slice(256 * s, 256 * s + 256)

    def lat_tok(s, a=0, n=2048):
        return slice(512 + 2048 * s + a, 512 + 2048 * s + a + n)

    cp_n = [0]

    def copy_alt(dst, src, reads, writes):
        cp_n[0] += 1
        if cp_n[0] % 2 == 0:
            k.op(k.act, lambda e: e.copy(dst, src), reads=reads, writes=writes)
        else:
            k.op(k.dve, lambda e: e.tensor_copy(dst, src), reads=reads, writes=writes)

    def stage_win(l, blocks):
        passes = [blocks[i:i + 3] for i in range(0, len(blocks), 3)]
        win = Wd["w_in"]
        with contextlib.ExitStack() as st:
            XNr = sb(st, "XNw", [128, 3, KC, 512], BF16, 3)
            WS = sb(st, "WSw", [128, 2, KC, 512], BF16, 2)
            COS = sb(st, "COS", [128, 2048], F32)
            SIN = sb(st, "SIN", [128, 2048], F32)
            PSW = sb(st, "PSW", [128, 128], BF16)
            OB = sb(st, "OBw", [128, 4, 4, 512], BF16, 4)
            TB = sb(st, "TBw", [128, 2, 512], BF16, 2)
            SQ = sb(st, "SQw", [128, 4, 512], BF16, 4)
            RSq = sb(st, "RSw", [128, 4, 512], F32, 4)
            QN = sb(st, "QNw", [128, 4, 512], BF16, 4)
            T1 = sb(st, "T1w", [128, 4, 512], F32, 4)
            T2 = sb(st, "T2w", [128, 4, 512], F32, 4)
            k.dma("sp", [(COS.t[:], cos_t), (SIN.t[:], sin_t), (PSW.t[:], pswap)], writes=[COS.res[0], SIN.res[0], PSW.res[0]])
            cnt = {"w": 0, "ob": 0, "tb": 0, "q": 0}

            pending = []

            def advance():
                for g_ in list(pending):
                    try:
                        next(g_)
                    except StopIteration:
                        pending.remove(g_)

            def drain():
                while pending:
                    advance()

            def qk_epilogue(p, which, blk, dst, dst_res, done):
                qi = cnt["q"] % 4
                cnt["q"] += 1
                k.op(k.act, lambda e: e.activation(out=SQ.t[:, qi, :], in_=p.t[:], func=AF.Square), reads=[p.res[0]], writes=[SQ.res[qi]])
                yield
                p2 = ps_next()
                k.mm(p2.t[:], [(ONES_H.t[:], SQ.t[:, qi, :])], reads=[SQ.res[qi], ONES_H.res[0]], writes=[p2.res[0]])
                rsqrt(RSq.t[:, qi, :], RSq.res[qi], p2.t[:], p2.res[0])
                if blk == 0:
                    k.op(k.dve, lambda e: e.scalar_tensor_tensor(out=dst, in0=p.t[:], scalar=QKN.t[:, l, which:which + 1], in1=RSq.t[:, qi, :],
                                                                 op0=ALU.mult, op1=ALU.mult), reads=[p.res[0], RSq.res[qi], QKN.res[0]], writes=[dst_res])
                    done()
                    return
                k.op(k.dve, lambda e: e.scalar_tensor_tensor(out=QN.t[:, qi, :], in0=p.t[:], scalar=QKN.t[:, l, which:which + 1], in1=RSq.t[:, qi, :],
                                                             op0=ALU.mult, op1=ALU.mult), reads=[p.res[0], RSq.res[qi], QKN.res[0]], writes=[QN.res[qi]])
                pos = ((blk - 1) % 4) * 512
                k.op(k.dve, lambda e: e.tensor_tensor(out=T1.t[:, qi, :], in0=QN.t[:, qi, :], in1=COS.t[:, pos:pos + 512], op=ALU.mult),
                     reads=[QN.res[qi], COS.res[0]], writes=[T1.res[qi]])
                yield
                p3 = ps_next()
                k.mm(p3.t[:], [(PSW.t[:], QN.t[:, qi, :])], reads=[PSW.res[0], QN.res[qi]], writes=[p3.res[0]])
                k.op(k.dve, lambda e: e.tensor_tensor(out=T2.t[:, qi, :], in0=p3.t[:], in1=SIN.t[:, pos:pos + 512], op=ALU.mult),
                     reads=[p3.res[0], SIN.res[0]], writes=[T2.res[qi]])
                k.op(k.dve, lambda e: e.tensor_tensor(out=dst, in0=T1.t[:, qi, :], in1=T2.t[:, qi, :], op=ALU.add),
                     reads=[T1.res[qi], T2.res[qi]], writes=[dst_res])
                done()

            def mkw(si):
                def f(b):
                    load_slab_c(("win", l, si), WS, b, [(0, KC, win[l, :, si * 512:(si + 1) * 512])], KC, 512)
                return f
            strm = Stream([mkw(si) for _ in passes for si in range(8)])
            for ps_ in passes:
                for bi, blk in enumerate(ps_):
                    k.dma("sp", [(XNr.t[:, bi, :, :], xnT_v[:, :, blk * 512:(blk + 1) * 512])], writes=[XNr.res[bi]])
                for si in range(8):
                    wb = strm.need(cnt["w"])
                    cnt["w"] += 1
                    writeback(WS, wb, KC, 512)
                    if si in (0, 1, 2, 3, 4, 6, 7):
                        subs = (0, 1) if si == 2 else (0, 1, 2, 3)
                        for bi, blk in enumerate(ps_):
                            ob = cnt["ob"] % 4
                            cnt["ob"] += 1
                            ts = slice(blk * 512, (blk + 1) * 512)
                            n = len(subs)
                            if si <= 1:
                                dv = qkT_v[:, si * 4:si * 4 + 4, ts]
                            elif si == 2:
                                dv = qkT_v[:, 8:10, ts]
                            else:
                                c0 = {3: 0, 4: 4, 6: 8, 7: 12}[si]
                                dv = rT_v[:, c0:c0 + 4, ts]
                            left = [n]

                            def done(left=left, dv=dv, ob=ob, n=n):
                                left[0] -= 1
                                if left[0] == 0:
                                    k.dma("sp", [(dv, OB.t[:, ob, 0:n, :])], reads=[OB.res[ob]])
                            for sub in subs:
                                p = ps_next()
                                k.mm(p.t[:], [(WS.t[:, wb, kc, sub * 128:(sub + 1) * 128], XNr.t[:, bi, kc, :]) for kc in range(KC)],
                                     reads=[WS.res[wb], XNr.res[bi]], writes=[p.res[0]])
                                dst = OB.t[:, ob, sub, :]
                                if si <= 2:
                                    g_ = qk_epilogue(p, 0 if si < 2 else 1, blk, dst, OB.res[ob], done)
                                    next(g_)
                                    advance()
                                    pending.append(g_)
                                elif si == 6:
                                    k.op(k.act, lambda e, dst=dst, p=p: e.activation(out=dst, in_=p.t[:], func=AF.Silu), reads=[p.res[0]], writes=[OB.res[ob]])
                                    done()
                                else:
                                    copy_alt(dst, p.t[:], [p.res[0]], [OB.res[ob]])
                                    done()
                        if si == 2:
                            drain()
                    if si in (2, 4, 5):
                        c0, c1, v0 = {2: (256, 512, 0), 4: (0, 512, 768), 5: (0, 512, 256)}[si]
                        ncol = c1 - c0
                        for bi, blk in enumerate(ps_):
                            for tt in range(4):
                                p = ps_next()
                                k.mm(p.t[:, 0:ncol], [(XNr.t[:, bi, kc, tt * 128:(tt + 1) * 128], WS.t[:, wb, kc, c0:c1]) for kc in range(KC)],
                                     reads=[WS.res[wb], XNr.res[bi]], writes=[p.res[0]])
                                tb = cnt["tb"] % 2
                                cnt["tb"] += 1
                                copy_alt(TB.t[:, tb, 0:ncol], p.t[:, 0:ncol], [p.res[0]], [TB.res[tb]])
                                r0 = blk * 512 + tt * 128
                                k.dma("sp", [(vtok[r0:r0 + 128, v0:v0 + ncol], TB.t[:, tb, 0:ncol])], reads=[TB.res[tb]])
            k.barrier()

    def stage_attn(l, bg=False):
        sc = 128.0 ** -0.5
        with contextlib.ExitStack() as st:
            KT = sb(st, "KTa", [128, 2, 2304], BF16, 2)
            V = sb(st, "Va", [128, 2, 18, 128], BF16, 2)
            QT = sb(st, "QTa", [128, 3, 512], BF16, 3)
            PT = sb(st, "PTa", [128, 4, 512], BF16, 4)
            RD = sb(st, "RDa", [128, 2, 512], F32, 2)
            OA = sb(st, "OAa", [128, 2, 512], BF16, 2)
            ACC = sb(st, "ACCa", [128, 2, 512], BF16, 2)
            task = adaln_task(st, bg_slabs["attn0"]) if bg else None
            cnt = {"pt": 0, "s": 0}
            units = []
            for s in range(2):
                for g in range(2):
                    gi = s * 2 + g
                    for hq in range(4 * g, 4 * g + 4):
                        if l == 0:
                            units.append((gi, s, g, hq, ctx_tok(s), 256, 2))
                        for qb in range(4):
                            units.append((gi, s, g, hq, lat_tok(s, qb * 512, 512), 512, 18))
            loaded_g = set()

            def load_unit(u):
                gi, s, g, hq, toks, nq, nkc = units[u]
                if gi not in loaded_g:
                    loaded_g.add(gi)
                    kv = gi % 2
                    k.dma("sp", [(KT.t[:, kv, 0:256], qkT_v[:, 8 + g, ctx_tok(s)]), (KT.t[:, kv, 256:2304], qkT_v[:, 8 + g, lat_tok(s)])],
                          writes=[KT.res[kv]])
                    k.dma("sp", [(V.t[:, kv, 0:2, :], vtok[ctx_tok(s), g * 128:(g + 1) * 128].rearrange("(c p) d -> p c d", p=128)),
                                 (V.t[:, kv, 2:18, :], vtok[lat_tok(s), g * 128:(g + 1) * 128].rearrange("(c p) d -> p c d", p=128))],
                          writes=[V.res[kv]])
                k.dma("sp", [(QT.t[:, u % 3, 0:nq], qkT_v[:, hq, toks])], writes=[QT.res[u % 3]])
            load_unit(0)
            for u, (gi, s, g, hq, toks, nq, nkc) in enumerate(units):
                if u + 1 < len(units):
                    load_unit(u + 1)
                kv = gi % 2
                qi = u % 3
                pss = []
                pO, pD = (PS[4], PS[5]) if u % 2 == 0 else (PS[6], PS[7])
                LOOK = 3

                def S(kc):
                    p = PS[cnt["s"] % 4]
                    cnt["s"] += 1
                    k.mm(p.t[:, 0:nq], [(KT.t[:, kv, kc * 128:(kc + 1) * 128], QT.t[:, qi, 0:nq])],
                         reads=[KT.res[kv], QT.res[qi]], writes=[p.res[0]])
                    pss.append(p)
                for kc in range(min(LOOK, nkc)):
                    S(kc)
                for kc in range(nkc):
                    if kc + LOOK < nkc:
                        S(kc + LOOK)
                    p = pss[kc]
                    pt = cnt["pt"] % 4
                    cnt["pt"] += 1
                    k.op(k.act, lambda e, p=p, pt=pt: e.activation(out=PT.t[:, pt, 0:nq], in_=p.t[:, 0:nq], func=AF.Exp, scale=sc),
                         reads=[p.res[0]], writes=[PT.res[pt]])
                    mm1(pO.t[:, 0:nq], V.t[:, kv, kc, :], PT.t[:, pt, 0:nq], kc == 0, kc == nkc - 1, [V.res[kv], PT.res[pt]], [pO.res[0]])
                    ai = u % 2
                    if kc == 0:
                        k.op(k.dve, lambda e, pt=pt: e.tensor_copy(ACC.t[:, ai, 0:nq], PT.t[:, pt, 0:nq]), reads=[PT.res[pt]], writes=[ACC.res[ai]])
                    else:
                        k.op(k.dve, lambda e, pt=pt: e.tensor_tensor(out=ACC.t[:, ai, 0:nq], in0=ACC.t[:, ai, 0:nq], in1=PT.t[:, pt, 0:nq], op=ALU.add),
                             reads=[PT.res[pt], ACC.res[ai]], writes=[ACC.res[ai]])
                mm1(pD.t[:, 0:nq], ONES_1.t[:], ACC.t[:, ai, 0:nq], True, True, [ONES_1.res[0], ACC.res[ai]], [pD.res[0]])
                oi = u % 2
                k.op(k.dve, lambda e: e.reciprocal(RD.t[:, oi, 0:nq], pD.t[:, 0:nq]), reads=[pD.res[0]], writes=[RD.res[oi]])
                k.op(k.dve, lambda e: e.tensor_tensor(out=OA.t[:, oi, 0:nq], in0=pO.t[:, 0:nq], in1=RD.t[:, oi, 0:nq], op=ALU.mult),
                     reads=[pO.res[0], RD.res[oi]], writes=[OA.res[oi]])
                k.dma("sp", [(yT_v[:, 4 + hq, toks], OA.t[:, oi, 0:nq])], reads=[OA.res[oi]])
                if task is not None:
                    next(task, None)
            if task is not None:
                for _ in task:
                    pass
                bg_finish("attn0")
            k.barrier()

    def stage_ret(l):
        cs = 128.0 ** -0.5
        with contextlib.ExitStack() as st:
            RT = sb(st, "RTr", [128, 8, 128], F32)
            LG = sb(st, "LGr", [128, 8], F32)
            E = sb(st, "Er", [128, 2, 128], F32)
            MASK = sb(st, "MASKr", [128, 128], F32)
            DQ = sb(st, "DQr", [128, 2, 128], F32)
            DK = sb(st, "DKr", [128, 2], F32)
            DCc = sb(st, "DCr", [128, 2], F32)
            QT = sb(st, "QTr", [128, 2304], BF16)
            KTt = sb(st, "KTr", [128, 2304], BF16)
            G = sb(st, "Gr", [128, 2304], BF16)
            KTOK = sb(st, "KTOKr", [128, 18, 128], BF16)
            VTOK = sb(st, "VTOKr", [128, 18, 128], BF16)
            QD = sb(st, "QDr", [128, 2, 2304], BF16, 2)
            KD = sb(st, "KDr", [128, 2, 18, 128], BF16, 2)
            U = sb(st, "Ur", [128, 2, 18, 128], F32, 2)
            S = sb(st, "Sr", [128, 2, 18, 128], F32, 2)
            SBF = sb(st, "SBFr", [128, 2, 18, 128], BF16)
            PM = sb(st, "PMr", [128, 2, 4, 128], BF16, 2)
            OSQ = sb(st, "OSQr", [128, 2, 512], BF16, 2)
            RS = sb(st, "RSr", [128, 2, 512], F32, 2)
            T = sb(st, "Tr", [128, 2, 512], F32, 2)
            YR = sb(st, "YRr", [128, 2304], BF16)
            k.dma("sp", [(RT.t[:], rtab)], writes=[RT.res[0]])
            k.op(k.act, lambda e: e.activation(out=LG.t[:], in_=RDEC.t[:, l * 8:(l + 1) * 8], func=AF.Exp), reads=[RDEC.res[0]], writes=[LG.res[0]])
            k.op(k.dve, lambda e: e.tensor_scalar(out=LG.t[:], in0=LG.t[:], scalar1=-1.0, scalar2=None, op0=ALU.mult), reads=[LG.res[0]], writes=[LG.res[0]])
            cnt = {"g": 0}
            for h in range(4):
                lgs = [LG.t[:, h:h + 1], LG.t[:, 4 + h:5 + h]]
                for d_ in range(2):
                    k.op(k.act, lambda e, d_=d_: e.activation(out=E.t[:, d_, :], in_=RT.t[:, 2 * d_, :], func=AF.Exp, scale=lgs[d_]),
                         reads=[RT.res[0], LG.res[0]], writes=[E.res[0]])
                    k.op(k.dve, lambda e, d_=d_: e.tensor_tensor(out=E.t[:, d_, :], in0=E.t[:, d_, :], in1=RT.t[:, 2 * d_ + 1, :], op=ALU.mult),
                         reads=[E.res[0], RT.res[0]], writes=[E.res[0]])
                    k.op(k.act, lambda e, d_=d_: e.activation(out=DQ.t[:, d_, :], in_=RT.t[:, 4 + d_, :], func=AF.Exp, scale=lgs[d_]),
                         reads=[RT.res[0], LG.res[0]], writes=[DQ.res[0]])
                    k.op(k.act, lambda e, d_=d_: e.activation(out=DK.t[:, d_:d_ + 1], in_=RT.t[:, 6 + d_, 0:1], func=AF.Exp, scale=lgs[d_]),
                         reads=[RT.res[0], LG.res[0]], writes=[DK.res[0]])
                    k.op(k.act, lambda e, d_=d_: e.activation(out=DCc.t[:, d_:d_ + 1], in_=RT.t[:, 5, 0:1], func=AF.Exp, scale=lgs[d_]),
                         reads=[RT.res[0], LG.res[0]], writes=[DCc.res[0]])
                k.op(k.dve, lambda e: e.tensor_scalar(out=DK.t[:], in0=DK.t[:], scalar1=cs, scalar2=None, op0=ALU.mult), reads=[DK.res[0]], writes=[DK.res[0]])
                k.op(k.dve, lambda e: e.tensor_tensor(out=MASK.t[:], in0=E.t[:, 0, :], in1=E.t[:, 1, :], op=ALU.add), reads=[E.res[0]], writes=[MASK.res[0]])
                k.op(k.dve, lambda e: e.tensor_scalar(out=MASK.t[:], in0=MASK.t[:], scalar1=cs, scalar2=None, op0=ALU.mult), reads=[MASK.res[0]], writes=[MASK.res[0]])
                for s in range(2):
                    def ld(dst, row):
                        return [(dst[:, 0:256], rT_v[:, row, ctx_tok(s)]), (dst[:, 256:2304], rT_v[:, row, lat_tok(s)])]
                    k.dma("sp", ld(QT.t, h), writes=[QT.res[0]])
                    k.dma("sp", ld(KTt.t, 4 + h), writes=[KTt.res[0]])
                    k.dma("sp", ld(G.t, 8 + h), writes=[G.res[0]])

                    def ldt(dst, c0):
                        return [(dst[:, 0:2, :], vtok[ctx_tok(s), c0:c0 + 128].rearrange("(c p) d -> p c d", p=128)),
                                (dst[:, 2:18, :], vtok[lat_tok(s), c0:c0 + 128].rearrange("(c p) d -> p c d", p=128))]
                    k.dma("sp", ldt(KTOK.t, 768 + h * 128), writes=[KTOK.res[0]])
                    k.dma("sp", ldt(VTOK.t, 256 + h * 128), writes=[VTOK.res[0]])
                    QT3 = QT.t[:].rearrange("p (c i) -> p c i", c=18)
                    for d_ in range(2):
                        dqb = DQ.t[:, d_, :].unsqueeze(1).to_broadcast([128, 18, 128])
                        k.op(k.pool, lambda e, d_=d_, dqb=dqb: e.tensor_tensor(out=QD.t[:, d_, :].rearrange("p (c i) -> p c i", c=18), in0=QT3, in1=dqb, op=ALU.mult),
                             reads=[QT.res[0], DQ.res[0]], writes=[QD.res[d_]])
                        k.op(k.act, lambda e, d_=d_: e.activation(out=KD.t[:, d_, :, :], in_=KTOK.t[:], func=AF.Identity, scale=DK.t[:, d_:d_ + 1]),
                             reads=[KTOK.res[0], DK.res[0]], writes=[KD.res[d_]])
                    for d_ in range(2):
                        for c0, n in ((0, 4), (4, 4), (8, 4), (12, 4), (16, 2)):
                            p = ps_next()

                            def fn(pe, d_=d_, c0=c0, n=n, p=p):
                                ins = None
                                for jj in range(n):
                                    ins = pe.matmul(p.t[:, jj * 128:(jj + 1) * 128], lhsT=KD.t[:, d_, c0 + jj, :], rhs=VTOK.t[:, c0 + jj, :], start=True, stop=True)
                                return ins
                            k.op(k.pe, fn, reads=[KD.res[d_], VTOK.res[0]], writes=[p.res[0]], skip_self=True)
                            copy_alt(U.t[:, d_, c0:c0 + n, :], p.t[:, 0:n * 128].rearrange("p (c e) -> p c e", c=n), [p.res[0]], [U.res[d_]])
                    orders = [list(range(18)), [1, 0] + list(range(17, 1, -1))]
                    for d_ in range(2):
                        k.op(k.dve, lambda e, d_=d_: e.memset(S.t[:, d_, orders[d_][0], :], 0.0), writes=[S.res[d_]])
                    for idx in range(17):
                        for d_ in range(2):
                            cur, nxt = orders[d_][idx], orders[d_][idx + 1]
                            k.op(k.dve, lambda e, d_=d_, cur=cur, nxt=nxt: e.scalar_tensor_tensor(
                                out=S.t[:, d_, nxt, :], in0=S.t[:, d_, cur, :], scalar=DCc.t[:, d_:d_ + 1], in1=U.t[:, d_, cur, :],
                                op0=ALU.mult, op1=ALU.add), reads=[S.res[d_], U.res[d_], DCc.res[0]], writes=[S.res[d_]])
                    k.op(k.act, lambda e: e.copy(SBF.t[:], S.t[:]), reads=[S.res[0], S.res[1]], writes=[SBF.res[0]])
                    for (c0, n) in ((0, 2), (2, 4), (6, 4), (10, 4), (14, 4)):
                        if l == 1 and c0 == 0:
                            continue
                        gi = cnt["g"] % 2
                        cnt["g"] += 1
                        pin = ps_next()

                        def fin(pe, c0=c0, n=n, pin=pin):
                            ins = None
                            for jj in range(n):
                                c = c0 + jj
                                ins = pe.matmul(pin.t[:, jj * 128:(jj + 1) * 128], lhsT=KTt.t[:, c * 128:(c + 1) * 128], rhs=QT.t[:, c * 128:(c + 1) * 128], start=True, stop=True)
                            return ins
                        k.op(k.pe, fin, reads=[KTt.res[0], QT.res[0]], writes=[pin.res[0]], skip_self=True)
                        mb = MASK.t[:].unsqueeze(1).to_broadcast([128, n, 128])
                        k.op(k.dve, lambda e, gi=gi, n=n, pin=pin, mb=mb: e.tensor_tensor(
                            out=PM.t[:, gi, 0:n, :], in0=pin.t[:, 0:n * 128].rearrange("p (c i) -> p c i", c=n), in1=mb, op=ALU.mult),
                            reads=[pin.res[0], MASK.res[0]], writes=[PM.res[gi]])
                        po = ps_next()

                        def fo(pe, c0=c0, n=n, po=po, gi=gi):
                            ins = None
                            for jj in range(n):
                                c = c0 + jj
                                o_ = po.t[:, jj * 128:(jj + 1) * 128]
                                pe.matmul(o_, lhsT=VTOK.t[:, c, :], rhs=PM.t[:, gi, jj, :], start=True, stop=False)
                                pe.matmul(o_, lhsT=SBF.t[:, 0, c, :], rhs=QD.t[:, 0, c * 128:(c + 1) * 128], start=False, stop=False)
                                ins = pe.matmul(o_, lhsT=SBF.t[:, 1, c, :], rhs=QD.t[:, 1, c * 128:(c + 1) * 128], start=False, stop=True)
                            return ins
                        k.op(k.pe, fo, reads=[VTOK.res[0], PM.res[gi], SBF.res[0], QD.res[0], QD.res[1]], writes=[po.res[0]], skip_self=True)
                        w = n * 128
                        k.op(k.act, lambda e, gi=gi, po=po, w=w: e.activation(out=OSQ.t[:, gi, 0:w], in_=po.t[:, 0:w], func=AF.Square), reads=[po.res[0]], writes=[OSQ.res[gi]])
                        pss = ps_next()
                        k.mm(pss.t[:, 0:w], [(ONES_H.t[:], OSQ.t[:, gi, 0:w])], reads=[OSQ.res[gi], ONES_H.res[0]], writes=[pss.res[0]])
                        rsqrt(RS.t[:, gi, 0:w], RS.res[gi], pss.t[:, 0:w], pss.res[0])
                        k.op(k.dve, lambda e, gi=gi, po=po, w=w: e.tensor_tensor(out=T.t[:, gi, 0:w], in0=po.t[:, 0:w], in1=RS.t[:, gi, 0:w], op=ALU.mult),
                             reads=[po.res[0], RS.res[gi]], writes=[T.res[gi]])
                        k.op(k.pool, lambda e, gi=gi, c0=c0, w=w: e.tensor_tensor(out=YR.t[:, c0 * 128:c0 * 128 + w], in0=T.t[:, gi, 0:w], in1=G.t[:, c0 * 128:c0 * 128 + w], op=ALU.mult),
                             reads=[T.res[gi], G.res[0]], writes=[YR.res[0]])
                    pairs = [(yT_v[:, 12 + h, lat_tok(s)], YR.t[:, 256:2304])]
                    if l == 0:
                        pairs.append((yT_v[:, 12 + h, ctx_tok(s)], YR.t[:, 0:256]))
                    k.dma("sp", pairs, reads=[YR.res[0]])
            k.barrier()

    def stage_four(l):
        with contextlib.ExitStack() as st:
            DCt = sb(st, "DCf", [128, 256], BF16)
            ZT = sb(st, "ZTf", [128, 4, 2048], BF16)
            A = sb(st, "Af", [128, 16, 4, 256], BF16, 32)
            TL = sb(st, "TLf", [128, 2, 2, 16, 512], BF16, 2)
            YF = sb(st, "YFf", [128, 2, 512], BF16, 2)
            k.dma("sp", [(DCt.t[:], dftc)], writes=[DCt.res[0]])
            cnt = {"t": 0, "y": 0}
            units = []
            for s in range(2):
                units.append((lat_tok(s), 2048, dftL))
            if l == 0:
                for s in range(2):
                    units.append((ctx_tok(s), 256, dftS))
            for (toks, L, tab) in units:
                ntt = L // 128
                bw = min(512, L)
                k.dma("sp", [(ZT.t[:, :, 0:L], rT_v[:, 12:16, toks])], writes=[ZT.res[0]])
                for tt in range(ntt):
                    for gp in range(2):
                        p = ps_next()

                        def fa(pe, tt=tt, gp=gp, p=p):
                            ins = None
                            for gi in range(2):
                                ins = pe.matmul(p.t[:, gi * 256:(gi + 1) * 256], lhsT=ZT.t[:, 2 * gp + gi, tt * 128:(tt + 1) * 128], rhs=DCt.t[:], start=True, stop=True)
                            return ins
                        k.op(k.pe, fa, reads=[ZT.res[0], DCt.res[0]], writes=[p.res[0]], skip_self=True)
                        copy_alt(A.t[:, tt, 2 * gp:2 * gp + 2, :], p.t[:].rearrange("p (g c) -> p g c", g=2), [p.res[0]], [A.res[tt * 2 + gp]])
                for tb in range(L // bw):
                    ti = cnt["t"] % 2
                    cnt["t"] += 1
                    tv = tab.rearrange("a (c p) n -> a p c n", p=128)
                    k.dma("sp", [(TL.t[:, ti, a, 0:ntt, 0:bw], tv[a, :, :, tb * bw:(tb + 1) * bw]) for a in range(2)], writes=[TL.res[ti]])
                    for g in range(4):
                        p = ps_next()
                        pairs = []
                        for tt in range(ntt):
                            pairs.append((A.t[:, tt, g, 0:128], TL.t[:, ti, 0, tt, 0:bw]))
                            pairs.append((A.t[:, tt, g, 128:256], TL.t[:, ti, 1, tt, 0:bw]))
                        k.mm(p.t[:, 0:bw], pairs, reads=A.res + [TL.res[ti]], writes=[p.res[0]])
                        yi = cnt["y"] % 2
                        cnt["y"] += 1
                        copy_alt(YF.t[:, yi, 0:bw], p.t[:, 0:bw], [p.res[0]], [YF.res[yi]])
                        t0 = toks.start + tb * bw
                        k.dma("sp", [(yT_v[:, g, t0:t0 + bw], YF.t[:, yi, 0:bw])], reads=[YF.res[yi]])
            k.barrier()

    def stage_merge(l, blocks):
        passes = [blocks[i:i + 2] for i in range(0, len(blocks), 2)]
        wmg = Wd["w_merge_gate"]
        with contextlib.ExitStack() as st:
            XNr = sb(st, "XNm", [128, 2, KC, 512], BF16, 2)
            Y = sb(st, "Ym", [128, 2, KC, 512], BF16, 2)
            WG = sb(st, "WGm", [128, 2, 4 * KC, 256], BF16, 2)
            SGm = sb(st, "SGm", [128, 3, 512], F32, 3)
            ACC = sb(st, "ACCm", [128, 2, 512], F32, 2)
            Tm = sb(st, "Tm", [128, 2, 512], F32, 2)
            MB = sb(st, "MBm", [128, 2, 2, 512], BF16, 2)
            cnt = {"w": 0, "sg": 0, "acc": 0, "t": 0, "mb": 0}

            def mkm(s_):
                def f(b):
                    cs_ = slice(s_ * 256, (s_ + 1) * 256)
                    parts = [(gi * KC, (gi + 1) * KC, wmg[l, :, gi * D + s_ * 256: gi * D + (s_ + 1) * 256]) for gi in range(3)]
                    parts += [(48, 52, Wd["w_branch_fourier"][l, :, cs_]), (52, 60, Wd["w_branch_attn"][l, :, cs_]), (60, 64, Wd["w_branch_ret"][l, :, cs_])]
                    load_slab_c(("mg", l, s_), WG, b, parts, 4 * KC, 256)
                return f
            strm = Stream([mkm(s_) for _ in passes for s_ in range(8)])
            for ps_ in passes:
                for bi, blk in enumerate(ps_):
                    ts = slice(blk * 512, (blk + 1) * 512)
                    k.dma("sp", [(XNr.t[:, bi, :, :], xnT_v[:, :, ts])], writes=[XNr.res[bi]])
                    k.dma("sp", [(Y.t[:, bi, :, :], yT_v[:, :, ts])], writes=[Y.res[bi]])
                for s in range(8):
                    wb = strm.need(cnt["w"])
                    cnt["w"] += 1
                    writeback(WG, wb, 4 * KC, 256)
                    for bi, blk in enumerate(ps_):
                        j = blk_j(blk)
                        mb = cnt["mb"] % 2
                        cnt["mb"] += 1
                        for sub in range(2):
                            dc = 2 * s + sub
                            ai = cnt["acc"] % 2
                            cnt["acc"] += 1
                            for gi, (k0, k1) in enumerate(((0, 4), (4, 12), (12, 16))):
                                pg = ps_next()
                                k.mm(pg.t[:], [(WG.t[:, wb, gi * KC + kc, sub * 128:(sub + 1) * 128], XNr.t[:, bi, kc, :]) for kc in range(KC)],
                                     reads=[WG.res[wb], XNr.res[bi]], writes=[pg.res[0]])
                                pb = ps_next()
                                k.mm(pb.t[:], [(WG.t[:, wb, 48 + kc, sub * 128:(sub + 1) * 128], Y.t[:, bi, kc, :]) for kc in range(k0, k1)],
                                     reads=[WG.res[wb], Y.res[bi]], writes=[pb.res[0]])
                                sg = cnt["sg"] % 3
                                cnt["sg"] += 1
                                k.op(k.act, lambda e, sg=sg, pg=pg, gi=gi, dc=dc: e.activation(out=SGm.t[:, sg, :], in_=pg.t[:], func=AF.Sigmoid,
                                                                                           bias=BMG.t[:, l, gi * 16 + dc:gi * 16 + dc + 1], scale=1.0),
                                     reads=[pg.res[0], BMG.res[0]], writes=[SGm.res[sg]])
                                if gi == 0:
                                    k.op(k.dve, lambda e, sg=sg, pb=pb, ai=ai: e.tensor_tensor(out=ACC.t[:, ai, :], in0=SGm.t[:, sg, :], in1=pb.t[:], op=ALU.mult),
                                         reads=[SGm.res[sg], pb.res[0]], writes=[ACC.res[ai]])
                                else:
                                    ti = cnt["t"] % 2
                                    cnt["t"] += 1
                                    k.op(k.dve, lambda e, sg=sg, pb=pb, ti=ti: e.tensor_tensor(out=Tm.t[:, ti, :], in0=SGm.t[:, sg, :], in1=pb.t[:], op=ALU.mult),
                                         reads=[SGm.res[sg], pb.res[0]], writes=[Tm.res[ti]])
                                    if gi == 1:
                                        k.op(k.dve, lambda e, ai=ai, ti=ti: e.tensor_tensor(out=ACC.t[:, ai, :], in0=ACC.t[:, ai, :], in1=Tm.t[:, ti, :], op=ALU.add),
                                             reads=[ACC.res[ai], Tm.res[ti]], writes=[ACC.res[ai]])
                                    else:
                                        k.op(k.dve, lambda e, ai=ai, ti=ti, mb=mb, sub=sub: e.tensor_tensor(out=MB.t[:, mb, sub, :], in0=ACC.t[:, ai, :], in1=Tm.t[:, ti, :], op=ALU.add),
                                             reads=[ACC.res[ai], Tm.res[ti]], writes=[MB.res[mb]])
                        k.dma("sp", [(mT_v[:, 2 * s:2 * s + 2, blk * 512:(blk + 1) * 512], MB.t[:, mb, :, :])], reads=[MB.res[mb]])
            k.barrier()

    allb = list(range(9))
    latb = list(range(1, 9))
    if want("in"):
        stage_in()
    if want("adaln"):
        stage_adaln_pre()
    if want("ffn1_0"):
        stage_norm(0, 0, allb)
    for l in range(DEPTH):
        mixb = allb if l == 0 else latb
        last = (l == DEPTH - 1)
        if want(f"ffn1_{l}"):
            stage_ffn_gu(l, Wd["ffn1_w_gate"], Wd["ffn1_w_up"], allb, "f1", bg=("gu0" if l == 0 else None))
            stage_gemm_resid(l, hidT_v, FKC, Wd["ffn1_w_down"][l], 3, allb, "d1", norm_after=(l, 1, allb))
        if want(f"win_{l}"):
            stage_win(l, allb)
        if want(f"attn_{l}"):
            stage_attn(l, bg=(l == 0))
        if want(f"ret_{l}"):
            stage_ret(l)
        if want(f"four_{l}"):
            stage_four(l)
        if want(f"merge_{l}"):
            stage_merge(l, mixb)
            stage_gemm_resid(l, mT_v, KC, Wd["w_out"][l], 4, mixb, "o", norm_after=(l, 2, mixb))
        if want(f"ffn2_{l}"):
            stage_ffn_gu(l, Wd["ffn2_w_gate"], Wd["ffn2_w_up"], mixb, "f2")
            stage_gemm_resid(l, hidT_v, FKC, Wd["ffn2_w_down"][l], 5, mixb, "d2", norm_after=(None if last else (l + 1, 0, allb)))
            if not last:
                pass
    if want("final"):
        stage_final()
    k.barrier()


def _consts():
    c = {}
    c["ident"] = np.eye(128, dtype=np.float32)
    t = np.arange(2048)
    row = (t // 64).astype(np.float32)
    col = (t % 64).astype(np.float32)
    inv = (10000.0 ** (-np.arange(0, 64, 2, dtype=np.float32) / 64)).astype(np.float32)
    ang = np.zeros((128, 2048), np.float32)
    for d in range(128):
        f = inv[d % 32]
        ang[d] = (row if d < 64 else col) * f
    c["cos_t"] = np.cos(ang).astype(np.float32)
    c["sin_t"] = np.sin(ang).astype(np.float32)
    P = np.zeros((128, 128), np.float32)
    for base in (0, 64):
        for i in range(32):
            P[base + i, base + 32 + i] = -1.0
            P[base + 32 + i, base + i] = 1.0
    c["pswap"] = P.T.copy().astype(ml_dtypes.bfloat16)
    p = np.arange(128, dtype=np.float32)[:, None]
    i = np.arange(128, dtype=np.float32)[None, :]
    rt = np.zeros((128, 8, 128), np.float32)
    rt[:, 0] = np.maximum(i - p, 0)
    rt[:, 1] = (i >= p)
    rt[:, 2] = np.maximum(p - i, 0)
    rt[:, 3] = (p >= i)
    rt[:, 4] = i + 1.0
    rt[:, 5] = 128.0 - i
    rt[:, 6] = 127.0 - p
    rt[:, 7] = p + 0.0 * i
    c["rtab"] = rt
    a = np.arange(128)
    ph = 2 * np.pi * np.outer(a, a) / 128
    c["dftc"] = (np.concatenate([np.cos(ph), np.sin(ph)], 1) / np.sqrt(128)).astype(ml_dtypes.bfloat16)
    for nm, L in (("dftL", 2048), ("dftS", 256)):
        a = np.arange(L)
        ph = 2 * np.pi * (np.outer(a, a) % L) / L
        c[nm] = np.stack([np.cos(ph), -np.sin(ph)]).astype(np.float32) / np.sqrt(L)
        c[nm] = c[nm].astype(ml_dtypes.bfloat16)
    return c


def make_in_maps(inputs):
    f = lambda a: np.ascontiguousarray(np.asarray(a, dtype=np.float32))
    cst = _consts()
    shared = {}
    for nm in ("w_ada", "ffn1_w_gate", "ffn1_w_up", "ffn1_w_down", "ffn2_w_gate", "ffn2_w_up", "ffn2_w_down", "w_in",
               "w_branch_fourier", "w_branch_attn", "w_branch_ret", "w_merge_gate", "w_out"):
        shared[nm] = f(inputs[nm])
    shared["b_ada_l"] = f(np.transpose(np.asarray(inputs["b_ada"]).reshape(DEPTH, 144, 128), (2, 0, 1)))
    g = np.stack([inputs["ffn1_norm"][0], inputs["mix_norm"][0], inputs["ffn2_norm"][0],
                  inputs["ffn1_norm"][1], inputs["mix_norm"][1], inputs["ffn2_norm"][1], inputs["final_norm"]])
    shared["gains"] = f(np.transpose(np.asarray(g).reshape(7, KC, 128), (2, 0, 1)))
    shared["qkn"] = f(np.stack([np.asarray(inputs["q_norm"]).T, np.asarray(inputs["k_norm"]).T], axis=2))
    shared["rdec"] = f(np.broadcast_to(np.asarray(inputs["ret_decay"]).reshape(1, DEPTH * 8), (128, DEPTH * 8)))
    shared["bmg"] = f(np.transpose(np.asarray(inputs["b_merge_gate"]).reshape(DEPTH, 48, 128), (2, 0, 1)))
    shared.update(cst)
    maps = []
    x = np.asarray(inputs["x"])
    c = np.asarray(inputs["c"])
    ctx = np.asarray(inputs["ctx"])
    cc = np.asarray(inputs["c_ctx"])
    for i in range(NCORES):
        m = dict(shared)
        m["xin"] = f(x[2 * i:2 * i + 2])
        m["ctxin"] = f(ctx[2 * i:2 * i + 2])
        cat = np.stack([c[2 * i], c[2 * i + 1], cc], axis=1)
        m["ccat"] = f(np.transpose(cat.reshape(KC, 128, 3), (1, 0, 2)))
        maps.append(m)
    return maps


def kernel(**inputs):
    nc = build_program()
    maps = make_in_maps(inputs)
    res = run_bass_kernel_spmd(nc, maps, core_ids=list(range(NCORES)))
    return np.concatenate([r["out"] for r in res.results], axis=0).astype(np.float32)
```

```python
import contextlib
import numpy as np
import ml_dtypes
import concourse.bass as bass
import concourse.mybir as mybir
from concourse.bass_utils import run_bass_kernel_spmd

F32 = mybir.dt.float32
BF16 = mybir.dt.bfloat16
AF = mybir.ActivationFunctionType
ALU = mybir.AluOpType

NCORES = 8
D = 2048
KC = 16
FF = 5632
FKC = 44
NT = 4608
NBLK = 9
EPS = 1e-6
DEPTH = 2
NS_ENG = 4
NS_DMA = 16


def blk_j(b):
    return 2 if b == 0 else (0 if b < 5 else 1)


class Res:
    __slots__ = ("w", "r")

    def __init__(self):
        self.w = []
        self.r = {}


class Sem:
    def __init__(self, h, idx, owner):
        self.h = h
        self.idx = idx
        self.owner = owner
        self.count = 0


class Eng:
    def __init__(self, name, be):
        self.name = name
        self.be = be
        self.sems = []
        self.n = 0
        self.waited = {}


class K:
    def __init__(self, nc, stack):
        self.nc = nc
        self.nsem = 0
        self.pe = Eng("pe", nc.tensor)
        self.act = Eng("act", nc.scalar)
        self.dve = Eng("dve", nc.vector)
        self.pool = Eng("pool", nc.gpsimd)
        self.sp = Eng("sp", nc.sync)
        self.engs = [self.pe, self.act, self.dve, self.pool, self.sp]
        for e in self.engs[:4]:
            for i in range(NS_ENG):
                e.sems.append(self._mk(stack, f"s_{e.name}{i}", e))
        self.dsem = {}
        self.dn = {}
        for q, e in (("sp", self.sp), ("pool", self.pool), ("act", self.act)):
            self.dsem[q] = [self._mk(stack, f"d_{q}{i}", None) for i in range(NS_DMA)]
            self.dn[q] = 0
        self.dq_eng = {"sp": self.sp, "pool": self.pool, "act": self.act}

    def _mk(self, stack, name, owner):
        h = stack.enter_context(self.nc.semaphore(name))
        s = Sem(h, self.nsem, owner)
        self.nsem += 1
        return s

    def _wait(self, eng, deps, skip_self=False):
        best = {}
        for (sem, val) in deps:
            if skip_self and sem.owner is eng:
                continue
            if eng.waited.get(sem.idx, 0) >= val:
                continue
            if sem.idx not in best or best[sem.idx][1] < val:
                best[sem.idx] = (sem, val)
        for sem, val in best.values():
            eng.be.wait_ge(sem.h, val)
            eng.waited[sem.idx] = val

    @staticmethod
    def _deps(reads, writes):
        deps = []
        for r in reads:
            deps.extend(r.w)
        for w in writes:
            deps.extend(w.w)
            deps.extend(w.r.values())
        return deps

    @staticmethod
    def _commit(evs, reads, writes):
        for r in reads:
            for ev in evs:
                old = r.r.get(ev[0].idx)
                if old is None or old[1] < ev[1]:
                    r.r[ev[0].idx] = ev
        for w in writes:
            w.w = list(evs)
            w.r = {}

    def op(self, eng, fn, reads=(), writes=(), skip_self=False):
        self._wait(eng, self._deps(reads, writes), skip_self)
        ins = fn(eng.be)
        k = eng.n
        eng.n += 1
        sem = eng.sems[k % NS_ENG]
        val = k // NS_ENG + 1
        ins.then_inc(sem.h, 1)
        ev = (sem, val)
        self._commit([ev], reads, writes)
        return ev

    def dma(self, q, pairs, reads=(), writes=(), acc_writes=()):
        eng = self.dq_eng[q]
        self._wait(eng, self._deps(reads, writes))
        evs = []
        for (o, i) in pairs:
            k = self.dn[q]
            self.dn[q] += 1
            sem = self.dsem[q][k % NS_DMA]
            if sem.count > 0:
                self._wait(eng, [(sem, 16 * sem.count)])
            eng.be.dma_start(out=o, in_=i).then_inc(sem.h, 16)
            sem.count += 1
            evs.append((sem, 16 * sem.count))
        self._commit(evs, reads, writes)
        for w in acc_writes:
            best = {ev[0].idx: ev for ev in w.w}
            for ev in evs:
                if ev[0].idx not in best or best[ev[0].idx][1] < ev[1]:
                    best[ev[0].idx] = ev
            w.w = list(best.values())
        return evs

    def all_events(self):
        evs = []
        for e in self.engs[:4]:
            if e.n > 0:
                k = e.n - 1
                evs.append((e.sems[k % NS_ENG], k // NS_ENG + 1))
        for q in self.dsem:
            for sem in self.dsem[q]:
                if sem.count > 0:
                    evs.append((sem, 16 * sem.count))
        return evs

    def barrier(self):
        evs = self.all_events()
        for e in self.engs:
            self._wait(e, evs)

    def mm(self, out_ap, pairs, reads, writes):
        n = len(pairs)

        def fn(pe):
            ins = None
            for i, (l, r) in enumerate(pairs):
                ins = pe.matmul(out_ap, lhsT=l, rhs=r, start=(i == 0), stop=(i == n - 1))
            return ins

        return self.op(self.pe, fn, reads, writes, skip_self=True)


class Tile:
    def __init__(self, t, n=1):
        self.t = t
        self.res = [Res() for _ in range(n)]


def build_program(dbg=(), stages=None):
    nc = bass.Bass("TRN2", target_bir_lowering=False)
    stack = contextlib.ExitStack()
    with stack:
        _emit(nc, stack, dbg, stages)
    return nc


def _emit(nc, gstack, dbg, stages):
    def din(name, shape, dt=F32):
        return nc.dram_tensor(name, list(shape), dt, kind="ExternalInput").ap()

    def dscr(name, shape, dt):
        kind = "ExternalOutput" if name in dbg else "Internal"
        return nc.dram_tensor(name, list(shape), dt, kind=kind).ap()

    xin = din("xin", [2, 2048, D])
    ctxin = din("ctxin", [2, 256, D])
    ccat = din("ccat", [128, KC, 3])
    w_ada = din("w_ada", [DEPTH, D, 9 * D])
    b_ada = din("b_ada_l", [128, DEPTH, 144])
    gains = din("gains", [128, 7, KC])
    qkn = din("qkn", [128, DEPTH, 2])
    rdec = din("rdec", [128, DEPTH * 8])
    bmg = din("bmg", [128, DEPTH, 48])
    Wd = {}
    for nm, shp in (("ffn1_w_gate", [D, FF]), ("ffn1_w_up", [D, FF]), ("ffn1_w_down", [FF, D]),
                    ("ffn2_w_gate", [D, FF]), ("ffn2_w_up", [D, FF]), ("ffn2_w_down", [FF, D]),
                    ("w_in", [D, 4096]), ("w_branch_fourier", [512, D]), ("w_branch_attn", [1024, D]),
                    ("w_branch_ret", [512, D]), ("w_merge_gate", [D, 3 * D]), ("w_out", [D, D])):
        Wd[nm] = din(nm, [DEPTH] + shp)
    ident = din("ident", [128, 128])
    cos_t = din("cos_t", [128, 2048])
    sin_t = din("sin_t", [128, 2048])
    pswap = din("pswap", [128, 128], BF16)
    rtab = din("rtab", [128, 8, 128])
    dftc = din("dftc", [128, 256], BF16)
    dftL = din("dftL", [2, 2048, 2048], BF16)
    dftS = din("dftS", [2, 256, 256], BF16)
    out = nc.dram_tensor("out", [2, 2048, D], F32, kind="ExternalOutput").ap()

    hT = dscr("hT", [D, NT], F32)
    xnT = dscr("xnT", [D, NT], BF16)
    hidT = dscr("hidT", [FF, NT], BF16)
    qkT = dscr("qkT", [1280, NT], BF16)
    rT = dscr("rT", [2048, NT], BF16)
    vtok = dscr("vtok", [NT, 1280], BF16)
    yT = dscr("yT", [2048, NT], BF16)
    mT = dscr("mT", [D, NT], BF16)
    moddbg = dscr("moddbg", [128, DEPTH * 144 * 3], F32)

    hT_v = hT.rearrange("(c p) t -> p c t", p=128)
    xnT_v = xnT.rearrange("(c p) t -> p c t", p=128)
    hidT_v = hidT.rearrange("(c p) t -> p c t", p=128)
    qkT_v = qkT.rearrange("(c p) t -> p c t", p=128)
    rT_v = rT.rearrange("(c p) t -> p c t", p=128)
    yT_v = yT.rearrange("(c p) t -> p c t", p=128)
    mT_v = mT.rearrange("(c p) t -> p c t", p=128)

    k = K(nc, gstack)
    want = (lambda s: True) if stages is None else (lambda s: s in stages)

    sbn = [0]

    def sb(stack, name, shape, dt, n=1):
        sbn[0] += 1
        t = stack.enter_context(nc.sbuf_tensor(f"{name}_{sbn[0]}", list(shape), dt))
        return Tile(t, n)

    PS = [Tile(gstack.enter_context(nc.psum_tensor(f"ps{i}", [128, 512], F32))) for i in range(8)]
    MOD = sb(gstack, "MOD", [128, DEPTH, 144, 3], F32)
    DER = sb(gstack, "DER", [128, DEPTH, 6, KC, 3], F32)
    GN = sb(gstack, "GN", [128, 7, KC], F32)
    QKN = sb(gstack, "QKN", [128, DEPTH, 2], F32)
    ONES_D = sb(gstack, "ONES_D", [128, 128], BF16)
    ONES_H = sb(gstack, "ONES_H", [128, 128], BF16)
    ONES_1 = sb(gstack, "ONES_1", [128, 128], BF16)
    IDN = sb(gstack, "IDN", [128, 128], F32)
    BMG = sb(gstack, "BMG", [128, DEPTH, 48], F32)
    RDEC = sb(gstack, "RDEC", [128, DEPTH * 8], F32)
    EPS_T = sb(gstack, "EPS_T", [128, 1], F32)
    psn = [0]

    def rsqrt(out_ap, out_res, in_ap, in_res):
        k.op(k.act, lambda e: e.activation(out=out_ap, in_=in_ap, func=AF.Sqrt, bias=EPS_T.t[:, 0:1], scale=1.0),
             reads=[in_res, EPS_T.res[0]], writes=[out_res])
        k.op(k.dve, lambda e: e.reciprocal(out_ap, out_ap), reads=[out_res], writes=[out_res])

    def ps_next():
        p = PS[psn[0] % 8]
        psn[0] += 1
        return p

    k.dma("sp", [(GN.t[:], gains), (QKN.t[:], qkn), (IDN.t[:], ident), (BMG.t[:], bmg), (RDEC.t[:], rdec)],
          writes=[GN.res[0], QKN.res[0], IDN.res[0], BMG.res[0], RDEC.res[0]])
    k.op(k.dve, lambda e: e.memset(EPS_T.t[:], EPS), writes=[EPS_T.res[0]])
    k.op(k.dve, lambda e: e.memset(ONES_D.t[:], 1.0 / D), writes=[ONES_D.res[0]])
    k.op(k.dve, lambda e: e.memset(ONES_H.t[:], 1.0 / 128), writes=[ONES_H.res[0]])
    k.op(k.dve, lambda e: e.memset(ONES_1.t[:], 1.0), writes=[ONES_1.res[0]])

    def stage_in():
        with contextlib.ExitStack() as st:
            XT = sb(st, "XT", [128, 2, D], F32, 2)
            HT = sb(st, "HTt", [128, 2, KC, 128], F32, 2)
            for i in range(36):
                b = i % 2
                if i < 2:
                    src = ctxin[0, i * 128:(i + 1) * 128, :]
                elif i < 4:
                    src = ctxin[1, (i - 2) * 128:(i - 1) * 128, :]
                elif i < 20:
                    src = xin[0, (i - 4) * 128:(i - 3) * 128, :]
                else:
                    src = xin[1, (i - 20) * 128:(i - 19) * 128, :]
                k.dma("sp", [(XT.t[:, b, :], src)], writes=[XT.res[b]])
                for g in range(4):
                    p = ps_next()

                    def fn(pe, g=g, p=p, b=b):
                        ins = None
                        for jj in range(4):
                            kc = g * 4 + jj
                            ins = pe.transpose(p.t[:, jj * 128:(jj + 1) * 128], XT.t[:, b, kc * 128:(kc + 1) * 128], IDN.t[:])
                        return ins
                    k.op(k.pe, fn, reads=[XT.res[b], IDN.res[0]], writes=[p.res[0]], skip_self=True)
                    eng = k.act if g % 2 == 0 else k.dve
                    dst = HT.t[:, b, g * 4:(g + 1) * 4, :]
                    srcp = p.t[:].rearrange("p (c t) -> p c t", c=4)
                    if g % 2 == 0:
                        k.op(k.act, lambda e, dst=dst, srcp=srcp: e.copy(dst, srcp), reads=[p.res[0]], writes=[HT.res[b]])
                    else:
                        k.op(k.dve, lambda e, dst=dst, srcp=srcp: e.tensor_copy(dst, srcp), reads=[p.res[0]], writes=[HT.res[b]])
                k.dma("sp", [(hT_v[:, :, i * 128:(i + 1) * 128], HT.t[:, b, :, :])], reads=[HT.res[b]])
            k.barrier()

    WSCR_N = 800000
    wscrs = [dscr(f"wscr{i}", [128, WSCR_N], BF16) for i in range(DEPTH)]
    wcache = {}
    wscr_offs = [0] * DEPTH
    pend_wb = {}

    def load_slab_c(key, tile, bi, parts, kcn, ncols):
        if key in wcache:
            off, n, res = wcache[key]
            src = wscrs[key[1]][:, off:off + n].rearrange("p (c n) -> p c n", c=kcn)
            h = kcn // 2
            k.dma("sp", [(tile.t[:, bi, 0:h, 0:ncols], src[:, 0:h, :]), (tile.t[:, bi, h:kcn, 0:ncols], src[:, h:kcn, :])],
                  reads=[res], writes=[tile.res[bi]])
            return
        pairs = []
        for (a, b2, src2d) in parts:
            v = src2d.rearrange("(c p) n -> p c n", p=128)
            for a2 in range(a, b2, 8):
                b3 = min(b2, a2 + 8)
                pairs.append((tile.t[:, bi, a2:b3, 0:ncols], v[:, a2 - a:b3 - a, :]))
        k.dma("pool", pairs, writes=[tile.res[bi]])
        n = kcn * ncols
        off = wscr_offs[key[1]]
        wscr_offs[key[1]] += n
        assert wscr_offs[key[1]] <= WSCR_N
        wcache[key] = (off, n, Res())
        pend_wb[(id(tile), bi)] = key

    def writeback(tile, bi, kcn, ncols):
        key = pend_wb.pop((id(tile), bi), None)
        if key is None:
            return
        off, n, res = wcache[key]
        dst = wscrs[key[1]][:, off:off + n].rearrange("p (c n) -> p c n", c=kcn)
        k.dma("sp", [(dst, tile.t[:, bi, 0:kcn, 0:ncols])], reads=[tile.res[bi]], writes=[res])

    class Stream:
        def __init__(self, loaders):
            self.loaders = loaders
            self.done = 0

        def need(self, i):
            while self.done <= min(i + 1, len(self.loaders) - 1):
                self.loaders[self.done](self.done % 2)
                self.done += 1
            return i % 2

    CA = sb(gstack, "CA", [128, KC, 3], BF16)
    BA = sb(gstack, "BA", [128, DEPTH, 144], F32)

    def der_a(l, n_):
        sc_i, g_i = ((1, l * 3 + 0), (4, l * 3 + 1), (7, l * 3 + 2))[n_]
        gb = GN.t[:, g_i, :].unsqueeze(2).to_broadcast([128, KC, 3])
        k.op(k.dve, lambda e: e.scalar_tensor_tensor(
            out=DER.t[:, l, n_, :, :], in0=MOD.t[:, l, sc_i * 16:(sc_i + 1) * 16, :], scalar=1.0, in1=gb,
            op0=ALU.add, op1=ALU.mult), reads=[MOD.res[0], GN.res[0]], writes=[DER.res[0]])

    def der_g(l, n_):
        mi, f = ((2, 0.5), (5, 1.0), (8, 0.5))[n_]
        k.op(k.dve, lambda e: e.tensor_scalar(
            out=DER.t[:, l, 3 + n_, :, :], in0=MOD.t[:, l, mi * 16:(mi + 1) * 16, :], scalar1=f, scalar2=None,
            op0=ALU.mult), reads=[MOD.res[0]], writes=[DER.res[0]])

    def adaln_task(st, slabs):
        WF = sb(st, "WFb", [128, 2, KC, 256], F32, 2)
        WA = sb(st, "WAb", [128, 2, KC, 256], BF16, 2)

        def mk(l, c0):
            def f(b):
                v = w_ada[l, :, c0:c0 + 256].rearrange("(c p) n -> p c n", p=128)
                k.dma("sp", [(WF.t[:, b, 0:8, :], v[:, 0:8, :]), (WF.t[:, b, 8:16, :], v[:, 8:16, :])], writes=[WF.res[b]])
            return f
        strm = Stream([mk(l, c0) for (l, c0) in slabs])

        def gen():
            N = len(slabs)
            for n in range(N + 1):
                if n < N:
                    bi = strm.need(n)
                    k.op(k.dve, lambda e: e.tensor_copy(WA.t[:, bi, :, :], WF.t[:, bi, :, :]), reads=[WF.res[bi]], writes=[WA.res[bi]])
                if n >= 1:
                    l, c0 = slabs[n - 1]
                    bj = (n - 1) % 2
                    for sub in range(2):
                        p = ps_next()
                        ch = c0 // 128 + sub
                        k.mm(p.t[:, 0:3], [(WA.t[:, bj, kc, sub * 128:(sub + 1) * 128], CA.t[:, kc, :]) for kc in range(KC)],
                             reads=[WA.res[bj], CA.res[0]], writes=[p.res[0]])
                        k.op(k.act, lambda e, p=p, ch=ch: e.activation(out=MOD.t[:, l, ch, :], in_=p.t[:, 0:3], func=AF.Identity,
                                                                   bias=BA.t[:, l, ch:ch + 1], scale=1.0),
                             reads=[p.res[0], BA.res[0]], writes=[MOD.res[0]])
                yield
        return gen()

    def stage_adaln_pre():
        with contextlib.ExitStack() as st:
            CC = sb(st, "CC", [128, KC, 3], F32)
            k.dma("sp", [(CC.t[:], ccat), (BA.t[:], b_ada)], writes=[CC.res[0], BA.res[0]])
            k.op(k.act, lambda e: e.activation(out=CA.t[:], in_=CC.t[:], func=AF.Silu), reads=[CC.res[0]], writes=[CA.res[0]])
            g_ = adaln_task(st, [(0, c0) for c0 in range(0, 3 * D, 256)])
            for _ in g_:
                pass
            der_a(0, 0)
            der_g(0, 0)
            k.barrier()

    bg_slabs = {"gu0": [(0, c0) for c0 in range(3 * D, 9 * D, 256)], "attn0": [(1, c0) for c0 in range(0, 9 * D, 256)]}

    def bg_finish(which):
        if which == "gu0":
            der_a(0, 1); der_a(0, 2); der_g(0, 1); der_g(0, 2)
        else:
            for n_ in range(3):
                der_a(1, n_); der_g(1, n_)
        if which == "attn0" and "moddbg" in dbg:
            k.dma("sp", [(moddbg, MOD.t[:].rearrange("p a b c -> p (a b c)"))], reads=[MOD.res[0]])

    def stage_norm(l, n_, blocks):
        sh_i = (0, 3, 6)[n_]
        with contextlib.ExitStack() as st:
            X = sb(st, "X", [128, 3, KC, 512], F32, 3)
            SQ = sb(st, "SQ", [128, 2, KC, 512], BF16, 2)
            RS = sb(st, "RS", [128, 2, 512], F32, 2)
            XN = sb(st, "XN", [128, 2, KC, 512], BF16, 4)

            def ldx(it):
                ts_ = slice(blocks[it] * 512, (blocks[it] + 1) * 512)
                k.dma("sp", [(X.t[:, it % 3, 0:8, :], hT_v[:, 0:8, ts_]), (X.t[:, it % 3, 8:16, :], hT_v[:, 8:16, ts_])], writes=[X.res[it % 3]])
            ldx(0)
            if len(blocks) > 1:
                ldx(1)
            for it, blk in enumerate(blocks):
                b = it % 2
                xb = it % 3
                j = blk_j(blk)
                ts = slice(blk * 512, (blk + 1) * 512)
                if it + 2 < len(blocks):
                    ldx(it + 2)
                k.op(k.act, lambda e: e.activation(out=SQ.t[:, b, :, :], in_=X.t[:, xb, :, :], func=AF.Square),
                     reads=[X.res[xb]], writes=[SQ.res[b]])
                p = ps_next()
                k.mm(p.t[:], [(ONES_D.t[:], SQ.t[:, b, kc, :]) for kc in range(KC)], reads=[SQ.res[b], ONES_D.res[0]], writes=[p.res[0]])
                rsqrt(RS.t[:, b, :], RS.res[b], p.t[:], p.res[0])
                rb = RS.t[:, b, :].unsqueeze(1).to_broadcast([128, KC, 512])
                k.op(k.dve, lambda e: e.tensor_tensor(out=X.t[:, xb, :, :], in0=X.t[:, xb, :, :], in1=rb, op=ALU.mult),
                     reads=[RS.res[b], X.res[xb]], writes=[X.res[xb]])
                def fn(e):
                    ins = None
                    for kc in range(0, 10):
                        ins = e.activation(out=XN.t[:, b, kc, :], in_=X.t[:, xb, kc, :], func=AF.Identity,
                                           bias=MOD.t[:, l, sh_i * 16 + kc, j:j + 1], scale=DER.t[:, l, n_, kc, j:j + 1])
                    return ins

                def fnp(e):
                    ins = None
                    for kc in range(10, KC):
                        ins = e.tensor_scalar(out=XN.t[:, b, kc, :], in0=X.t[:, xb, kc, :], scalar1=DER.t[:, l, n_, kc, j:j + 1],
                                              scalar2=MOD.t[:, l, sh_i * 16 + kc, j:j + 1], op0=ALU.mult, op1=ALU.add)
                    return ins
                k.op(k.act, fn, reads=[X.res[xb], MOD.res[0], DER.res[0]], writes=[XN.res[2 * b]])
                k.op(k.pool, fnp, reads=[X.res[xb], MOD.res[0], DER.res[0]], writes=[XN.res[2 * b + 1]])
                k.dma("sp", [(xnT_v[:, :, ts], XN.t[:, b, :, :])], reads=[XN.res[2 * b], XN.res[2 * b + 1]])
            k.barrier()

    def stage_ffn_gu(l, wg, wu, blocks, tag, bg=None):
        passes = [blocks[i:i + 3] for i in range(0, len(blocks), 3)]
        nxb = 3 if bg else 6
        with contextlib.ExitStack() as st:
            XNr = sb(st, "XNr", [128, nxb, KC, 512], BF16, nxb)
            WGU = sb(st, "WGU", [128, 2, 2 * KC, 512], BF16, 2)
            SG = sb(st, "SG", [128, 3, 512], F32, 3)
            HB = sb(st, "HB", [128, 2, 4, 512], BF16, 2)
            task = adaln_task(st, bg_slabs[bg]) if bg else None
            nsg = 0
            nhb = 0
            order = [(pi, jc) for pi in range(len(passes)) for jc in range(11)]

            def mk(jc):
                def f(b):
                    load_slab_c((tag, l, "gu", jc), WGU, b, [(0, KC, wg[l, :, jc * 512:(jc + 1) * 512]), (KC, 2 * KC, wu[l, :, jc * 512:(jc + 1) * 512])], 2 * KC, 512)
                return f
            strm = Stream([mk(jc) for (_, jc) in order])
            xloaded = set()

            def ldpass(pi):
                if pi >= len(passes) or pi in xloaded:
                    return
                xloaded.add(pi)
                for bi, blk in enumerate(passes[pi]):
                    xb = (pi * 3 + bi) % nxb
                    k.dma("sp", [(XNr.t[:, xb, :, :], xnT_v[:, :, blk * 512:(blk + 1) * 512])], writes=[XNr.res[xb]])
            n = 0
            it = 0
            for pi, ps_ in enumerate(passes):
                ldpass(pi)
                if nxb == 6:
                    ldpass(pi + 1)
                for jc in range(11):
                    wb = strm.need(n)
                    n += 1
                    writeback(WGU, wb, 2 * KC, 512)
                    for bi, blk in enumerate(ps_):
                        xb = (pi * 3 + bi) % nxb
                        hb = nhb % 2
                        nhb += 1
                        for sub in range(4):
                            pg = ps_next()
                            pu = ps_next()
                            k.mm(pg.t[:], [(WGU.t[:, wb, kc, sub * 128:(sub + 1) * 128], XNr.t[:, xb, kc, :]) for kc in range(KC)],
                                 reads=[WGU.res[wb], XNr.res[xb]], writes=[pg.res[0]])
                            k.mm(pu.t[:], [(WGU.t[:, wb, KC + kc, sub * 128:(sub + 1) * 128], XNr.t[:, xb, kc, :]) for kc in range(KC)],
                                 reads=[WGU.res[wb], XNr.res[xb]], writes=[pu.res[0]])
                            sg = nsg % 3
                            nsg += 1
                            k.op(k.act, lambda e, sg=sg, pg=pg: e.activation(out=SG.t[:, sg, :], in_=pg.t[:], func=AF.Silu),
                                 reads=[pg.res[0]], writes=[SG.res[sg]])
                            k.op(k.dve, lambda e, sg=sg, pu=pu, hb=hb, sub=sub: e.tensor_tensor(
                                out=HB.t[:, hb, sub, :], in0=SG.t[:, sg, :], in1=pu.t[:], op=ALU.mult),
                                reads=[SG.res[sg], pu.res[0]], writes=[HB.res[hb]])
                        k.dma("sp", [(hidT_v[:, jc * 4:(jc + 1) * 4, blk * 512:(blk + 1) * 512], HB.t[:, hb, :, :])], reads=[HB.res[hb]])
                        it += 1
                        if task is not None and it % 2 == 0:
                            next(task, None)
            if task is not None:
                for _ in task:
                    pass
                bg_finish(bg)
            k.barrier()

    hT_res = [Res() for _ in range(NBLK)]

    def norm_task(st, l, n_, ready, state):
        sh_i = (0, 3, 6)[n_]
        X = sb(st, "Xb", [128, 2, KC, 256], F32, 2)
        XN = sb(st, "XNb", [128, 2, KC, 256], BF16, 2)
        RS = sb(st, "RSb", [128, 256], F32)
        nh = 0
        while True:
            if not ready:
                if state["done"]:
                    return
                yield
                continue
            blk = ready.pop(0)
            j = blk_j(blk)
            for half in range(2):
                xb = nh % 2
                nh += 1
                t0 = blk * 512 + half * 256
                k.dma("act", [(X.t[:, xb, 0:8, :], hT_v[:, 0:8, t0:t0 + 256]), (X.t[:, xb, 8:16, :], hT_v[:, 8:16, t0:t0 + 256])],
                      reads=[hT_res[blk]], writes=[X.res[xb]])
                yield
                k.op(k.act, lambda e: e.activation(out=XN.t[:, xb, :, :], in_=X.t[:, xb, :, :], func=AF.Square),
                     reads=[X.res[xb]], writes=[XN.res[xb]])
                yield
                p = ps_next()
                k.mm(p.t[:, 0:256], [(ONES_D.t[:], XN.t[:, xb, kc, :]) for kc in range(KC)], reads=[XN.res[xb], ONES_D.res[0]], writes=[p.res[0]])
                rsqrt(RS.t[:], RS.res[0], p.t[:, 0:256], p.res[0])
                rb = RS.t[:].unsqueeze(1).to_broadcast([128, KC, 256])
                k.op(k.dve, lambda e: e.tensor_tensor(out=X.t[:, xb, :, :], in0=X.t[:, xb, :, :], in1=rb, op=ALU.mult),
                     reads=[RS.res[0], X.res[xb]], writes=[X.res[xb]])

                def fn(e):
                    ins = None
                    for kc in range(KC):
                        ins = e.activation(out=XN.t[:, xb, kc, :], in_=X.t[:, xb, kc, :], func=AF.Identity,
                                           bias=MOD.t[:, l, sh_i * 16 + kc, j:j + 1], scale=DER.t[:, l, n_, kc, j:j + 1])
                    return ins
                k.op(k.act, fn, reads=[X.res[xb], MOD.res[0], DER.res[0]], writes=[XN.res[xb]])
                k.dma("act", [(xnT_v[:, :, t0:t0 + 256], XN.t[:, xb, :, :])], reads=[XN.res[xb]])
                yield

    def stage_gemm_resid(l, inT_v, kcn, w2d, g_i, blocks, name, norm_after=None):
        passes = [blocks[i:i + 2] for i in range(0, len(blocks), 2)]
        nib = (2 if norm_after else 3) if kcn > 16 else 4
        with contextlib.ExitStack() as st:
            IN = sb(st, "IN" + name, [128, nib, kcn, 512], BF16, nib)
            WS = sb(st, "WS" + name, [128, 2, kcn, 256], BF16, 2)
            HO = sb(st, "HO" + name, [128, 4, 512], F32, 4)
            ready = []
            state = {"done": False}
            task = None
            if norm_after is not None:
                task = norm_task(st, norm_after[0], norm_after[1], ready, state)
                nblocks = norm_after[2]
            for blk in blocks:
                hT_res[blk] = Res()
            nho = 0
            order = [(pi, s_) for pi in range(len(passes)) for s_ in range(8)]

            def mk(s_):
                def f(b):
                    load_slab_c((name, l, "w", s_), WS, b, [(0, kcn, w2d[:, s_ * 256:(s_ + 1) * 256])], kcn, 256)
                return f
            strm = Stream([mk(s_) for (_, s_) in order])
            seq = list(blocks)
            loaded = [0]

            def ensure(upto):
                while loaded[0] < min(upto, len(seq)):
                    c = loaded[0]
                    blk = seq[c]
                    ts_ = slice(blk * 512, (blk + 1) * 512)
                    pairs = []
                    for a_ in range(0, kcn, 8):
                        b2 = min(kcn, a_ + 8)
                        pairs.append((IN.t[:, c % nib, a_:b2, :], inT_v[:, a_:b2, ts_]))
                    k.dma("sp", pairs, writes=[IN.res[c % nib]])
                    loaded[0] += 1
            n = 0
            c0 = 0
            for ps_ in passes:
                ensure(c0 + len(ps_))
                ensure(c0 + nib)
                for s in range(8):
                    wb = strm.need(n)
                    n += 1
                    writeback(WS, wb, kcn, 256)
                    for bi, blk in enumerate(ps_):
                        ib = (c0 + bi) % nib
                        j = blk_j(blk)
                        ts = slice(blk * 512, (blk + 1) * 512)
                        for sub in range(2):
                            dc = s * 2 + sub
                            p = ps_next()
                            ho = nho % 4
                            nho += 1
                            k.dma("sp", [(HO.t[:, ho, :], hT_v[:, dc, ts])], writes=[HO.res[ho]])
                            k.mm(p.t[:], [(WS.t[:, wb, kc, sub * 128:(sub + 1) * 128], IN.t[:, ib, kc, :]) for kc in range(kcn)],
                                 reads=[WS.res[wb], IN.res[ib]], writes=[p.res[0]])
                            k.op(k.dve, lambda e, p=p, ho=ho, dc=dc, j=j: e.scalar_tensor_tensor(
                                out=HO.t[:, ho, :], in0=p.t[:], scalar=DER.t[:, l, g_i, dc, j:j + 1], in1=HO.t[:, ho, :],
                                op0=ALU.mult, op1=ALU.add), reads=[p.res[0], HO.res[ho], DER.res[0]], writes=[HO.res[ho]])
                            k.dma("sp", [(hT_v[:, dc, ts], HO.t[:, ho, :])], reads=[HO.res[ho]], acc_writes=[hT_res[blk]])
                            if task is not None:
                                next(task, None)
                c0 += len(ps_)
                if task is not None:
                    ready.extend(b_ for b_ in ps_ if b_ in nblocks)
            if task is not None:
                state["done"] = True
                for _ in task:
                    pass
            k.barrier()

    def stage_final():
        with contextlib.ExitStack() as st:
            X = sb(st, "Xf", [128, 2, KC, 512], F32, 2)
            SQ = sb(st, "SQf", [128, 2, KC, 512], BF16, 2)
            RS = sb(st, "RSf", [128, 2, 512], F32, 2)
            OT = sb(st, "OTf", [128, 2, D], F32, 2)
            no = 0
            def ldx(it):
                ts_ = slice((it + 1) * 512, (it + 2) * 512)
                k.dma("sp", [(X.t[:, it % 2, 0:8, :], hT_v[:, 0:8, ts_]), (X.t[:, it % 2, 8:16, :], hT_v[:, 8:16, ts_])], writes=[X.res[it % 2]])
            ldx(0)
            for it, blk in enumerate(range(1, 9)):
                b = it % 2
                ts = slice(blk * 512, (blk + 1) * 512)
                if it + 1 < 8:
                    ldx(it + 1)
                k.op(k.act, lambda e, b=b: e.activation(out=SQ.t[:, b, :, :], in_=X.t[:, b, :, :], func=AF.Square),
                     reads=[X.res[b]], writes=[SQ.res[b]])
                p = ps_next()
                k.mm(p.t[:], [(ONES_D.t[:], SQ.t[:, b, kc, :]) for kc in range(KC)], reads=[SQ.res[b], ONES_D.res[0]], writes=[p.res[0]])
                rsqrt(RS.t[:, b, :], RS.res[b], p.t[:], p.res[0])
                rb = RS.t[:, b, :].unsqueeze(1).to_broadcast([128, KC, 512])
                k.op(k.dve, lambda e, b=b, rb=rb: e.tensor_tensor(out=X.t[:, b, :, :], in0=X.t[:, b, :, :], in1=rb, op=ALU.mult),
                     reads=[RS.res[b], X.res[b]], writes=[X.res[b]])

                def fn(e, b=b):
                    ins = None
                    for kc in range(KC):
                        ins = e.activation(out=X.t[:, b, kc, :], in_=X.t[:, b, kc, :], func=AF.Identity, scale=GN.t[:, 6, kc:kc + 1])
                    return ins
                k.op(k.act, fn, reads=[X.res[b], GN.res[0]], writes=[X.res[b]])
                for tt in range(4):
                    ob = no % 2
                    no += 1
                    for g in range(4):
                        p = ps_next()

                        def fnt(pe, g=g, p=p, b=b, tt=tt):
                            ins = None
                            for jj in range(4):
                                kc = g * 4 + jj
                                ins = pe.transpose(p.t[:, jj * 128:(jj + 1) * 128], X.t[:, b, kc, tt * 128:(tt + 1) * 128], IDN.t[:])
                            return ins
                        k.op(k.pe, fnt, reads=[X.res[b], IDN.res[0]], writes=[p.res[0]], skip_self=True)
                        dst = OT.t[:, ob, g * 512:(g + 1) * 512]
                        if g % 2 == 0:
                            k.op(k.act, lambda e, dst=dst, p=p: e.copy(dst, p.t[:]), reads=[p.res[0]], writes=[OT.res[ob]])
                        else:
                            k.op(k.dve, lambda e, dst=dst, p=p: e.tensor_copy(dst, p.t[:]), reads=[p.res[0]], writes=[OT.res[ob]])
                    s = 0 if blk < 5 else 1
                    t0 = ((blk - 1) % 4) * 512 + tt * 128
                    k.dma("sp", [(out[s, t0:t0 + 128, :], OT.t[:, ob, :])], reads=[OT.res[ob]])
            k.barrier()

    def mm1(out_ap, l_ap, r_ap, start, stop, reads, writes):
        return k.op(k.pe, lambda pe: pe.matmul(out_ap, lhsT=l_ap, rhs=r_ap, start=start, stop=stop), reads, writes, skip_self=True)

    def ctx_tok(s):
        return slice(256 * s, 256 * s + 256)

    def lat_tok(s, a=0, n=2048):
        return slice(512 + 2048 * s + a, 512 + 2048 * s + a + n)

    cp_n = [0]

    def copy_alt(dst, src, reads, writes):
        cp_n[0] += 1
        if cp_n[0] % 2 == 0:
            k.op(k.act, lambda e: e.copy(dst, src), reads=reads, writes=writes)
        else:
            k.op(k.dve, lambda e: e.tensor_copy(dst, src), reads=reads, writes=writes)

    def stage_win(l, blocks):
        passes = [blocks[i:i + 3] for i in range(0, len(blocks), 3)]
        win = Wd["w_in"]
        with contextlib.ExitStack() as st:
            XNr = sb(st, "XNw", [128, 3, KC, 512], BF16, 3)
            WS = sb(st, "WSw", [128, 2, KC, 512], BF16, 2)
            COS = sb(st, "COS", [128, 2048], F32)
            SIN = sb(st, "SIN", [128, 2048], F32)
            PSW = sb(st, "PSW", [128, 128], BF16)
            OB = sb(st, "OBw", [128, 4, 4, 512], BF16, 4)
            TB = sb(st, "TBw", [128, 2, 512], BF16, 2)
            SQ = sb(st, "SQw", [128, 4, 512], BF16, 4)
            RSq = sb(st, "RSw", [128, 4, 512], F32, 4)
            QN = sb(st, "QNw", [128, 4, 512], BF16, 4)
            T1 = sb(st, "T1w", [128, 4, 512], F32, 4)
            T2 = sb(st, "T2w", [128, 4, 512], F32, 4)
            k.dma("sp", [(COS.t[:], cos_t), (SIN.t[:], sin_t), (PSW.t[:], pswap)], writes=[COS.res[0], SIN.res[0], PSW.res[0]])
            cnt = {"w": 0, "ob": 0, "tb": 0, "q": 0}

            pending = []

            def advance():
                for g_ in list(pending):
                    try:
                        next(g_)
                    except StopIteration:
                        pending.remove(g_)

            def drain():
                while pending:
                    advance()

            def qk_epilogue(p, which, blk, dst, dst_res, done):
                qi = cnt["q"] % 4
                cnt["q"] += 1
                k.op(k.act, lambda e: e.activation(out=SQ.t[:, qi, :], in_=p.t[:], func=AF.Square), reads=[p.res[0]], writes=[SQ.res[qi]])
                yield
                p2 = ps_next()
                k.mm(p2.t[:], [(ONES_H.t[:], SQ.t[:, qi, :])], reads=[SQ.res[qi], ONES_H.res[0]], writes=[p2.res[0]])
                rsqrt(RSq.t[:, qi, :], RSq.res[qi], p2.t[:], p2.res[0])
                if blk == 0:
                    k.op(k.dve, lambda e: e.scalar_tensor_tensor(out=dst, in0=p.t[:], scalar=QKN.t[:, l, which:which + 1], in1=RSq.t[:, qi, :],
                                                                 op0=ALU.mult, op1=ALU.mult), reads=[p.res[0], RSq.res[qi], QKN.res[0]], writes=[dst_res])
                    done()
                    return
                k.op(k.dve, lambda e: e.scalar_tensor_tensor(out=QN.t[:, qi, :], in0=p.t[:], scalar=QKN.t[:, l, which:which + 1], in1=RSq.t[:, qi, :],
                                                             op0=ALU.mult, op1=ALU.mult), reads=[p.res[0], RSq.res[qi], QKN.res[0]], writes=[QN.res[qi]])
                pos = ((blk - 1) % 4) * 512
                k.op(k.dve, lambda e: e.tensor_tensor(out=T1.t[:, qi, :], in0=QN.t[:, qi, :], in1=COS.t[:, pos:pos + 512], op=ALU.mult),
                     reads=[QN.res[qi], COS.res[0]], writes=[T1.res[qi]])
                yield
                p3 = ps_next()
                k.mm(p3.t[:], [(PSW.t[:], QN.t[:, qi, :])], reads=[PSW.res[0], QN.res[qi]], writes=[p3.res[0]])
                k.op(k.dve, lambda e: e.tensor_tensor(out=T2.t[:, qi, :], in0=p3.t[:], in1=SIN.t[:, pos:pos + 512], op=ALU.mult),
                     reads=[p3.res[0], SIN.res[0]], writes=[T2.res[qi]])
                k.op(k.dve, lambda e: e.tensor_tensor(out=dst, in0=T1.t[:, qi, :], in1=T2.t[:, qi, :], op=ALU.add),
                     reads=[T1.res[qi], T2.res[qi]], writes=[dst_res])
                done()

            def mkw(si):
                def f(b):
                    load_slab_c(("win", l, si), WS, b, [(0, KC, win[l, :, si * 512:(si + 1) * 512])], KC, 512)
                return f
            strm = Stream([mkw(si) for _ in passes for si in range(8)])
            for ps_ in passes:
                for bi, blk in enumerate(ps_):
                    k.dma("sp", [(XNr.t[:, bi, :, :], xnT_v[:, :, blk * 512:(blk + 1) * 512])], writes=[XNr.res[bi]])
                for si in range(8):
                    wb = strm.need(cnt["w"])
                    cnt["w"] += 1
                    writeback(WS, wb, KC, 512)
                    if si in (0, 1, 2, 3, 4, 6, 7):
                        subs = (0, 1) if si == 2 else (0, 1, 2, 3)
                        for bi, blk in enumerate(ps_):
                            ob = cnt["ob"] % 4
                            cnt["ob"] += 1
                            ts = slice(blk * 512, (blk + 1) * 512)
                            n = len(subs)
                            if si <= 1:
                                dv = qkT_v[:, si * 4:si * 4 + 4, ts]
                            elif si == 2:
                                dv = qkT_v[:, 8:10, ts]
                            else:
                                c0 = {3: 0, 4: 4, 6: 8, 7: 12}[si]
                                dv = rT_v[:, c0:c0 + 4, ts]
                            left = [n]

                            def done(left=left, dv=dv, ob=ob, n=n):
                                left[0] -= 1
                                if left[0] == 0:
                                    k.dma("sp", [(dv, OB.t[:, ob, 0:n, :])], reads=[OB.res[ob]])
                            for sub in subs:
                                p = ps_next()
                                k.mm(p.t[:], [(WS.t[:, wb, kc, sub * 128:(sub + 1) * 128], XNr.t[:, bi, kc, :]) for kc in range(KC)],
                                     reads=[WS.res[wb], XNr.res[bi]], writes=[p.res[0]])
                                dst = OB.t[:, ob, sub, :]
                                if si <= 2:
                                    g_ = qk_epilogue(p, 0 if si < 2 else 1, blk, dst, OB.res[ob], done)
                                    next(g_)
                                    advance()
                                    pending.append(g_)
                                elif si == 6:
                                    k.op(k.act, lambda e, dst=dst, p=p: e.activation(out=dst, in_=p.t[:], func=AF.Silu), reads=[p.res[0]], writes=[OB.res[ob]])
                                    done()
                                else:
                                    copy_alt(dst, p.t[:], [p.res[0]], [OB.res[ob]])
                                    done()
                        if si == 2:
                            drain()
                    if si in (2, 4, 5):
                        c0, c1, v0 = {2: (256, 512, 0), 4: (0, 512, 768), 5: (0, 512, 256)}[si]
                        ncol = c1 - c0
                        for bi, blk in enumerate(ps_):
                            for tt in range(4):
                                p = ps_next()
                                k.mm(p.t[:, 0:ncol], [(XNr.t[:, bi, kc, tt * 128:(tt + 1) * 128], WS.t[:, wb, kc, c0:c1]) for kc in range(KC)],
                                     reads=[WS.res[wb], XNr.res[bi]], writes=[p.res[0]])
                                tb = cnt["tb"] % 2
                                cnt["tb"] += 1
                                copy_alt(TB.t[:, tb, 0:ncol], p.t[:, 0:ncol], [p.res[0]], [TB.res[tb]])
                                r0 = blk * 512 + tt * 128
                                k.dma("sp", [(vtok[r0:r0 + 128, v0:v0 + ncol], TB.t[:, tb, 0:ncol])], reads=[TB.res[tb]])
            k.barrier()

    def stage_attn(l, bg=False):
        sc = 128.0 ** -0.5
        with contextlib.ExitStack() as st:
            KT = sb(st, "KTa", [128, 2, 2304], BF16, 2)
            V = sb(st, "Va", [128, 2, 18, 128], BF16, 2)
            QT = sb(st, "QTa", [128, 3, 512], BF16, 3)
            PT = sb(st, "PTa", [128, 4, 512], BF16, 4)
            RD = sb(st, "RDa", [128, 2, 512], F32, 2)
            OA = sb(st, "OAa", [128, 2, 512], BF16, 2)
            ACC = sb(st, "ACCa", [128, 2, 512], BF16, 2)
            task = adaln_task(st, bg_slabs["attn0"]) if bg else None
            cnt = {"pt": 0, "s": 0}
            units = []
            for s in range(2):
                for g in range(2):
                    gi = s * 2 + g
                    for hq in range(4 * g, 4 * g + 4):
                        if l == 0:
                            units.append((gi, s, g, hq, ctx_tok(s), 256, 2))
                        for qb in range(4):
                            units.append((gi, s, g, hq, lat_tok(s, qb * 512, 512), 512, 18))
            loaded_g = set()

            def load_unit(u):
                gi, s, g, hq, toks, nq, nkc = units[u]
                if gi not in loaded_g:
                    loaded_g.add(gi)
                    kv = gi % 2
                    k.dma("sp", [(KT.t[:, kv, 0:256], qkT_v[:, 8 + g, ctx_tok(s)]), (KT.t[:, kv, 256:2304], qkT_v[:, 8 + g, lat_tok(s)])],
                          writes=[KT.res[kv]])
                    k.dma("sp", [(V.t[:, kv, 0:2, :], vtok[ctx_tok(s), g * 128:(g + 1) * 128].rearrange("(c p) d -> p c d", p=128)),
                                 (V.t[:, kv, 2:18, :], vtok[lat_tok(s), g * 128:(g + 1) * 128].rearrange("(c p) d -> p c d", p=128))],
                          writes=[V.res[kv]])
                k.dma("sp", [(QT.t[:, u % 3, 0:nq], qkT_v[:, hq, toks])], writes=[QT.res[u % 3]])
            load_unit(0)
            for u, (gi, s, g, hq, toks, nq, nkc) in enumerate(units):
                if u + 1 < len(units):
                    load_unit(u + 1)
                kv = gi % 2
                qi = u % 3
                pss = []
                pO, pD = (PS[4], PS[5]) if u % 2 == 0 else (PS[6], PS[7])
                LOOK = 3

                def S(kc):
                    p = PS[cnt["s"] % 4]
                    cnt["s"] += 1
                    k.mm(p.t[:, 0:nq], [(KT.t[:, kv, kc * 128:(kc + 1) * 128], QT.t[:, qi, 0:nq])],
                         reads=[KT.res[kv], QT.res[qi]], writes=[p.res[0]])
                    pss.append(p)
                for kc in range(min(LOOK, nkc)):
                    S(kc)
                for kc in range(nkc):
                    if kc + LOOK < nkc:
                        S(kc + LOOK)
                    p = pss[kc]
                    pt = cnt["pt"] % 4
                    cnt["pt"] += 1
                    k.op(k.act, lambda e, p=p, pt=pt: e.activation(out=PT.t[:, pt, 0:nq], in_=p.t[:, 0:nq], func=AF.Exp, scale=sc),
                         reads=[p.res[0]], writes=[PT.res[pt]])
                    mm1(pO.t[:, 0:nq], V.t[:, kv, kc, :], PT.t[:, pt, 0:nq], kc == 0, kc == nkc - 1, [V.res[kv], PT.res[pt]], [pO.res[0]])
                    ai = u % 2
                    if kc == 0:
                        k.op(k.dve, lambda e, pt=pt: e.tensor_copy(ACC.t[:, ai, 0:nq], PT.t[:, pt, 0:nq]), reads=[PT.res[pt]], writes=[ACC.res[ai]])
                    else:
                        k.op(k.dve, lambda e, pt=pt: e.tensor_tensor(out=ACC.t[:, ai, 0:nq], in0=ACC.t[:, ai, 0:nq], in1=PT.t[:, pt, 0:nq], op=ALU.add),
                             reads=[PT.res[pt], ACC.res[ai]], writes=[ACC.res[ai]])
                mm1(pD.t[:, 0:nq], ONES_1.t[:], ACC.t[:, ai, 0:nq], True, True, [ONES_1.res[0], ACC.res[ai]], [pD.res[0]])
                oi = u % 2
                k.op(k.dve, lambda e: e.reciprocal(RD.t[:, oi, 0:nq], pD.t[:, 0:nq]), reads=[pD.res[0]], writes=[RD.res[oi]])
                k.op(k.dve, lambda e: e.tensor_tensor(out=OA.t[:, oi, 0:nq], in0=pO.t[:, 0:nq], in1=RD.t[:, oi, 0:nq], op=ALU.mult),
                     reads=[pO.res[0], RD.res[oi]], writes=[OA.res[oi]])
                k.dma("sp", [(yT_v[:, 4 + hq, toks], OA.t[:, oi, 0:nq])], reads=[OA.res[oi]])
                if task is not None:
                    next(task, None)
            if task is not None:
                for _ in task:
                    pass
                bg_finish("attn0")
            k.barrier()

    def stage_ret(l):
        cs = 128.0 ** -0.5
        with contextlib.ExitStack() as st:
            RT = sb(st, "RTr", [128, 8, 128], F32)
            LG = sb(st, "LGr", [128, 8], F32)
            E = sb(st, "Er", [128, 2, 128], F32)
            MASK = sb(st, "MASKr", [128, 128], F32)
            DQ = sb(st, "DQr", [128, 2, 128], F32)
            DK = sb(st, "DKr", [128, 2], F32)
            DCc = sb(st, "DCr", [128, 2], F32)
            QT = sb(st, "QTr", [128, 2304], BF16)
            KTt = sb(st, "KTr", [128, 2304], BF16)
            G = sb(st, "Gr", [128, 2304], BF16)
            KTOK = sb(st, "KTOKr", [128, 18, 128], BF16)
            VTOK = sb(st, "VTOKr", [128, 18, 128], BF16)
            QD = sb(st, "QDr", [128, 2, 2304], BF16, 2)
            KD = sb(st, "KDr", [128, 2, 18, 128], BF16, 2)
            U = sb(st, "Ur", [128, 2, 18, 128], F32, 2)
            S = sb(st, "Sr", [128, 2, 18, 128], F32, 2)
            SBF = sb(st, "SBFr", [128, 2, 18, 128], BF16)
            PM = sb(st, "PMr", [128, 2, 4, 128], BF16, 2)
            OSQ = sb(st, "OSQr", [128, 2, 512], BF16, 2)
            RS = sb(st, "RSr", [128, 2, 512], F32, 2)
            T = sb(st, "Tr", [128, 2, 512], F32, 2)
            YR = sb(st, "YRr", [128, 2304], BF16)
            k.dma("sp", [(RT.t[:], rtab)], writes=[RT.res[0]])
            k.op(k.act, lambda e: e.activation(out=LG.t[:], in_=RDEC.t[:, l * 8:(l + 1) * 8], func=AF.Exp), reads=[RDEC.res[0]], writes=[LG.res[0]])
            k.op(k.dve, lambda e: e.tensor_scalar(out=LG.t[:], in0=LG.t[:], scalar1=-1.0, scalar2=None, op0=ALU.mult), reads=[LG.res[0]], writes=[LG.res[0]])
            cnt = {"g": 0}
            for h in range(4):
                lgs = [LG.t[:, h:h + 1], LG.t[:, 4 + h:5 + h]]
                for d_ in range(2):
                    k.op(k.act, lambda e, d_=d_: e.activation(out=E.t[:, d_, :], in_=RT.t[:, 2 * d_, :], func=AF.Exp, scale=lgs[d_]),
                         reads=[RT.res[0], LG.res[0]], writes=[E.res[0]])
                    k.op(k.dve, lambda e, d_=d_: e.tensor_tensor(out=E.t[:, d_, :], in0=E.t[:, d_, :], in1=RT.t[:, 2 * d_ + 1, :], op=ALU.mult),
                         reads=[E.res[0], RT.res[0]], writes=[E.res[0]])
                    k.op(k.act, lambda e, d_=d_: e.activation(out=DQ.t[:, d_, :], in_=RT.t[:, 4 + d_, :], func=AF.Exp, scale=lgs[d_]),
                         reads=[RT.res[0], LG.res[0]], writes=[DQ.res[0]])
                    k.op(k.act, lambda e, d_=d_: e.activation(out=DK.t[:, d_:d_ + 1], in_=RT.t[:, 6 + d_, 0:1], func=AF.Exp, scale=lgs[d_]),
                         reads=[RT.res[0], LG.res[0]], writes=[DK.res[0]])
                    k.op(k.act, lambda e, d_=d_: e.activation(out=DCc.t[:, d_:d_ + 1], in_=RT.t[:, 5, 0:1], func=AF.Exp, scale=lgs[d_]),
                         reads=[RT.res[0], LG.res[0]], writes=[DCc.res[0]])
                k.op(k.dve, lambda e: e.tensor_scalar(out=DK.t[:], in0=DK.t[:], scalar1=cs, scalar2=None, op0=ALU.mult), reads=[DK.res[0]], writes=[DK.res[0]])
                k.op(k.dve, lambda e: e.tensor_tensor(out=MASK.t[:], in0=E.t[:, 0, :], in1=E.t[:, 1, :], op=ALU.add), reads=[E.res[0]], writes=[MASK.res[0]])
                k.op(k.dve, lambda e: e.tensor_scalar(out=MASK.t[:], in0=MASK.t[:], scalar1=cs, scalar2=None, op0=ALU.mult), reads=[MASK.res[0]], writes=[MASK.res[0]])
                for s in range(2):
                    def ld(dst, row):
                        return [(dst[:, 0:256], rT_v[:, row, ctx_tok(s)]), (dst[:, 256:2304], rT_v[:, row, lat_tok(s)])]
                    k.dma("sp", ld(QT.t, h), writes=[QT.res[0]])
                    k.dma("sp", ld(KTt.t, 4 + h), writes=[KTt.res[0]])
                    k.dma("sp", ld(G.t, 8 + h), writes=[G.res[0]])

                    def ldt(dst, c0):
                        return [(dst[:, 0:2, :], vtok[ctx_tok(s), c0:c0 + 128].rearrange("(c p) d -> p c d", p=128)),
                                (dst[:, 2:18, :], vtok[lat_tok(s), c0:c0 + 128].rearrange("(c p) d -> p c d", p=128))]
                    k.dma("sp", ldt(KTOK.t, 768 + h * 128), writes=[KTOK.res[0]])
                    k.dma("sp", ldt(VTOK.t, 256 + h * 128), writes=[VTOK.res[0]])
                    QT3 = QT.t[:].rearrange("p (c i) -> p c i", c=18)
                    for d_ in range(2):
                        dqb = DQ.t[:, d_, :].unsqueeze(1).to_broadcast([128, 18, 128])
                        k.op(k.pool, lambda e, d_=d_, dqb=dqb: e.tensor_tensor(out=QD.t[:, d_, :].rearrange("p (c i) -> p c i", c=18), in0=QT3, in1=dqb, op=ALU.mult),
                             reads=[QT.res[0], DQ.res[0]], writes=[QD.res[d_]])
                        k.op(k.act, lambda e, d_=d_: e.activation(out=KD.t[:, d_, :, :], in_=KTOK.t[:], func=AF.Identity, scale=DK.t[:, d_:d_ + 1]),
                             reads=[KTOK.res[0], DK.res[0]], writes=[KD.res[d_]])
                    for d_ in range(2):
                        for c0, n in ((0, 4), (4, 4), (8, 4), (12, 4), (16, 2)):
                            p = ps_next()

                            def fn(pe, d_=d_, c0=c0, n=n, p=p):
                                ins = None
                                for jj in range(n):
                                    ins = pe.matmul(p.t[:, jj * 128:(jj + 1) * 128], lhsT=KD.t[:, d_, c0 + jj, :], rhs=VTOK.t[:, c0 + jj, :], start=True, stop=True)
                                return ins
                            k.op(k.pe, fn, reads=[KD.res[d_], VTOK.res[0]], writes=[p.res[0]], skip_self=True)
                            copy_alt(U.t[:, d_, c0:c0 + n, :], p.t[:, 0:n * 128].rearrange("p (c e) -> p c e", c=n), [p.res[0]], [U.res[d_]])
                    orders = [list(range(18)), [1, 0] + list(range(17, 1, -1))]
                    for d_ in range(2):
                        k.op(k.dve, lambda e, d_=d_: e.memset(S.t[:, d_, orders[d_][0], :], 0.0), writes=[S.res[d_]])
                    for idx in range(17):
                        for d_ in range(2):
                            cur, nxt = orders[d_][idx], orders[d_][idx + 1]
                            k.op(k.dve, lambda e, d_=d_, cur=cur, nxt=nxt: e.scalar_tensor_tensor(
                                out=S.t[:, d_, nxt, :], in0=S.t[:, d_, cur, :], scalar=DCc.t[:, d_:d_ + 1], in1=U.t[:, d_, cur, :],
                                op0=ALU.mult, op1=ALU.add), reads=[S.res[d_], U.res[d_], DCc.res[0]], writes=[S.res[d_]])
                    k.op(k.act, lambda e: e.copy(SBF.t[:], S.t[:]), reads=[S.res[0], S.res[1]], writes=[SBF.res[0]])
                    for (c0, n) in ((0, 2), (2, 4), (6, 4), (10, 4), (14, 4)):
                        if l == 1 and c0 == 0:
                            continue
                        gi = cnt["g"] % 2
                        cnt["g"] += 1
                        pin = ps_next()

                        def fin(pe, c0=c0, n=n, pin=pin):
                            ins = None
                            for jj in range(n):
                                c = c0 + jj
                                ins = pe.matmul(pin.t[:, jj * 128:(jj + 1) * 128], lhsT=KTt.t[:, c * 128:(c + 1) * 128], rhs=QT.t[:, c * 128:(c + 1) * 128], start=True, stop=True)
                            return ins
                        k.op(k.pe, fin, reads=[KTt.res[0], QT.res[0]], writes=[pin.res[0]], skip_self=True)
                        mb = MASK.t[:].unsqueeze(1).to_broadcast([128, n, 128])
                        k.op(k.dve, lambda e, gi=gi, n=n, pin=pin, mb=mb: e.tensor_tensor(
                            out=PM.t[:, gi, 0:n, :], in0=pin.t[:, 0:n * 128].rearrange("p (c i) -> p c i", c=n), in1=mb, op=ALU.mult),
                            reads=[pin.res[0], MASK.res[0]], writes=[PM.res[gi]])
                        po = ps_next()

                        def fo(pe, c0=c0, n=n, po=po, gi=gi):
                            ins = None
                            for jj in range(n):
                                c = c0 + jj
                                o_ = po.t[:, jj * 128:(jj + 1) * 128]
                                pe.matmul(o_, lhsT=VTOK.t[:, c, :], rhs=PM.t[:, gi, jj, :], start=True, stop=False)
                                pe.matmul(o_, lhsT=SBF.t[:, 0, c, :], rhs=QD.t[:, 0, c * 128:(c + 1) * 128], start=False, stop=False)
                                ins = pe.matmul(o_, lhsT=SBF.t[:, 1, c, :], rhs=QD.t[:, 1, c * 128:(c + 1) * 128], start=False, stop=True)
                            return ins
                        k.op(k.pe, fo, reads=[VTOK.res[0], PM.res[gi], SBF.res[0], QD.res[0], QD.res[1]], writes=[po.res[0]], skip_self=True)
                        w = n * 128
                        k.op(k.act, lambda e, gi=gi, po=po, w=w: e.activation(out=OSQ.t[:, gi, 0:w], in_=po.t[:, 0:w], func=AF.Square), reads=[po.res[0]], writes=[OSQ.res[gi]])
                        pss = ps_next()
                        k.mm(pss.t[:, 0:w], [(ONES_H.t[:], OSQ.t[:, gi, 0:w])], reads=[OSQ.res[gi], ONES_H.res[0]], writes=[pss.res[0]])
                        rsqrt(RS.t[:, gi, 0:w], RS.res[gi], pss.t[:, 0:w], pss.res[0])
                        k.op(k.dve, lambda e, gi=gi, po=po, w=w: e.tensor_tensor(out=T.t[:, gi, 0:w], in0=po.t[:, 0:w], in1=RS.t[:, gi, 0:w], op=ALU.mult),
                             reads=[po.res[0], RS.res[gi]], writes=[T.res[gi]])
                        k.op(k.pool, lambda e, gi=gi, c0=c0, w=w: e.tensor_tensor(out=YR.t[:, c0 * 128:c0 * 128 + w], in0=T.t[:, gi, 0:w], in1=G.t[:, c0 * 128:c0 * 128 + w], op=ALU.mult),
                             reads=[T.res[gi], G.res[0]], writes=[YR.res[0]])
                    pairs = [(yT_v[:, 12 + h, lat_tok(s)], YR.t[:, 256:2304])]
                    if l == 0:
                        pairs.append((yT_v[:, 12 + h, ctx_tok(s)], YR.t[:, 0:256]))
                    k.dma("sp", pairs, reads=[YR.res[0]])
            k.barrier()

    def stage_four(l):
        with contextlib.ExitStack() as st:
            DCt = sb(st, "DCf", [128, 256], BF16)
            ZT = sb(st, "ZTf", [128, 4, 2048], BF16)
            A = sb(st, "Af", [128, 16, 4, 256], BF16, 32)
            TL = sb(st, "TLf", [128, 2, 2, 16, 512], BF16, 2)
            YF = sb(st, "YFf", [128, 2, 512], BF16, 2)
            k.dma("sp", [(DCt.t[:], dftc)], writes=[DCt.res[0]])
            cnt = {"t": 0, "y": 0}
            units = []
            for s in range(2):
                units.append((lat_tok(s), 2048, dftL))
            if l == 0:
                for s in range(2):
                    units.append((ctx_tok(s), 256, dftS))
            for (toks, L, tab) in units:
                ntt = L // 128
                bw = min(512, L)
                k.dma("sp", [(ZT.t[:, :, 0:L], rT_v[:, 12:16, toks])], writes=[ZT.res[0]])
                for tt in range(ntt):
                    for gp in range(2):
                        p = ps_next()

                        def fa(pe, tt=tt, gp=gp, p=p):
                            ins = None
                            for gi in range(2):
                                ins = pe.matmul(p.t[:, gi * 256:(gi + 1) * 256], lhsT=ZT.t[:, 2 * gp + gi, tt * 128:(tt + 1) * 128], rhs=DCt.t[:], start=True, stop=True)
                            return ins
                        k.op(k.pe, fa, reads=[ZT.res[0], DCt.res[0]], writes=[p.res[0]], skip_self=True)
                        copy_alt(A.t[:, tt, 2 * gp:2 * gp + 2, :], p.t[:].rearrange("p (g c) -> p g c", g=2), [p.res[0]], [A.res[tt * 2 + gp]])
                for tb in range(L // bw):
                    ti = cnt["t"] % 2
                    cnt["t"] += 1
                    tv = tab.rearrange("a (c p) n -> a p c n", p=128)
                    k.dma("sp", [(TL.t[:, ti, a, 0:ntt, 0:bw], tv[a, :, :, tb * bw:(tb + 1) * bw]) for a in range(2)], writes=[TL.res[ti]])
                    for g in range(4):
                        p = ps_next()
                        pairs = []
                        for tt in range(ntt):
                            pairs.append((A.t[:, tt, g, 0:128], TL.t[:, ti, 0, tt, 0:bw]))
                            pairs.append((A.t[:, tt, g, 128:256], TL.t[:, ti, 1, tt, 0:bw]))
                        k.mm(p.t[:, 0:bw], pairs, reads=A.res + [TL.res[ti]], writes=[p.res[0]])
                        yi = cnt["y"] % 2
                        cnt["y"] += 1
                        copy_alt(YF.t[:, yi, 0:bw], p.t[:, 0:bw], [p.res[0]], [YF.res[yi]])
                        t0 = toks.start + tb * bw
                        k.dma("sp", [(yT_v[:, g, t0:t0 + bw], YF.t[:, yi, 0:bw])], reads=[YF.res[yi]])
            k.barrier()

    def stage_merge(l, blocks):
        passes = [blocks[i:i + 2] for i in range(0, len(blocks), 2)]
        wmg = Wd["w_merge_gate"]
        with contextlib.ExitStack() as st:
            XNr = sb(st, "XNm", [128, 2, KC, 512], BF16, 2)
            Y = sb(st, "Ym", [128, 2, KC, 512], BF16, 2)
            WG = sb(st, "WGm", [128, 2, 4 * KC, 256], BF16, 2)
            SGm = sb(st, "SGm", [128, 3, 512], F32, 3)
            ACC = sb(st, "ACCm", [128, 2, 512], F32, 2)
            Tm = sb(st, "Tm", [128, 2, 512], F32, 2)
            MB = sb(st, "MBm", [128, 2, 2, 512], BF16, 2)
            cnt = {"w": 0, "sg": 0, "acc": 0, "t": 0, "mb": 0}

            def mkm(s_):
                def f(b):
                    cs_ = slice(s_ * 256, (s_ + 1) * 256)
                    parts = [(gi * KC, (gi + 1) * KC, wmg[l, :, gi * D + s_ * 256: gi * D + (s_ + 1) * 256]) for gi in range(3)]
                    parts += [(48, 52, Wd["w_branch_fourier"][l, :, cs_]), (52, 60, Wd["w_branch_attn"][l, :, cs_]), (60, 64, Wd["w_branch_ret"][l, :, cs_])]
                    load_slab_c(("mg", l, s_), WG, b, parts, 4 * KC, 256)
                return f
            strm = Stream([mkm(s_) for _ in passes for s_ in range(8)])
            for ps_ in passes:
                for bi, blk in enumerate(ps_):
                    ts = slice(blk * 512, (blk + 1) * 512)
                    k.dma("sp", [(XNr.t[:, bi, :, :], xnT_v[:, :, ts])], writes=[XNr.res[bi]])
                    k.dma("sp", [(Y.t[:, bi, :, :], yT_v[:, :, ts])], writes=[Y.res[bi]])
                for s in range(8):
                    wb = strm.need(cnt["w"])
                    cnt["w"] += 1
                    writeback(WG, wb, 4 * KC, 256)
                    for bi, blk in enumerate(ps_):
                        j = blk_j(blk)
                        mb = cnt["mb"] % 2
                        cnt["mb"] += 1
                        for sub in range(2):
                            dc = 2 * s + sub
                            ai = cnt["acc"] % 2
                            cnt["acc"] += 1
                            for gi, (k0, k1) in enumerate(((0, 4), (4, 12), (12, 16))):
                                pg = ps_next()
                                k.mm(pg.t[:], [(WG.t[:, wb, gi * KC + kc, sub * 128:(sub + 1) * 128], XNr.t[:, bi, kc, :]) for kc in range(KC)],
                                     reads=[WG.res[wb], XNr.res[bi]], writes=[pg.res[0]])
                                pb = ps_next()
                                k.mm(pb.t[:], [(WG.t[:, wb, 48 + kc, sub * 128:(sub + 1) * 128], Y.t[:, bi, kc, :]) for kc in range(k0, k1)],
                                     reads=[WG.res[wb], Y.res[bi]], writes=[pb.res[0]])
                                sg = cnt["sg"] % 3
                                cnt["sg"] += 1
                                k.op(k.act, lambda e, sg=sg, pg=pg, gi=gi, dc=dc: e.activation(out=SGm.t[:, sg, :], in_=pg.t[:], func=AF.Sigmoid,
                                                                                           bias=BMG.t[:, l, gi * 16 + dc:gi * 16 + dc + 1], scale=1.0),
                                     reads=[pg.res[0], BMG.res[0]], writes=[SGm.res[sg]])
                                if gi == 0:
                                    k.op(k.dve, lambda e, sg=sg, pb=pb, ai=ai: e.tensor_tensor(out=ACC.t[:, ai, :], in0=SGm.t[:, sg, :], in1=pb.t[:], op=ALU.mult),
                                         reads=[SGm.res[sg], pb.res[0]], writes=[ACC.res[ai]])
                                else:
                                    ti = cnt["t"] % 2
                                    cnt["t"] += 1
                                    k.op(k.dve, lambda e, sg=sg, pb=pb, ti=ti: e.tensor_tensor(out=Tm.t[:, ti, :], in0=SGm.t[:, sg, :], in1=pb.t[:], op=ALU.mult),
                                         reads=[SGm.res[sg], pb.res[0]], writes=[Tm.res[ti]])
                                    if gi == 1:
                                        k.op(k.dve, lambda e, ai=ai, ti=ti: e.tensor_tensor(out=ACC.t[:, ai, :], in0=ACC.t[:, ai, :], in1=Tm.t[:, ti, :], op=ALU.add),
                                             reads=[ACC.res[ai], Tm.res[ti]], writes=[ACC.res[ai]])
                                    else:
                                        k.op(k.dve, lambda e, ai=ai, ti=ti, mb=mb, sub=sub: e.tensor_tensor(out=MB.t[:, mb, sub, :], in0=ACC.t[:, ai, :], in1=Tm.t[:, ti, :], op=ALU.add),
                                             reads=[ACC.res[ai], Tm.res[ti]], writes=[MB.res[mb]])
                        k.dma("sp", [(mT_v[:, 2 * s:2 * s + 2, blk * 512:(blk + 1) * 512], MB.t[:, mb, :, :])], reads=[MB.res[mb]])
            k.barrier()

    allb = list(range(9))
    latb = list(range(1, 9))
    if want("in"):
        stage_in()
    if want("adaln"):
        stage_adaln_pre()
    if want("ffn1_0"):
        stage_norm(0, 0, allb)
    for l in range(DEPTH):
        mixb = allb if l == 0 else latb
        last = (l == DEPTH - 1)
        if want(f"ffn1_{l}"):
            stage_ffn_gu(l, Wd["ffn1_w_gate"], Wd["ffn1_w_up"], allb, "f1", bg=("gu0" if l == 0 else None))
            stage_gemm_resid(l, hidT_v, FKC, Wd["ffn1_w_down"][l], 3, allb, "d1", norm_after=(l, 1, allb))
        if want(f"win_{l}"):
            stage_win(l, allb)
        if want(f"attn_{l}"):
            stage_attn(l, bg=(l == 0))
        if want(f"ret_{l}"):
            stage_ret(l)
        if want(f"four_{l}"):
            stage_four(l)
        if want(f"merge_{l}"):
            stage_merge(l, mixb)
            stage_gemm_resid(l, mT_v, KC, Wd["w_out"][l], 4, mixb, "o", norm_after=(l, 2, mixb))
        if want(f"ffn2_{l}"):
            stage_ffn_gu(l, Wd["ffn2_w_gate"], Wd["ffn2_w_up"], mixb, "f2")
            stage_gemm_resid(l, hidT_v, FKC, Wd["ffn2_w_down"][l], 5, mixb, "d2", norm_after=(None if last else (l + 1, 0, allb)))
            if not last:
                pass
    if want("final"):
        stage_final()
    k.barrier()


def _consts():
    c = {}
    c["ident"] = np.eye(128, dtype=np.float32)
    t = np.arange(2048)
    row = (t // 64).astype(np.float32)
    col = (t % 64).astype(np.float32)
    inv = (10000.0 ** (-np.arange(0, 64, 2, dtype=np.float32) / 64)).astype(np.float32)
    ang = np.zeros((128, 2048), np.float32)
    for d in range(128):
        f = inv[d % 32]
        ang[d] = (row if d < 64 else col) * f
    c["cos_t"] = np.cos(ang).astype(np.float32)
    c["sin_t"] = np.sin(ang).astype(np.float32)
    P = np.zeros((128, 128), np.float32)
    for base in (0, 64):
        for i in range(32):
            P[base + i, base + 32 + i] = -1.0
            P[base + 32 + i, base + i] = 1.0
    c["pswap"] = P.T.copy().astype(ml_dtypes.bfloat16)
    p = np.arange(128, dtype=np.float32)[:, None]
    i = np.arange(128, dtype=np.float32)[None, :]
    rt = np.zeros((128, 8, 128), np.float32)
    rt[:, 0] = np.maximum(i - p, 0)
    rt[:, 1] = (i >= p)
    rt[:, 2] = np.maximum(p - i, 0)
    rt[:, 3] = (p >= i)
    rt[:, 4] = i + 1.0
    rt[:, 5] = 128.0 - i
    rt[:, 6] = 127.0 - p
    rt[:, 7] = p + 0.0 * i
    c["rtab"] = rt
    a = np.arange(128)
    ph = 2 * np.pi * np.outer(a, a) / 128
    c["dftc"] = (np.concatenate([np.cos(ph), np.sin(ph)], 1) / np.sqrt(128)).astype(ml_dtypes.bfloat16)
    for nm, L in (("dftL", 2048), ("dftS", 256)):
        a = np.arange(L)
        ph = 2 * np.pi * (np.outer(a, a) % L) / L
        c[nm] = np.stack([np.cos(ph), -np.sin(ph)]).astype(np.float32) / np.sqrt(L)
        c[nm] = c[nm].astype(ml_dtypes.bfloat16)
    return c


def make_in_maps(inputs):
    f = lambda a: np.ascontiguousarray(np.asarray(a, dtype=np.float32))
    cst = _consts()
    shared = {}
    for nm in ("w_ada", "ffn1_w_gate", "ffn1_w_up", "ffn1_w_down", "ffn2_w_gate", "ffn2_w_up", "ffn2_w_down", "w_in",
               "w_branch_fourier", "w_branch_attn", "w_branch_ret", "w_merge_gate", "w_out"):
        shared[nm] = f(inputs[nm])
    shared["b_ada_l"] = f(np.transpose(np.asarray(inputs["b_ada"]).reshape(DEPTH, 144, 128), (2, 0, 1)))
    g = np.stack([inputs["ffn1_norm"][0], inputs["mix_norm"][0], inputs["ffn2_norm"][0],
                  inputs["ffn1_norm"][1], inputs["mix_norm"][1], inputs["ffn2_norm"][1], inputs["final_norm"]])
    shared["gains"] = f(np.transpose(np.asarray(g).reshape(7, KC, 128), (2, 0, 1)))
    shared["qkn"] = f(np.stack([np.asarray(inputs["q_norm"]).T, np.asarray(inputs["k_norm"]).T], axis=2))
    shared["rdec"] = f(np.broadcast_to(np.asarray(inputs["ret_decay"]).reshape(1, DEPTH * 8), (128, DEPTH * 8)))
    shared["bmg"] = f(np.transpose(np.asarray(inputs["b_merge_gate"]).reshape(DEPTH, 48, 128), (2, 0, 1)))
    shared.update(cst)
    maps = []
    x = np.asarray(inputs["x"])
    c = np.asarray(inputs["c"])
    ctx = np.asarray(inputs["ctx"])
    cc = np.asarray(inputs["c_ctx"])
    for i in range(NCORES):
        m = dict(shared)
        m["xin"] = f(x[2 * i:2 * i + 2])
        m["ctxin"] = f(ctx[2 * i:2 * i + 2])
        cat = np.stack([c[2 * i], c[2 * i + 1], cc], axis=1)
        m["ccat"] = f(np.transpose(cat.reshape(KC, 128, 3), (1, 0, 2)))
        maps.append(m)
    return maps


def kernel(**inputs):
    nc = build_program()
    maps = make_in_maps(inputs)
    res = run_bass_kernel_spmd(nc, maps, core_ids=list(range(NCORES)))
    return np.concatenate([r["out"] for r in res.results], axis=0).astype(np.float32)
```

```python
import contextlib
import numpy as np
import ml_dtypes
import concourse.bass as bass
import concourse.mybir as mybir
from concourse.bass_utils import run_bass_kernel_spmd

F32 = mybir.dt.float32
BF16 = mybir.dt.bfloat16
AF = mybir.ActivationFunctionType
ALU = mybir.AluOpType

NCORES = 8
D = 2048
KC = 16
FF = 5632
FKC = 44
NT = 4608
NBLK = 9
EPS = 1e-6
DEPTH = 2
NS_ENG = 4
NS_DMA = 16


def blk_j(b):
    return 2 if b == 0 else (0 if b < 5 else 1)


class Res:
    __slots__ = ("w", "r")

    def __init__(self):
        self.w = []
        self.r = {}


class Sem:
    def __init__(self, h, idx, owner):
        self.h = h
        self.idx = idx
        self.owner = owner
        self.count = 0


class Eng:
    def __init__(self, name, be):
        self.name = name
        self.be = be
        self.sems = []
        self.n = 0
        self.waited = {}


class K:
    def __init__(self, nc, stack):
        self.nc = nc
        self.nsem = 0
        self.pe = Eng("pe", nc.tensor)
        self.act = Eng("act", nc.scalar)
        self.dve = Eng("dve", nc.vector)
        self.pool = Eng("pool", nc.gpsimd)
        self.sp = Eng("sp", nc.sync)
        self.engs = [self.pe, self.act, self.dve, self.pool, self.sp]
        for e in self.engs[:4]:
            for i in range(NS_ENG):
                e.sems.append(self._mk(stack, f"s_{e.name}{i}", e))
        self.dsem = {}
        self.dn = {}
        for q, e in (("sp", self.sp), ("pool", self.pool), ("act", self.act)):
            self.dsem[q] = [self._mk(stack, f"d_{q}{i}", None) for i in range(NS_DMA)]
            self.dn[q] = 0
        self.dq_eng = {"sp": self.sp, "pool": self.pool, "act": self.act}

    def _mk(self, stack, name, owner):
        h = stack.enter_context(self.nc.semaphore(name))
        s = Sem(h, self.nsem, owner)
        self.nsem += 1
        return s

    def _wait(self, eng, deps, skip_self=False):
        best = {}
        for (sem, val) in deps:
            if skip_self and sem.owner is eng:
                continue
            if eng.waited.get(sem.idx, 0) >= val:
                continue
            if sem.idx not in best or best[sem.idx][1] < val:
                best[sem.idx] = (sem, val)
        for sem, val in best.values():
            eng.be.wait_ge(sem.h, val)
            eng.waited[sem.idx] = val

    @staticmethod
    def _deps(reads, writes):
        deps = []
        for r in reads:
            deps.extend(r.w)
        for w in writes:
            deps.extend(w.w)
            deps.extend(w.r.values())
        return deps

    @staticmethod
    def _commit(evs, reads, writes):
        for r in reads:
            for ev in evs:
                old = r.r.get(ev[0].idx)
                if old is None or old[1] < ev[1]:
                    r.r[ev[0].idx] = ev
        for w in writes:
            w.w = list(evs)
            w.r = {}

    def op(self, eng, fn, reads=(), writes=(), skip_self=False):
        self._wait(eng, self._deps(reads, writes), skip_self)
        ins = fn(eng.be)
        k = eng.n
        eng.n += 1
        sem = eng.sems[k % NS_ENG]
        val = k // NS_ENG + 1
        ins.then_inc(sem.h, 1)
        ev = (sem, val)
        self._commit([ev], reads, writes)
        return ev

    def dma(self, q, pairs, reads=(), writes=(), acc_writes=()):
        eng = self.dq_eng[q]
        self._wait(eng, self._deps(reads, writes))
        evs = []
        for (o, i) in pairs:
            k = self.dn[q]
            self.dn[q] += 1
            sem = self.dsem[q][k % NS_DMA]
            if sem.count > 0:
                self._wait(eng, [(sem, 16 * sem.count)])
            eng.be.dma_start(out=o, in_=i).then_inc(sem.h, 16)
            sem.count += 1
            evs.append((sem, 16 * sem.count))
        self._commit(evs, reads, writes)
        for w in acc_writes:
            best = {ev[0].idx: ev for ev in w.w}
            for ev in evs:
                if ev[0].idx not in best or best[ev[0].idx][1] < ev[1]:
                    best[ev[0].idx] = ev
            w.w = list(best.values())
        return evs

    def all_events(self):
        evs = []
        for e in self.engs[:4]:
            if e.n > 0:
                k = e.n - 1
                evs.append((e.sems[k % NS_ENG], k // NS_ENG + 1))
        for q in self.dsem:
            for sem in self.dsem[q]:
                if sem.count > 0:
                    evs.append((sem, 16 * sem.count))
        return evs

    def barrier(self):
        evs = self.all_events()
        for e in self.engs:
            self._wait(e, evs)

    def mm(self, out_ap, pairs, reads, writes):
        n = len(pairs)

        def fn(pe):
            ins = None
            for i, (l, r) in enumerate(pairs):
                ins = pe.matmul(out_ap, lhsT=l, rhs=r, start=(i == 0), stop=(i == n - 1))
            return ins

        return self.op(self.pe, fn, reads, writes, skip_self=True)


class Tile:
    def __init__(self, t, n=1):
        self.t = t
        self.res = [Res() for _ in range(n)]


def build_program(dbg=(), stages=None):
    nc = bass.Bass("TRN2", target_bir_lowering=False)
    stack = contextlib.ExitStack()
    with stack:
        _emit(nc, stack, dbg, stages)
    return nc


def _emit(nc, gstack, dbg, stages):
    def din(name, shape, dt=F32):
        return nc.dram_tensor(name, list(shape), dt, kind="ExternalInput").ap()

    def dscr(name, shape, dt):
        kind = "ExternalOutput" if name in dbg else "Internal"
        return nc.dram_tensor(name, list(shape), dt, kind=kind).ap()

    xin = din("xin", [2, 2048, D])
    ctxin = din("ctxin", [2, 256, D])
    ccat = din("ccat", [128, KC, 3])
    w_ada = din("w_ada", [DEPTH, D, 9 * D])
    b_ada = din("b_ada_l", [128, DEPTH, 144])
    gains = din("gains", [128, 7, KC])
    qkn = din("qkn", [128, DEPTH, 2])
    rdec = din("rdec", [128, DEPTH * 8])
    bmg = din("bmg", [128, DEPTH, 48])
    Wd = {}
    for nm, shp in (("ffn1_w_gate", [D, FF]), ("ffn1_w_up", [D, FF]), ("ffn1_w_down", [FF, D]),
                    ("ffn2_w_gate", [D, FF]), ("ffn2_w_up", [D, FF]), ("ffn2_w_down", [FF, D]),
                    ("w_in", [D, 4096]), ("w_branch_fourier", [512, D]), ("w_branch_attn", [1024, D]),
                    ("w_branch_ret", [512, D]), ("w_merge_gate", [D, 3 * D]), ("w_out", [D, D])):
        Wd[nm] = din(nm, [DEPTH] + shp)
    ident = din("ident", [128, 128])
    cos_t = din("cos_t", [128, 2048])
    sin_t = din("sin_t", [128, 2048])
    pswap = din("pswap", [128, 128], BF16)
    rtab = din("rtab", [128, 8, 128])
    dftc = din("dftc", [128, 256], BF16)
    dftL = din("dftL", [2, 2048, 2048], BF16)
    dftS = din("dftS", [2, 256, 256], BF16)
    out = nc.dram_tensor("out", [2, 2048, D], F32, kind="ExternalOutput").ap()

    hT = dscr("hT", [D, NT], F32)
    xnT = dscr("xnT", [D, NT], BF16)
    hidT = dscr("hidT", [FF, NT], BF16)
    qkT = dscr("qkT", [1280, NT], BF16)
    rT = dscr("rT", [2048, NT], BF16)
    vtok = dscr("vtok", [NT, 1280], BF16)
    yT = dscr("yT", [2048, NT], BF16)
    mT = dscr("mT", [D, NT], BF16)
    moddbg = dscr("moddbg", [128, DEPTH * 144 * 3], F32)

    hT_v = hT.rearrange("(c p) t -> p c t", p=128)
    xnT_v = xnT.rearrange("(c p) t -> p c t", p=128)
    hidT_v = hidT.rearrange("(c p) t -> p c t", p=128)
    qkT_v = qkT.rearrange("(c p) t -> p c t", p=128)
    rT_v = rT.rearrange("(c p) t -> p c t", p=128)
    yT_v = yT.rearrange("(c p) t -> p c t", p=128)
    mT_v = mT.rearrange("(c p) t -> p c t", p=128)

    k = K(nc, gstack)
    want = (lambda s: True) if stages is None else (lambda s: s in stages)

    sbn = [0]

    def sb(stack, name, shape, dt, n=1):
        sbn[0] += 1
        t = stack.enter_context(nc.sbuf_tensor(f"{name}_{sbn[0]}", list(shape), dt))
        return Tile(t, n)

    PS = [Tile(gstack.enter_context(nc.psum_tensor(f"ps{i}", [128, 512], F32))) for i in range(8)]
    MOD = sb(gstack, "MOD", [128, DEPTH, 144, 3], F32)
    DER = sb(gstack, "DER", [128, DEPTH, 6, KC, 3], F32)
    GN = sb(gstack, "GN", [128, 7, KC], F32)
    QKN = sb(gstack, "QKN", [128, DEPTH, 2], F32)
    ONES_D = sb(gstack, "ONES_D", [128, 128], BF16)
    ONES_H = sb(gstack, "ONES_H", [128, 128], BF16)
    ONES_1 = sb(gstack, "ONES_1", [128, 128], BF16)
    IDN = sb(gstack, "IDN", [128, 128], F32)
    BMG = sb(gstack, "BMG", [128, DEPTH, 48], F32)
    RDEC = sb(gstack, "RDEC", [128, DEPTH * 8], F32)
    EPS_T = sb(gstack, "EPS_T", [128, 1], F32)
    psn = [0]

    def rsqrt(out_ap, out_res, in_ap, in_res):
        k.op(k.act, lambda e: e.activation(out=out_ap, in_=in_ap, func=AF.Sqrt, bias=EPS_T.t[:, 0:1], scale=1.0),
             reads=[in_res, EPS_T.res[0]], writes=[out_res])
        k.op(k.dve, lambda e: e.reciprocal(out_ap, out_ap), reads=[out_res], writes=[out_res])

    def ps_next():
        p = PS[psn[0] % 8]
        psn[0] += 1
        return p

    k.dma("sp", [(GN.t[:], gains), (QKN.t[:], qkn), (IDN.t[:], ident), (BMG.t[:], bmg), (RDEC.t[:], rdec)],
          writes=[GN.res[0], QKN.res[0], IDN.res[0], BMG.res[0], RDEC.res[0]])
    k.op(k.dve, lambda e: e.memset(EPS_T.t[:], EPS), writes=[EPS_T.res[0]])
    k.op(k.dve, lambda e: e.memset(ONES_D.t[:], 1.0 / D), writes=[ONES_D.res[0]])
    k.op(k.dve, lambda e: e.memset(ONES_H.t[:], 1.0 / 128), writes=[ONES_H.res[0]])
    k.op(k.dve, lambda e: e.memset(ONES_1.t[:], 1.0), writes=[ONES_1.res[0]])

    def stage_in():
        with contextlib.ExitStack() as st:
            XT = sb(st, "XT", [128, 2, D], F32, 2)
            HT = sb(st, "HTt", [128, 2, KC, 128], F32, 2)
            for i in range(36):
                b = i % 2
                if i < 2:
                    src = ctxin[0, i * 128:(i + 1) * 128, :]
                elif i < 4:
                    src = ctxin[1, (i - 2) * 128:(i - 1) * 128, :]
                elif i < 20:
                    src = xin[0, (i - 4) * 128:(i - 3) * 128, :]
                else:
                    src = xin[1, (i - 20) * 128:(i - 19) * 128, :]
                k.dma("sp", [(XT.t[:, b, :], src)], writes=[XT.res[b]])
                for g in range(4):
                    p = ps_next()

                    def fn(pe, g=g, p=p, b=b):
                        ins = None
                        for jj in range(4):
                            kc = g * 4 + jj
                            ins = pe.transpose(p.t[:, jj * 128:(jj + 1) * 128], XT.t[:, b, kc * 128:(kc + 1) * 128], IDN.t[:])
                        return ins
                    k.op(k.pe, fn, reads=[XT.res[b], IDN.res[0]], writes=[p.res[0]], skip_self=True)
                    eng = k.act if g % 2 == 0 else k.dve
                    dst = HT.t[:, b, g * 4:(g + 1) * 4, :]
                    srcp = p.t[:].rearrange("p (c t) -> p c t", c=4)
                    if g % 2 == 0:
                        k.op(k.act, lambda e, dst=dst, srcp=srcp: e.copy(dst, srcp), reads=[p.res[0]], writes=[HT.res[b]])
                    else:
                        k.op(k.dve, lambda e, dst=dst, srcp=srcp: e.tensor_copy(dst, srcp), reads=[p.res[0]], writes=[HT.res[b]])
                k.dma("sp", [(hT_v[:, :, i * 128:(i + 1) * 128], HT.t[:, b, :, :])], reads=[HT.res[b]])
            k.barrier()

    WSCR_N = 800000
    wscrs = [dscr(f"wscr{i}", [128, WSCR_N], BF16) for i in range(DEPTH)]
    wcache = {}
    wscr_offs = [0] * DEPTH
    pend_wb = {}

    def load_slab_c(key, tile, bi, parts, kcn, ncols):
        if key in wcache:
            off, n, res = wcache[key]
            src = wscrs[key[1]][:, off:off + n].rearrange("p (c n) -> p c n", c=kcn)
            h = kcn // 2
            k.dma("sp", [(tile.t[:, bi, 0:h, 0:ncols], src[:, 0:h, :]), (tile.t[:, bi, h:kcn, 0:ncols], src[:, h:kcn, :])],
                  reads=[res], writes=[tile.res[bi]])
            return
        pairs = []
        for (a, b2, src2d) in parts:
            v = src2d.rearrange("(c p) n -> p c n", p=128)
            for a2 in range(a, b2, 8):
                b3 = min(b2, a2 + 8)
                pairs.append((tile.t[:, bi, a2:b3, 0:ncols], v[:, a2 - a:b3 - a, :]))
        k.dma("pool", pairs, writes=[tile.res[bi]])
        n = kcn * ncols
        off = wscr_offs[key[1]]
        wscr_offs[key[1]] += n
        assert wscr_offs[key[1]] <= WSCR_N
        wcache[key] = (off, n, Res())
        pend_wb[(id(tile), bi)] = key

    def writeback(tile, bi, kcn, ncols):
        key = pend_wb.pop((id(tile), bi), None)
        if key is None:
            return
        off, n, res = wcache[key]
        dst = wscrs[key[1]][:, off:off + n].rearrange("p (c n) -> p c n", c=kcn)
        k.dma("sp", [(dst, tile.t[:, bi, 0:kcn, 0:ncols])], reads=[tile.res[bi]], writes=[res])

    class Stream:
        def __init__(self, loaders):
            self.loaders = loaders
            self.done = 0

        def need(self, i):
            while self.done <= min(i + 1, len(self.loaders) - 1):
                self.loaders[self.done](self.done % 2)
                self.done += 1
            return i % 2

    CA = sb(gstack, "CA", [128, KC, 3], BF16)
    BA = sb(gstack, "BA", [128, DEPTH, 144], F32)

    def der_a(l, n_):
        sc_i, g_i = ((1, l * 3 + 0), (4, l * 3 + 1), (7, l * 3 + 2))[n_]
        gb = GN.t[:, g_i, :].unsqueeze(2).to_broadcast([128, KC, 3])
        k.op(k.dve, lambda e: e.scalar_tensor_tensor(
            out=DER.t[:, l, n_, :, :], in0=MOD.t[:, l, sc_i * 16:(sc_i + 1) * 16, :], scalar=1.0, in1=gb,
            op0=ALU.add, op1=ALU.mult), reads=[MOD.res[0], GN.res[0]], writes=[DER.res[0]])

    def der_g(l, n_):
        mi, f = ((2, 0.5), (5, 1.0), (8, 0.5))[n_]
        k.op(k.dve, lambda e: e.tensor_scalar(
            out=DER.t[:, l, 3 + n_, :, :], in0=MOD.t[:, l, mi * 16:(mi + 1) * 16, :], scalar1=f, scalar2=None,
            op0=ALU.mult), reads=[MOD.res[0]], writes=[DER.res[0]])

    def adaln_task(st, slabs):
        WF = sb(st, "WFb", [128, 2, KC, 256], F32, 2)
        WA = sb(st, "WAb", [128, 2, KC, 256], BF16, 2)

        def mk(l, c0):
            def f(b):
                v = w_ada[l, :, c0:c0 + 256].rearrange("(c p) n -> p c n", p=128)
                k.dma("sp", [(WF.t[:, b, 0:8, :], v[:, 0:8, :]), (WF.t[:, b, 8:16, :], v[:, 8:16, :])], writes=[WF.res[b]])
            return f
        strm = Stream([mk(l, c0) for (l, c0) in slabs])

        def gen():
            N = len(slabs)
            for n in range(N + 1):
                if n < N:
                    bi = strm.need(n)
                    k.op(k.dve, lambda e: e.tensor_copy(WA.t[:, bi, :, :], WF.t[:, bi, :, :]), reads=[WF.res[bi]], writes=[WA.res[bi]])
                if n >= 1:
                    l, c0 = slabs[n - 1]
                    bj = (n - 1) % 2
                    for sub in range(2):
                        p = ps_next()
                        ch = c0 // 128 + sub
                        k.mm(p.t[:, 0:3], [(WA.t[:, bj, kc, sub * 128:(sub + 1) * 128], CA.t[:, kc, :]) for kc in range(KC)],
                             reads=[WA.res[bj], CA.res[0]], writes=[p.res[0]])
                        k.op(k.act, lambda e, p=p, ch=ch: e.activation(out=MOD.t[:, l, ch, :], in_=p.t[:, 0:3], func=AF.Identity,
                                                                   bias=BA.t[:, l, ch:ch + 1], scale=1.0),
                             reads=[p.res[0], BA.res[0]], writes=[MOD.res[0]])
                yield
        return gen()

    def stage_adaln_pre():
        with contextlib.ExitStack() as st:
            CC = sb(st, "CC", [128, KC, 3], F32)
            k.dma("sp", [(CC.t[:], ccat), (BA.t[:], b_ada)], writes=[CC.res[0], BA.res[0]])
            k.op(k.act, lambda e: e.activation(out=CA.t[:], in_=CC.t[:], func=AF.Silu), reads=[CC.res[0]], writes=[CA.res[0]])
            g_ = adaln_task(st, [(0, c0) for c0 in range(0, 3 * D, 256)])
            for _ in g_:
                pass
            der_a(0, 0)
            der_g(0, 0)
            k.barrier()

    bg_slabs = {"gu0": [(0, c0) for c0 in range(3 * D, 9 * D, 256)], "attn0": [(1, c0) for c0 in range(0, 9 * D, 256)]}

    def bg_finish(which):
        if which == "gu0":
            der_a(0, 1); der_a(0, 2); der_g(0, 1); der_g(0, 2)
        else:
            for n_ in range(3):
                der_a(1, n_); der_g(1, n_)
        if which == "attn0" and "moddbg" in dbg:
            k.dma("sp", [(moddbg, MOD.t[:].rearrange("p a b c -> p (a b c)"))], reads=[MOD.res[0]])

    def stage_norm(l, n_, blocks):
        sh_i = (0, 3, 6)[n_]
        with contextlib.ExitStack() as st:
            X = sb(st, "X", [128, 3, KC, 512], F32, 3)
            SQ = sb(st, "SQ", [128, 2, KC, 512], BF16, 2)
            RS = sb(st, "RS", [128, 2, 512], F32, 2)
            XN = sb(st, "XN", [128, 2, KC, 512], BF16, 4)

            def ldx(it):
                ts_ = slice(blocks[it] * 512, (blocks[it] + 1) * 512)
                k.dma("sp", [(X.t[:, it % 3, 0:8, :], hT_v[:, 0:8, ts_]), (X.t[:, it % 3, 8:16, :], hT_v[:, 8:16, ts_])], writes=[X.res[it % 3]])
            ldx(0)
            if len(blocks) > 1:
                ldx(1)
            for it, blk in enumerate(blocks):
                b = it % 2
                xb = it % 3
                j = blk_j(blk)
                ts = slice(blk * 512, (blk + 1) * 512)
                if it + 2 < len(blocks):
                    ldx(it + 2)
                k.op(k.act, lambda e: e.activation(out=SQ.t[:, b, :, :], in_=X.t[:, xb, :, :], func=AF.Square),
                     reads=[X.res[xb]], writes=[SQ.res[b]])
                p = ps_next()
                k.mm(p.t[:], [(ONES_D.t[:], SQ.t[:, b, kc, :]) for kc in range(KC)], reads=[SQ.res[b], ONES_D.res[0]], writes=[p.res[0]])
                rsqrt(RS.t[:, b, :], RS.res[b], p.t[:], p.res[0])
                rb = RS.t[:, b, :].unsqueeze(1).to_broadcast([128, KC, 512])
                k.op(k.dve, lambda e: e.tensor_tensor(out=X.t[:, xb, :, :], in0=X.t[:, xb, :, :], in1=rb, op=ALU.mult),
                     reads=[RS.res[b], X.res[xb]], writes=[X.res[xb]])
                def fn(e):
                    ins = None
                    for kc in range(0, 10):
                        ins = e.activation(out=XN.t[:, b, kc, :], in_=X.t[:, xb, kc, :], func=AF.Identity,
                                           bias=MOD.t[:, l, sh_i * 16 + kc, j:j + 1], scale=DER.t[:, l, n_, kc, j:j + 1])
                    return ins

                def fnp(e):
                    ins = None
                    for kc in range(10, KC):
                        ins = e.tensor_scalar(out=XN.t[:, b, kc, :], in0=X.t[:, xb, kc, :], scalar1=DER.t[:, l, n_, kc, j:j + 1],
                                              scalar2=MOD.t[:, l, sh_i * 16 + kc, j:j + 1], op0=ALU.mult, op1=ALU.add)
                    return ins
                k.op(k.act, fn, reads=[X.res[xb], MOD.res[0], DER.res[0]], writes=[XN.res[2 * b]])
                k.op(k.pool, fnp, reads=[X.res[xb], MOD.res[0], DER.res[0]], writes=[XN.res[2 * b + 1]])
                k.dma("sp", [(xnT_v[:, :, ts], XN.t[:, b, :, :])], reads=[XN.res[2 * b], XN.res[2 * b + 1]])
            k.barrier()

    def stage_ffn_gu(l, wg, wu, blocks, tag, bg=None):
        passes = [blocks[i:i + 3] for i in range(0, len(blocks), 3)]
        nxb = 3 if bg else 6
        with contextlib.ExitStack() as st:
            XNr = sb(st, "XNr", [128, nxb, KC, 512], BF16, nxb)
            WGU = sb(st, "WGU", [128, 2, 2 * KC, 512], BF16, 2)
            SG = sb(st, "SG", [128, 3, 512], F32, 3)
            HB = sb(st, "HB", [128, 2, 4, 512], BF16, 2)
            task = adaln_task(st, bg_slabs[bg]) if bg else None
            nsg = 0
            nhb = 0
            order = [(pi, jc) for pi in range(len(passes)) for jc in range(11)]

            def mk(jc):
                def f(b):
                    load_slab_c((tag, l, "gu", jc), WGU, b, [(0, KC, wg[l, :, jc * 512:(jc + 1) * 512]), (KC, 2 * KC, wu[l, :, jc * 512:(jc + 1) * 512])], 2 * KC, 512)
                return f
            strm = Stream([mk(jc) for (_, jc) in order])
            xloaded = set()

            def ldpass(pi):
                if pi >= len(passes) or pi in xloaded:
                    return
                xloaded.add(pi)
                for bi, blk in enumerate(passes[pi]):
                    xb = (pi * 3 + bi) % nxb
                    k.dma("sp", [(XNr.t[:, xb, :, :], xnT_v[:, :, blk * 512:(blk + 1) * 512])], writes=[XNr.res[xb]])
            n = 0
            it = 0
            for pi, ps_ in enumerate(passes):
                ldpass(pi)
                if nxb == 6:
                    ldpass(pi + 1)
                for jc in range(11):
                    wb = strm.need(n)
                    n += 1
                    writeback(WGU, wb, 2 * KC, 512)
                    for bi, blk in enumerate(ps_):
                        xb = (pi * 3 + bi) % nxb
                        hb = nhb % 2
                        nhb += 1
                        for sub in range(4):
                            pg = ps_next()
                            pu = ps_next()
                            k.mm(pg.t[:], [(WGU.t[:, wb, kc, sub * 128:(sub + 1) * 128], XNr.t[:, xb, kc, :]) for kc in range(KC)],
                                 reads=[WGU.res[wb], XNr.res[xb]], writes=[pg.res[0]])
                            k.mm(pu.t[:], [(WGU.t[:, wb, KC + kc, sub * 128:(sub + 1) * 128], XNr.t[:, xb, kc, :]) for kc in range(KC)],
                                 reads=[WGU.res[wb], XNr.res[xb]], writes=[pu.res[0]])
                            sg = nsg % 3
                            nsg += 1
                            k.op(k.act, lambda e, sg=sg, pg=pg: e.activation(out=SG.t[:, sg, :], in_=pg.t[:], func=AF.Silu),
                                 reads=[pg.res[0]], writes=[SG.res[sg]])
                            k.op(k.dve, lambda e, sg=sg, pu=pu, hb=hb, sub=sub: e.tensor_tensor(
                                out=HB.t[:, hb, sub, :], in0=SG.t[:, sg, :], in1=pu.t[:], op=ALU.mult),
                                reads=[SG.res[sg], pu.res[0]], writes=[HB.res[hb]])
                        k.dma("sp", [(hidT_v[:, jc * 4:(jc + 1) * 4, blk * 512:(blk + 1) * 512], HB.t[:, hb, :, :])], reads=[HB.res[hb]])
                        it += 1
                        if task is not None and it % 2 == 0:
                            next(task, None)
            if task is not None:
                for _ in task:
                    pass
                bg_finish(bg)
            k.barrier()

    hT_res = [Res() for _ in range(NBLK)]

    def norm_task(st, l, n_, ready, state):
        sh_i = (0, 3, 6)[n_]
        X = sb(st, "Xb", [128, 2, KC, 256], F32, 2)
        XN = sb(st, "XNb", [128, 2, KC, 256], BF16, 2)
        RS = sb(st, "RSb", [128, 256], F32)
        nh = 0
        while True:
            if not ready:
                if state["done"]:
                    return
                yield
                continue
            blk = ready.pop(0)
            j = blk_j(blk)
            for half in range(2):
                xb = nh % 2
                nh += 1
                t0 = blk * 512 + half * 256
                k.dma("act", [(X.t[:, xb, 0:8, :], hT_v[:, 0:8, t0:t0 + 256]), (X.t[:, xb, 8:16, :], hT_v[:, 8:16, t0:t0 + 256])],
                      reads=[hT_res[blk]], writes=[X.res[xb]])
                yield
                k.op(k.act, lambda e: e.activation(out=XN.t[:, xb, :, :], in_=X.t[:, xb, :, :], func=AF.Square),
                     reads=[X.res[xb]], writes=[XN.res[xb]])
                yield
                p = ps_next()
                k.mm(p.t[:, 0:256], [(ONES_D.t[:], XN.t[:, xb, kc, :]) for kc in range(KC)], reads=[XN.res[xb], ONES_D.res[0]], writes=[p.res[0]])
                rsqrt(RS.t[:], RS.res[0], p.t[:, 0:256], p.res[0])
                rb = RS.t[:].unsqueeze(1).to_broadcast([128, KC, 256])
                k.op(k.dve, lambda e: e.tensor_tensor(out=X.t[:, xb, :, :], in0=X.t[:, xb, :, :], in1=rb, op=ALU.mult),
                     reads=[RS.res[0], X.res[xb]], writes=[X.res[xb]])

                def fn(e):
                    ins = None
                    for kc in range(KC):
                        ins = e.activation(out=XN.t[:, xb, kc, :], in_=X.t[:, xb, kc, :], func=AF.Identity,
                                           bias=MOD.t[:, l, sh_i * 16 + kc, j:j + 1], scale=DER.t[:, l, n_, kc, j:j + 1])
                    return ins
                k.op(k.act, fn, reads=[X.res[xb], MOD.res[0], DER.res[0]], writes=[XN.res[xb]])
                k.dma("act", [(xnT_v[:, :, t0:t0 + 256], XN.t[:, xb, :, :])], reads=[XN.res[xb]])
                yield

    def stage_gemm_resid(l, inT_v, kcn, w2d, g_i, blocks, name, norm_after=None):
        passes = [blocks[i:i + 2] for i in range(0, len(blocks), 2)]
        nib = (2 if norm_after else 3) if kcn > 16 else 4
        with contextlib.ExitStack() as st:
            IN = sb(st, "IN" + name, [128, nib, kcn, 512], BF16, nib)
            WS = sb(st, "WS" + name, [128, 2, kcn, 256], BF16, 2)
            HO = sb(st, "HO" + name, [128, 4, 512], F32, 4)
            ready = []
            state = {"done": False}
            task = None
            if norm_after is not None:
                task = norm_task(st, norm_after[0], norm_after[1], ready, state)
                nblocks = norm_after[2]
            for blk in blocks:
                hT_res[blk] = Res()
            nho = 0
            order = [(pi, s_) for pi in range(len(passes)) for s_ in range(8)]

            def mk(s_):
                def f(b):
                    load_slab_c((name, l, "w", s_), WS, b, [(0, kcn, w2d[:, s_ * 256:(s_ + 1) * 256])], kcn, 256)
                return f
            strm = Stream([mk(s_) for (_, s_) in order])
            seq = list(blocks)
            loaded = [0]

            def ensure(upto):
                while loaded[0] < min(upto, len(seq)):
                    c = loaded[0]
                    blk = seq[c]
                    ts_ = slice(blk * 512, (blk + 1) * 512)
                    pairs = []
                    for a_ in range(0, kcn, 8):
                        b2 = min(kcn, a_ + 8)
                        pairs.append((IN.t[:, c % nib, a_:b2, :], inT_v[:, a_:b2, ts_]))
                    k.dma("sp", pairs, writes=[IN.res[c % nib]])
                    loaded[0] += 1
            n = 0
            c0 = 0
            for ps_ in passes:
                ensure(c0 + len(ps_))
                ensure(c0 + nib)
                for s in range(8):
                    wb = strm.need(n)
                    n += 1
                    writeback(WS, wb, kcn, 256)
                    for bi, blk in enumerate(ps_):
                        ib = (c0 + bi) % nib
                        j = blk_j(blk)
                        ts = slice(blk * 512, (blk + 1) * 512)
                        for sub in range(2):
                            dc = s * 2 + sub
                            p = ps_next()
                            ho = nho % 4
                            nho += 1
                            k.dma("sp", [(HO.t[:, ho, :], hT_v[:, dc, ts])], writes=[HO.res[ho]])
                            k.mm(p.t[:], [(WS.t[:, wb, kc, sub * 128:(sub + 1) * 128], IN.t[:, ib, kc, :]) for kc in range(kcn)],
                                 reads=[WS.res[wb], IN.res[ib]], writes=[p.res[0]])
                            k.op(k.dve, lambda e, p=p, ho=ho, dc=dc, j=j: e.scalar_tensor_tensor(
                                out=HO.t[:, ho, :], in0=p.t[:], scalar=DER.t[:, l, g_i, dc, j:j + 1], in1=HO.t[:, ho, :],
                                op0=ALU.mult, op1=ALU.add), reads=[p.res[0], HO.res[ho], DER.res[0]], writes=[HO.res[ho]])
                            k.dma("sp", [(hT_v[:, dc, ts], HO.t[:, ho, :])], reads=[HO.res[ho]], acc_writes=[hT_res[blk]])
                            if task is not None and (kcn > 16 or sub == 1):
                                next(task, None)
                c0 += len(ps_)
                if task is not None:
                    ready.extend(b_ for b_ in ps_ if b_ in nblocks)
            if task is not None:
                state["done"] = True
                for _ in task:
                    pass
            k.barrier()

    def stage_final():
        with contextlib.ExitStack() as st:
            X = sb(st, "Xf", [128, 2, KC, 512], F32, 2)
            SQ = sb(st, "SQf", [128, 2, KC, 512], BF16, 2)
            RS = sb(st, "RSf", [128, 2, 512], F32, 2)
            OT = sb(st, "OTf", [128, 2, D], F32, 2)
            no = 0
            def ldx(it):
                ts_ = slice((it + 1) * 512, (it + 2) * 512)
                k.dma("sp", [(X.t[:, it % 2, 0:8, :], hT_v[:, 0:8, ts_]), (X.t[:, it % 2, 8:16, :], hT_v[:, 8:16, ts_])], writes=[X.res[it % 2]])
            ldx(0)
            for it, blk in enumerate(range(1, 9)):
                b = it % 2
                ts = slice(blk * 512, (blk + 1) * 512)
                if it + 1 < 8:
                    ldx(it + 1)
                k.op(k.act, lambda e, b=b: e.activation(out=SQ.t[:, b, :, :], in_=X.t[:, b, :, :], func=AF.Square),
                     reads=[X.res[b]], writes=[SQ.res[b]])
                p = ps_next()
                k.mm(p.t[:], [(ONES_D.t[:], SQ.t[:, b, kc, :]) for kc in range(KC)], reads=[SQ.res[b], ONES_D.res[0]], writes=[p.res[0]])
                rsqrt(RS.t[:, b, :], RS.res[b], p.t[:], p.res[0])
                rb = RS.t[:, b, :].unsqueeze(1).to_broadcast([128, KC, 512])
                k.op(k.dve, lambda e, b=b, rb=rb: e.tensor_tensor(out=X.t[:, b, :, :], in0=X.t[:, b, :, :], in1=rb, op=ALU.mult),
                     reads=[RS.res[b], X.res[b]], writes=[X.res[b]])

                def fn(e, b=b):
                    ins = None
                    for kc in range(KC):
                        ins = e.activation(out=X.t[:, b, kc, :], in_=X.t[:, b, kc, :], func=AF.Identity, scale=GN.t[:, 6, kc:kc + 1])
                    return ins
                k.op(k.act, fn, reads=[X.res[b], GN.res[0]], writes=[X.res[b]])
                for tt in range(4):
                    ob = no % 2
                    no += 1
                    for g in range(4):
                        p = ps_next()

                        def fnt(pe, g=g, p=p, b=b, tt=tt):
                            ins = None
                            for jj in range(4):
                                kc = g * 4 + jj
                                ins = pe.transpose(p.t[:, jj * 128:(jj + 1) * 128], X.t[:, b, kc, tt * 128:(tt + 1) * 128], IDN.t[:])
                            return ins
                        k.op(k.pe, fnt, reads=[X.res[b], IDN.res[0]], writes=[p.res[0]], skip_self=True)
                        dst = OT.t[:, ob, g * 512:(g + 1) * 512]
                        if g % 2 == 0:
                            k.op(k.act, lambda e, dst=dst, p=p: e.copy(dst, p.t[:]), reads=[p.res[0]], writes=[OT.res[ob]])
                        else:
                            k.op(k.dve, lambda e, dst=dst, p=p: e.tensor_copy(dst, p.t[:]), reads=[p.res[0]], writes=[OT.res[ob]])
                    s = 0 if blk < 5 else 1
                    t0 = ((blk - 1) % 4) * 512 + tt * 128
                    k.dma("sp", [(out[s, t0:t0 + 128, :], OT.t[:, ob, :])], reads=[OT.res[ob]])
            k.barrier()

    def mm1(out_ap, l_ap, r_ap, start, stop, reads, writes):
        return k.op(k.pe, lambda pe: pe.matmul(out_ap, lhsT=l_ap, rhs=r_ap, start=start, stop=stop), reads, writes, skip_self=True)

    def ctx_tok(s):
        return slice(256 * s, 256 * s + 256)

    def lat_tok(s, a=0, n=2048):
        return slice(512 + 2048 * s + a, 512 + 2048 * s + a + n)

    cp_n = [0]

    def copy_alt(dst, src, reads, writes):
        cp_n[0] += 1
        if cp_n[0] % 2 == 0:
            k.op(k.act, lambda e: e.copy(dst, src), reads=reads, writes=writes)
        else:
            k.op(k.dve, lambda e: e.tensor_copy(dst, src), reads=reads, writes=writes)

    def stage_win(l, blocks):
        passes = [blocks[i:i + 3] for i in range(0, len(blocks), 3)]
        win = Wd["w_in"]
        with contextlib.ExitStack() as st:
            XNr = sb(st, "XNw", [128, 3, KC, 512], BF16, 3)
            WS = sb(st, "WSw", [128, 2, KC, 512], BF16, 2)
            COS = sb(st, "COS", [128, 2048], F32)
            SIN = sb(st, "SIN", [128, 2048], F32)
            PSW = sb(st, "PSW", [128, 128], BF16)
            OB = sb(st, "OBw", [128, 4, 4, 512], BF16, 4)
            TB = sb(st, "TBw", [128, 2, 512], BF16, 2)
            SQ = sb(st, "SQw", [128, 4, 512], BF16, 4)
            RSq = sb(st, "RSw", [128, 4, 512], F32, 4)
            QN = sb(st, "QNw", [128, 4, 512], BF16, 4)
            T1 = sb(st, "T1w", [128, 4, 512], F32, 4)
            T2 = sb(st, "T2w", [128, 4, 512], F32, 4)
            k.dma("sp", [(COS.t[:], cos_t), (SIN.t[:], sin_t), (PSW.t[:], pswap)], writes=[COS.res[0], SIN.res[0], PSW.res[0]])
            cnt = {"w": 0, "ob": 0, "tb": 0, "q": 0}

            pending = []

            def advance():
                for g_ in list(pending):
                    try:
                        next(g_)
                    except StopIteration:
                        pending.remove(g_)

            def drain():
                while pending:
                    advance()

            def qk_epilogue(p, which, blk, dst, dst_res, done):
                qi = cnt["q"] % 4
                cnt["q"] += 1
                k.op(k.act, lambda e: e.activation(out=SQ.t[:, qi, :], in_=p.t[:], func=AF.Square), reads=[p.res[0]], writes=[SQ.res[qi]])
                yield
                p2 = ps_next()
                k.mm(p2.t[:], [(ONES_H.t[:], SQ.t[:, qi, :])], reads=[SQ.res[qi], ONES_H.res[0]], writes=[p2.res[0]])
                rsqrt(RSq.t[:, qi, :], RSq.res[qi], p2.t[:], p2.res[0])
                if blk == 0:
                    k.op(k.dve, lambda e: e.scalar_tensor_tensor(out=dst, in0=p.t[:], scalar=QKN.t[:, l, which:which + 1], in1=RSq.t[:, qi, :],
                                                                 op0=ALU.mult, op1=ALU.mult), reads=[p.res[0], RSq.res[qi], QKN.res[0]], writes=[dst_res])
                    done()
                    return
                k.op(k.dve, lambda e: e.scalar_tensor_tensor(out=QN.t[:, qi, :], in0=p.t[:], scalar=QKN.t[:, l, which:which + 1], in1=RSq.t[:, qi, :],
                                                             op0=ALU.mult, op1=ALU.mult), reads=[p.res[0], RSq.res[qi], QKN.res[0]], writes=[QN.res[qi]])
                pos = ((blk - 1) % 4) * 512
                k.op(k.dve, lambda e: e.tensor_tensor(out=T1.t[:, qi, :], in0=QN.t[:, qi, :], in1=COS.t[:, pos:pos + 512], op=ALU.mult),
                     reads=[QN.res[qi], COS.res[0]], writes=[T1.res[qi]])
                yield
                p3 = ps_next()
                k.mm(p3.t[:], [(PSW.t[:], QN.t[:, qi, :])], reads=[PSW.res[0], QN.res[qi]], writes=[p3.res[0]])
                k.op(k.dve, lambda e: e.tensor_tensor(out=T2.t[:, qi, :], in0=p3.t[:], in1=SIN.t[:, pos:pos + 512], op=ALU.mult),
                     reads=[p3.res[0], SIN.res[0]], writes=[T2.res[qi]])
                k.op(k.dve, lambda e: e.tensor_tensor(out=dst, in0=T1.t[:, qi, :], in1=T2.t[:, qi, :], op=ALU.add),
                     reads=[T1.res[qi], T2.res[qi]], writes=[dst_res])
                done()

            def mkw(si):
                def f(b):
                    load_slab_c(("win", l, si), WS, b, [(0, KC, win[l, :, si * 512:(si + 1) * 512])], KC, 512)
                return f
            strm = Stream([mkw(si) for _ in passes for si in range(8)])
            for ps_ in passes:
                for bi, blk in enumerate(ps_):
                    k.dma("sp", [(XNr.t[:, bi, :, :], xnT_v[:, :, blk * 512:(blk + 1) * 512])], writes=[XNr.res[bi]])
                for si in range(8):
                    wb = strm.need(cnt["w"])
                    cnt["w"] += 1
                    writeback(WS, wb, KC, 512)
                    if si in (0, 1, 2, 3, 4, 6, 7):
                        subs = (0, 1) if si == 2 else (0, 1, 2, 3)
                        for bi, blk in enumerate(ps_):
                            ob = cnt["ob"] % 4
                            cnt["ob"] += 1
                            ts = slice(blk * 512, (blk + 1) * 512)
                            n = len(subs)
                            if si <= 1:
                                dv = qkT_v[:, si * 4:si * 4 + 4, ts]
                            elif si == 2:
                                dv = qkT_v[:, 8:10, ts]
                            else:
                                c0 = {3: 0, 4: 4, 6: 8, 7: 12}[si]
                                dv = rT_v[:, c0:c0 + 4, ts]
                            left = [n]

                            def done(left=left, dv=dv, ob=ob, n=n):
                                left[0] -= 1
                                if left[0] == 0:
                                    k.dma("sp", [(dv, OB.t[:, ob, 0:n, :])], reads=[OB.res[ob]])
                            for sub in subs:
                                p = ps_next()
                                k.mm(p.t[:], [(WS.t[:, wb, kc, sub * 128:(sub + 1) * 128], XNr.t[:, bi, kc, :]) for kc in range(KC)],
                                     reads=[WS.res[wb], XNr.res[bi]], writes=[p.res[0]])
                                dst = OB.t[:, ob, sub, :]
                                if si <= 2:
                                    g_ = qk_epilogue(p, 0 if si < 2 else 1, blk, dst, OB.res[ob], done)
                                    next(g_)
                                    advance()
                                    pending.append(g_)
                                elif si == 6:
                                    k.op(k.act, lambda e, dst=dst, p=p: e.activation(out=dst, in_=p.t[:], func=AF.Silu), reads=[p.res[0]], writes=[OB.res[ob]])
                                    done()
                                else:
                                    copy_alt(dst, p.t[:], [p.res[0]], [OB.res[ob]])
                                    done()
                        if si == 2:
                            drain()
                    if si in (2, 4, 5):
                        c0, c1, v0 = {2: (256, 512, 0), 4: (0, 512, 768), 5: (0, 512, 256)}[si]
                        ncol = c1 - c0
                        for bi, blk in enumerate(ps_):
                            for tt in range(4):
                                p = ps_next()
                                k.mm(p.t[:, 0:ncol], [(XNr.t[:, bi, kc, tt * 128:(tt + 1) * 128], WS.t[:, wb, kc, c0:c1]) for kc in range(KC)],
                                     reads=[WS.res[wb], XNr.res[bi]], writes=[p.res[0]])
                                tb = cnt["tb"] % 2
                                cnt["tb"] += 1
                                copy_alt(TB.t[:, tb, 0:ncol], p.t[:, 0:ncol], [p.res[0]], [TB.res[tb]])
                                r0 = blk * 512 + tt * 128
                                k.dma("sp", [(vtok[r0:r0 + 128, v0:v0 + ncol], TB.t[:, tb, 0:ncol])], reads=[TB.res[tb]])
            k.barrier()

    def stage_attn(l, bg=False):
        sc = 128.0 ** -0.5
        with contextlib.ExitStack() as st:
            KT = sb(st, "KTa", [128, 2, 2304], BF16, 2)
            V = sb(st, "Va", [128, 2, 18, 128], BF16, 2)
            QT = sb(st, "QTa", [128, 3, 512], BF16, 3)
            PT = sb(st, "PTa", [128, 4, 512], BF16, 4)
            RD = sb(st, "RDa", [128, 2, 512], F32, 2)
            OA = sb(st, "OAa", [128, 2, 512], BF16, 2)
            ACC = sb(st, "ACCa", [128, 2, 512], BF16, 2)
            task = adaln_task(st, bg_slabs["attn0"]) if bg else None
            cnt = {"pt": 0, "s": 0}
            units = []
            for s in range(2):
                for g in range(2):
                    gi = s * 2 + g
                    for hq in range(4 * g, 4 * g + 4):
                        if l == 0:
                            units.append((gi, s, g, hq, ctx_tok(s), 256, 2))
                        for qb in range(4):
                            units.append((gi, s, g, hq, lat_tok(s, qb * 512, 512), 512, 18))
            loaded_g = set()

            def load_unit(u):
                gi, s, g, hq, toks, nq, nkc = units[u]
                if gi not in loaded_g:
                    loaded_g.add(gi)
                    kv = gi % 2
                    k.dma("sp", [(KT.t[:, kv, 0:256], qkT_v[:, 8 + g, ctx_tok(s)]), (KT.t[:, kv, 256:2304], qkT_v[:, 8 + g, lat_tok(s)])],
                          writes=[KT.res[kv]])
                    k.dma("sp", [(V.t[:, kv, 0:2, :], vtok[ctx_tok(s), g * 128:(g + 1) * 128].rearrange("(c p) d -> p c d", p=128)),
                                 (V.t[:, kv, 2:18, :], vtok[lat_tok(s), g * 128:(g + 1) * 128].rearrange("(c p) d -> p c d", p=128))],
                          writes=[V.res[kv]])
                k.dma("sp", [(QT.t[:, u % 3, 0:nq], qkT_v[:, hq, toks])], writes=[QT.res[u % 3]])
            load_unit(0)
            for u, (gi, s, g, hq, toks, nq, nkc) in enumerate(units):
                if u + 1 < len(units):
                    load_unit(u + 1)
                kv = gi % 2
                qi = u % 3
                pss = []
                pO, pD = (PS[4], PS[5]) if u % 2 == 0 else (PS[6], PS[7])
                LOOK = 3

                def S(kc):
                    p = PS[cnt["s"] % 4]
                    cnt["s"] += 1
                    k.mm(p.t[:, 0:nq], [(KT.t[:, kv, kc * 128:(kc + 1) * 128], QT.t[:, qi, 0:nq])],
                         reads=[KT.res[kv], QT.res[qi]], writes=[p.res[0]])
                    pss.append(p)
                for kc in range(min(LOOK, nkc)):
                    S(kc)
                for kc in range(nkc):
                    if kc + LOOK < nkc:
                        S(kc + LOOK)
                    p = pss[kc]
                    pt = cnt["pt"] % 4
                    cnt["pt"] += 1
                    k.op(k.act, lambda e, p=p, pt=pt: e.activation(out=PT.t[:, pt, 0:nq], in_=p.t[:, 0:nq], func=AF.Exp, scale=sc),
                         reads=[p.res[0]], writes=[PT.res[pt]])
                    mm1(pO.t[:, 0:nq], V.t[:, kv, kc, :], PT.t[:, pt, 0:nq], kc == 0, kc == nkc - 1, [V.res[kv], PT.res[pt]], [pO.res[0]])
                    ai = u % 2
                    if kc == 0:
                        k.op(k.dve, lambda e, pt=pt: e.tensor_copy(ACC.t[:, ai, 0:nq], PT.t[:, pt, 0:nq]), reads=[PT.res[pt]], writes=[ACC.res[ai]])
                    else:
                        k.op(k.dve, lambda e, pt=pt: e.tensor_tensor(out=ACC.t[:, ai, 0:nq], in0=ACC.t[:, ai, 0:nq], in1=PT.t[:, pt, 0:nq], op=ALU.add),
                             reads=[PT.res[pt], ACC.res[ai]], writes=[ACC.res[ai]])
                mm1(pD.t[:, 0:nq], ONES_1.t[:], ACC.t[:, ai, 0:nq], True, True, [ONES_1.res[0], ACC.res[ai]], [pD.res[0]])
                oi = u % 2
                k.op(k.dve, lambda e: e.reciprocal(RD.t[:, oi, 0:nq], pD.t[:, 0:nq]), reads=[pD.res[0]], writes=[RD.res[oi]])
                k.op(k.dve, lambda e: e.tensor_tensor(out=OA.t[:, oi, 0:nq], in0=pO.t[:, 0:nq], in1=RD.t[:, oi, 0:nq], op=ALU.mult),
                     reads=[pO.res[0], RD.res[oi]], writes=[OA.res[oi]])
                k.dma("sp", [(yT_v[:, 4 + hq, toks], OA.t[:, oi, 0:nq])], reads=[OA.res[oi]])
                if task is not None:
                    next(task, None)
            if task is not None:
                for _ in task:
                    pass
                bg_finish("attn0")
            k.barrier()

    def stage_ret(l):
        cs = 128.0 ** -0.5
        with contextlib.ExitStack() as st:
            RT = sb(st, "RTr", [128, 8, 128], F32)
            LG = sb(st, "LGr", [128, 8], F32)
            E = sb(st, "Er", [128, 2, 128], F32)
            MASK = sb(st, "MASKr", [128, 128], F32)
            DQ = sb(st, "DQr", [128, 2, 128], F32)
            DK = sb(st, "DKr", [128, 2], F32)
            DCc = sb(st, "DCr", [128, 2], F32)
            QT = sb(st, "QTr", [128, 2304], BF16)
            KTt = sb(st, "KTr", [128, 2304], BF16)
            G = sb(st, "Gr", [128, 2304], BF16)
            KTOK = sb(st, "KTOKr", [128, 18, 128], BF16)
            VTOK = sb(st, "VTOKr", [128, 18, 128], BF16)
            QD = sb(st, "QDr", [128, 2, 2304], BF16, 2)
            KD = sb(st, "KDr", [128, 2, 18, 128], BF16, 2)
            U = sb(st, "Ur", [128, 2, 18, 128], F32, 2)
            S = sb(st, "Sr", [128, 2, 18, 128], F32, 2)
            SBF = sb(st, "SBFr", [128, 2, 18, 128], BF16)
            PM = sb(st, "PMr", [128, 2, 4, 128], BF16, 2)
            OSQ = sb(st, "OSQr", [128, 2, 512], BF16, 2)
            RS = sb(st, "RSr", [128, 2, 512], F32, 2)
            T = sb(st, "Tr", [128, 2, 512], F32, 2)
            YR = sb(st, "YRr", [128, 2304], BF16)
            k.dma("sp", [(RT.t[:], rtab)], writes=[RT.res[0]])
            k.op(k.act, lambda e: e.activation(out=LG.t[:], in_=RDEC.t[:, l * 8:(l + 1) * 8], func=AF.Exp), reads=[RDEC.res[0]], writes=[LG.res[0]])
            k.op(k.dve, lambda e: e.tensor_scalar(out=LG.t[:], in0=LG.t[:], scalar1=-1.0, scalar2=None, op0=ALU.mult), reads=[LG.res[0]], writes=[LG.res[0]])
            cnt = {"g": 0}
            for h in range(4):
                lgs = [LG.t[:, h:h + 1], LG.t[:, 4 + h:5 + h]]
                for d_ in range(2):
                    k.op(k.act, lambda e, d_=d_: e.activation(out=E.t[:, d_, :], in_=RT.t[:, 2 * d_, :], func=AF.Exp, scale=lgs[d_]),
                         reads=[RT.res[0], LG.res[0]], writes=[E.res[0]])
                    k.op(k.dve, lambda e, d_=d_: e.tensor_tensor(out=E.t[:, d_, :], in0=E.t[:, d_, :], in1=RT.t[:, 2 * d_ + 1, :], op=ALU.mult),
                         reads=[E.res[0], RT.res[0]], writes=[E.res[0]])
                    k.op(k.act, lambda e, d_=d_: e.activation(out=DQ.t[:, d_, :], in_=RT.t[:, 4 + d_, :], func=AF.Exp, scale=lgs[d_]),
                         reads=[RT.res[0], LG.res[0]], writes=[DQ.res[0]])
                    k.op(k.act, lambda e, d_=d_: e.activation(out=DK.t[:, d_:d_ + 1], in_=RT.t[:, 6 + d_, 0:1], func=AF.Exp, scale=lgs[d_]),
                         reads=[RT.res[0], LG.res[0]], writes=[DK.res[0]])
                    k.op(k.act, lambda e, d_=d_: e.activation(out=DCc.t[:, d_:d_ + 1], in_=RT.t[:, 5, 0:1], func=AF.Exp, scale=lgs[d_]),
                         reads=[RT.res[0], LG.res[0]], writes=[DCc.res[0]])
                k.op(k.dve, lambda e: e.tensor_scalar(out=DK.t[:], in0=DK.t[:], scalar1=cs, scalar2=None, op0=ALU.mult), reads=[DK.res[0]], writes=[DK.res[0]])
                k.op(k.dve, lambda e: e.tensor_tensor(out=MASK.t[:], in0=E.t[:, 0, :], in1=E.t[:, 1, :], op=ALU.add), reads=[E.res[0]], writes=[MASK.res[0]])
                k.op(k.dve, lambda e: e.tensor_scalar(out=MASK.t[:], in0=MASK.t[:], scalar1=cs, scalar2=None, op0=ALU.mult), reads=[MASK.res[0]], writes=[MASK.res[0]])
                for s in range(2):
                    def ld(dst, row):
                        return [(dst[:, 0:256], rT_v[:, row, ctx_tok(s)]), (dst[:, 256:2304], rT_v[:, row, lat_tok(s)])]
                    k.dma("sp", ld(QT.t, h), writes=[QT.res[0]])
                    k.dma("sp", ld(KTt.t, 4 + h), writes=[KTt.res[0]])
                    k.dma("sp", ld(G.t, 8 + h), writes=[G.res[0]])

                    def ldt(dst, c0):
                        return [(dst[:, 0:2, :], vtok[ctx_tok(s), c0:c0 + 128].rearrange("(c p) d -> p c d", p=128)),
                                (dst[:, 2:18, :], vtok[lat_tok(s), c0:c0 + 128].rearrange("(c p) d -> p c d", p=128))]
                    k.dma("sp", ldt(KTOK.t, 768 + h * 128), writes=[KTOK.res[0]])
                    k.dma("sp", ldt(VTOK.t, 256 + h * 128), writes=[VTOK.res[0]])
                    QT3 = QT.t[:].rearrange("p (c i) -> p c i", c=18)
                    for d_ in range(2):
                        dqb = DQ.t[:, d_, :].unsqueeze(1).to_broadcast([128, 18, 128])
                        k.op(k.pool, lambda e, d_=d_, dqb=dqb: e.tensor_tensor(out=QD.t[:, d_, :].rearrange("p (c i) -> p c i", c=18), in0=QT3, in1=dqb, op=ALU.mult),
                             reads=[QT.res[0], DQ.res[0]], writes=[QD.res[d_]])
                        k.op(k.act, lambda e, d_=d_: e.activation(out=KD.t[:, d_, :, :], in_=KTOK.t[:], func=AF.Identity, scale=DK.t[:, d_:d_ + 1]),
                             reads=[KTOK.res[0], DK.res[0]], writes=[KD.res[d_]])
                    for d_ in range(2):
                        for c0, n in ((0, 4), (4, 4), (8, 4), (12, 4), (16, 2)):
                            p = ps_next()

                            def fn(pe, d_=d_, c0=c0, n=n, p=p):
                                ins = None
                                for jj in range(n):
                                    ins = pe.matmul(p.t[:, jj * 128:(jj + 1) * 128], lhsT=KD.t[:, d_, c0 + jj, :], rhs=VTOK.t[:, c0 + jj, :], start=True, stop=True)
                                return ins
                            k.op(k.pe, fn, reads=[KD.res[d_], VTOK.res[0]], writes=[p.res[0]], skip_self=True)
                            copy_alt(U.t[:, d_, c0:c0 + n, :], p.t[:, 0:n * 128].rearrange("p (c e) -> p c e", c=n), [p.res[0]], [U.res[d_]])
                    orders = [list(range(18)), [1, 0] + list(range(17, 1, -1))]
                    for d_ in range(2):
                        k.op(k.dve, lambda e, d_=d_: e.memset(S.t[:, d_, orders[d_][0], :], 0.0), writes=[S.res[d_]])
                    for idx in range(17):
                        for d_ in range(2):
                            cur, nxt = orders[d_][idx], orders[d_][idx + 1]
                            k.op(k.dve, lambda e, d_=d_, cur=cur, nxt=nxt: e.scalar_tensor_tensor(
                                out=S.t[:, d_, nxt, :], in0=S.t[:, d_, cur, :], scalar=DCc.t[:, d_:d_ + 1], in1=U.t[:, d_, cur, :],
                                op0=ALU.mult, op1=ALU.add), reads=[S.res[d_], U.res[d_], DCc.res[0]], writes=[S.res[d_]])
                    k.op(k.act, lambda e: e.copy(SBF.t[:], S.t[:]), reads=[S.res[0], S.res[1]], writes=[SBF.res[0]])
                    for (c0, n) in ((0, 2), (2, 4), (6, 4), (10, 4), (14, 4)):
                        if l == 1 and c0 == 0:
                            continue
                        gi = cnt["g"] % 2
                        cnt["g"] += 1
                        pin = ps_next()

                        def fin(pe, c0=c0, n=n, pin=pin):
                            ins = None
                            for jj in range(n):
                                c = c0 + jj
                                ins = pe.matmul(pin.t[:, jj * 128:(jj + 1) * 128], lhsT=KTt.t[:, c * 128:(c + 1) * 128], rhs=QT.t[:, c * 128:(c + 1) * 128], start=True, stop=True)
                            return ins
                        k.op(k.pe, fin, reads=[KTt.res[0], QT.res[0]], writes=[pin.res[0]], skip_self=True)
                        mb = MASK.t[:].unsqueeze(1).to_broadcast([128, n, 128])
                        k.op(k.dve, lambda e, gi=gi, n=n, pin=pin, mb=mb: e.tensor_tensor(
                            out=PM.t[:, gi, 0:n, :], in0=pin.t[:, 0:n * 128].rearrange("p (c i) -> p c i", c=n), in1=mb, op=ALU.mult),
                            reads=[pin.res[0], MASK.res[0]], writes=[PM.res[gi]])
                        po = ps_next()

                        def fo(pe, c0=c0, n=n, po=po, gi=gi):
                            ins = None
                            for jj in range(n):
                                c = c0 + jj
                                o_ = po.t[:, jj * 128:(jj + 1) * 128]
                                pe.matmul(o_, lhsT=VTOK.t[:, c, :], rhs=PM.t[:, gi, jj, :], start=True, stop=False)
                                pe.matmul(o_, lhsT=SBF.t[:, 0, c, :], rhs=QD.t[:, 0, c * 128:(c + 1) * 128], start=False, stop=False)
                                ins = pe.matmul(o_, lhsT=SBF.t[:, 1, c, :], rhs=QD.t[:, 1, c * 128:(c + 1) * 128], start=False, stop=True)
                            return ins
                        k.op(k.pe, fo, reads=[VTOK.res[0], PM.res[gi], SBF.res[0], QD.res[0], QD.res[1]], writes=[po.res[0]], skip_self=True)
                        w = n * 128
                        k.op(k.act, lambda e, gi=gi, po=po, w=w: e.activation(out=OSQ.t[:, gi, 0:w], in_=po.t[:, 0:w], func=AF.Square), reads=[po.res[0]], writes=[OSQ.res[gi]])
                        pss = ps_next()
                        k.mm(pss.t[:, 0:w], [(ONES_H.t[:], OSQ.t[:, gi, 0:w])], reads=[OSQ.res[gi], ONES_H.res[0]], writes=[pss.res[0]])
                        rsqrt(RS.t[:, gi, 0:w], RS.res[gi], pss.t[:, 0:w], pss.res[0])
                        k.op(k.dve, lambda e, gi=gi, po=po, w=w: e.tensor_tensor(out=T.t[:, gi, 0:w], in0=po.t[:, 0:w], in1=RS.t[:, gi, 0:w], op=ALU.mult),
                             reads=[po.res[0], RS.res[gi]], writes=[T.res[gi]])
                        k.op(k.pool, lambda e, gi=gi, c0=c0, w=w: e.tensor_tensor(out=YR.t[:, c0 * 128:c0 * 128 + w], in0=T.t[:, gi, 0:w], in1=G.t[:, c0 * 128:c0 * 128 + w], op=ALU.mult),
                             reads=[T.res[gi], G.res[0]], writes=[YR.res[0]])
                    pairs = [(yT_v[:, 12 + h, lat_tok(s)], YR.t[:, 256:2304])]
                    if l == 0:
                        pairs.append((yT_v[:, 12 + h, ctx_tok(s)], YR.t[:, 0:256]))
                    k.dma("sp", pairs, reads=[YR.res[0]])
            k.barrier()

    def stage_four(l):
        with contextlib.ExitStack() as st:
            DCt = sb(st, "DCf", [128, 256], BF16)
            ZT = sb(st, "ZTf", [128, 4, 2048], BF16)
            A = sb(st, "Af", [128, 16, 4, 256], BF16, 32)
            TL = sb(st, "TLf", [128, 2, 2, 16, 512], BF16, 2)
            YF = sb(st, "YFf", [128, 2, 512], BF16, 2)
            k.dma("sp", [(DCt.t[:], dftc)], writes=[DCt.res[0]])
            cnt = {"t": 0, "y": 0}
            units = []
            for s in range(2):
                units.append((lat_tok(s), 2048, dftL))
            if l == 0:
                for s in range(2):
                    units.append((ctx_tok(s), 256, dftS))
            for (toks, L, tab) in units:
                ntt = L // 128
                bw = min(512, L)
                k.dma("sp", [(ZT.t[:, :, 0:L], rT_v[:, 12:16, toks])], writes=[ZT.res[0]])
                for tt in range(ntt):
                    for gp in range(2):
                        p = ps_next()

                        def fa(pe, tt=tt, gp=gp, p=p):
                            ins = None
                            for gi in range(2):
                                ins = pe.matmul(p.t[:, gi * 256:(gi + 1) * 256], lhsT=ZT.t[:, 2 * gp + gi, tt * 128:(tt + 1) * 128], rhs=DCt.t[:], start=True, stop=True)
                            return ins
                        k.op(k.pe, fa, reads=[ZT.res[0], DCt.res[0]], writes=[p.res[0]], skip_self=True)
                        copy_alt(A.t[:, tt, 2 * gp:2 * gp + 2, :], p.t[:].rearrange("p (g c) -> p g c", g=2), [p.res[0]], [A.res[tt * 2 + gp]])
                for tb in range(L // bw):
                    ti = cnt["t"] % 2
                    cnt["t"] += 1
                    tv = tab.rearrange("a (c p) n -> a p c n", p=128)
                    k.dma("sp", [(TL.t[:, ti, a, 0:ntt, 0:bw], tv[a, :, :, tb * bw:(tb + 1) * bw]) for a in range(2)], writes=[TL.res[ti]])
                    for g in range(4):
                        p = ps_next()
                        pairs = []
                        for tt in range(ntt):
                            pairs.append((A.t[:, tt, g, 0:128], TL.t[:, ti, 0, tt, 0:bw]))
                            pairs.append((A.t[:, tt, g, 128:256], TL.t[:, ti, 1, tt, 0:bw]))
                        k.mm(p.t[:, 0:bw], pairs, reads=A.res + [TL.res[ti]], writes=[p.res[0]])
                        yi = cnt["y"] % 2
                        cnt["y"] += 1
                        copy_alt(YF.t[:, yi, 0:bw], p.t[:, 0:bw], [p.res[0]], [YF.res[yi]])
                        t0 = toks.start + tb * bw
                        k.dma("sp", [(yT_v[:, g, t0:t0 + bw], YF.t[:, yi, 0:bw])], reads=[YF.res[yi]])
            k.barrier()

    def stage_merge(l, blocks):
        passes = [blocks[i:i + 2] for i in range(0, len(blocks), 2)]
        wmg = Wd["w_merge_gate"]
        with contextlib.ExitStack() as st:
            XNr = sb(st, "XNm", [128, 2, KC, 512], BF16, 2)
            Y = sb(st, "Ym", [128, 2, KC, 512], BF16, 2)
            WG = sb(st, "WGm", [128, 2, 4 * KC, 256], BF16, 2)
            SGm = sb(st, "SGm", [128, 3, 512], F32, 3)
            ACC = sb(st, "ACCm", [128, 2, 512], F32, 2)
            Tm = sb(st, "Tm", [128, 2, 512], F32, 2)
            MB = sb(st, "MBm", [128, 2, 2, 512], BF16, 2)
            cnt = {"w": 0, "sg": 0, "acc": 0, "t": 0, "mb": 0}

            def mkm(s_):
                def f(b):
                    cs_ = slice(s_ * 256, (s_ + 1) * 256)
                    parts = [(gi * KC, (gi + 1) * KC, wmg[l, :, gi * D + s_ * 256: gi * D + (s_ + 1) * 256]) for gi in range(3)]
                    parts += [(48, 52, Wd["w_branch_fourier"][l, :, cs_]), (52, 60, Wd["w_branch_attn"][l, :, cs_]), (60, 64, Wd["w_branch_ret"][l, :, cs_])]
                    load_slab_c(("mg", l, s_), WG, b, parts, 4 * KC, 256)
                return f
            strm = Stream([mkm(s_) for _ in passes for s_ in range(8)])
            for ps_ in passes:
                for bi, blk in enumerate(ps_):
                    ts = slice(blk * 512, (blk + 1) * 512)
                    k.dma("sp", [(XNr.t[:, bi, :, :], xnT_v[:, :, ts])], writes=[XNr.res[bi]])
                    k.dma("sp", [(Y.t[:, bi, :, :], yT_v[:, :, ts])], writes=[Y.res[bi]])
                for s in range(8):
                    wb = strm.need(cnt["w"])
                    cnt["w"] += 1
                    writeback(WG, wb, 4 * KC, 256)
                    for bi, blk in enumerate(ps_):
                        j = blk_j(blk)
                        mb = cnt["mb"] % 2
                        cnt["mb"] += 1
                        for sub in range(2):
                            dc = 2 * s + sub
                            ai = cnt["acc"] % 2
                            cnt["acc"] += 1
                            for gi, (k0, k1) in enumerate(((0, 4), (4, 12), (12, 16))):
                                pg = ps_next()
                                k.mm(pg.t[:], [(WG.t[:, wb, gi * KC + kc, sub * 128:(sub + 1) * 128], XNr.t[:, bi, kc, :]) for kc in range(KC)],
                                     reads=[WG.res[wb], XNr.res[bi]], writes=[pg.res[0]])
                                pb = ps_next()
                                k.mm(pb.t[:], [(WG.t[:, wb, 48 + kc, sub * 128:(sub + 1) * 128], Y.t[:, bi, kc, :]) for kc in range(k0, k1)],
                                     reads=[WG.res[wb], Y.res[bi]], writes=[pb.res[0]])
                                sg = cnt["sg"] % 3
                                cnt["sg"] += 1
                                k.op(k.act, lambda e, sg=sg, pg=pg, gi=gi, dc=dc: e.activation(out=SGm.t[:, sg, :], in_=pg.t[:], func=AF.Sigmoid,
                                                                                           bias=BMG.t[:, l, gi * 16 + dc:gi * 16 + dc + 1], scale=1.0),
                                     reads=[pg.res[0], BMG.res[0]], writes=[SGm.res[sg]])
                                if gi == 0:
                                    k.op(k.dve, lambda e, sg=sg, pb=pb, ai=ai: e.tensor_tensor(out=ACC.t[:, ai, :], in0=SGm.t[:, sg, :], in1=pb.t[:], op=ALU.mult),
                                         reads=[SGm.res[sg], pb.res[0]], writes=[ACC.res[ai]])
                                else:
                                    ti = cnt["t"] % 2
                                    cnt["t"] += 1
                                    k.op(k.dve, lambda e, sg=sg, pb=pb, ti=ti: e.tensor_tensor(out=Tm.t[:, ti, :], in0=SGm.t[:, sg, :], in1=pb.t[:], op=ALU.mult),
                                         reads=[SGm.res[sg], pb.res[0]], writes=[Tm.res[ti]])
                                    if gi == 1:
                                        k.op(k.dve, lambda e, ai=ai, ti=ti: e.tensor_tensor(out=ACC.t[:, ai, :], in0=ACC.t[:, ai, :], in1=Tm.t[:, ti, :], op=ALU.add),
                                             reads=[ACC.res[ai], Tm.res[ti]], writes=[ACC.res[ai]])
                                    else:
                                        k.op(k.dve, lambda e, ai=ai, ti=ti, mb=mb, sub=sub: e.tensor_tensor(out=MB.t[:, mb, sub, :], in0=ACC.t[:, ai, :], in1=Tm.t[:, ti, :], op=ALU.add),
                                             reads=[ACC.res[ai], Tm.res[ti]], writes=[MB.res[mb]])
                        k.dma("sp", [(mT_v[:, 2 * s:2 * s + 2, blk * 512:(blk + 1) * 512], MB.t[:, mb, :, :])], reads=[MB.res[mb]])
            k.barrier()

    allb = list(range(9))
    latb = list(range(1, 9))
    if want("in"):
        stage_in()
    if want("adaln"):
        stage_adaln_pre()
    if want("ffn1_0"):
        stage_norm(0, 0, allb)
    for l in range(DEPTH):
        mixb = allb if l == 0 else latb
        last = (l == DEPTH - 1)
        if want(f"ffn1_{l}"):
            stage_ffn_gu(l, Wd["ffn1_w_gate"], Wd["ffn1_w_up"], allb, "f1", bg=("gu0" if l == 0 else None))
            stage_gemm_resid(l, hidT_v, FKC, Wd["ffn1_w_down"][l], 3, allb, "d1", norm_after=(l, 1, allb))
        if want(f"win_{l}"):
            stage_win(l, allb)
        if want(f"attn_{l}"):
            stage_attn(l, bg=(l == 0))
        if want(f"ret_{l}"):
            stage_ret(l)
        if want(f"four_{l}"):
            stage_four(l)
        if want(f"merge_{l}"):
            stage_merge(l, mixb)
            stage_gemm_resid(l, mT_v, KC, Wd["w_out"][l], 4, mixb, "o", norm_after=(l, 2, mixb))
        if want(f"ffn2_{l}"):
            stage_ffn_gu(l, Wd["ffn2_w_gate"], Wd["ffn2_w_up"], mixb, "f2")
            stage_gemm_resid(l, hidT_v, FKC, Wd["ffn2_w_down"][l], 5, mixb, "d2", norm_after=(None if last else (l + 1, 0, allb)))
            if not last:
                pass
    if want("final"):
        stage_final()
    k.barrier()


def _consts():
    c = {}
    c["ident"] = np.eye(128, dtype=np.float32)
    t = np.arange(2048)
    row = (t // 64).astype(np.float32)
    col = (t % 64).astype(np.float32)
    inv = (10000.0 ** (-np.arange(0, 64, 2, dtype=np.float32) / 64)).astype(np.float32)
    ang = np.zeros((128, 2048), np.float32)
    for d in range(128):
        f = inv[d % 32]
        ang[d] = (row if d < 64 else col) * f
    c["cos_t"] = np.cos(ang).astype(np.float32)
    c["sin_t"] = np.sin(ang).astype(np.float32)
    P = np.zeros((128, 128), np.float32)
    for base in (0, 64):
        for i in range(32):
            P[base + i, base + 32 + i] = -1.0
            P[base + 32 + i, base + i] = 1.0
    c["pswap"] = P.T.copy().astype(ml_dtypes.bfloat16)
    p = np.arange(128, dtype=np.float32)[:, None]
    i = np.arange(128, dtype=np.float32)[None, :]
    rt = np.zeros((128, 8, 128), np.float32)
    rt[:, 0] = np.maximum(i - p, 0)
    rt[:, 1] = (i >= p)
    rt[:, 2] = np.maximum(p - i, 0)
    rt[:, 3] = (p >= i)
    rt[:, 4] = i + 1.0
    rt[:, 5] = 128.0 - i
    rt[:, 6] = 127.0 - p
    rt[:, 7] = p + 0.0 * i
    c["rtab"] = rt
    a = np.arange(128)
    ph = 2 * np.pi * np.outer(a, a) / 128
    c["dftc"] = (np.concatenate([np.cos(ph), np.sin(ph)], 1) / np.sqrt(128)).astype(ml_dtypes.bfloat16)
    for nm, L in (("dftL", 2048), ("dftS", 256)):
        a = np.arange(L)
        ph = 2 * np.pi * (np.outer(a, a) % L) / L
        c[nm] = np.stack([np.cos(ph), -np.sin(ph)]).astype(np.float32) / np.sqrt(L)
        c[nm] = c[nm].astype(ml_dtypes.bfloat16)
    return c


def make_in_maps(inputs):
    f = lambda a: np.ascontiguousarray(np.asarray(a, dtype=np.float32))
    cst = _consts()
    shared = {}
    for nm in ("w_ada", "ffn1_w_gate", "ffn1_w_up", "ffn1_w_down", "ffn2_w_gate", "ffn2_w_up", "ffn2_w_down", "w_in",
               "w_branch_fourier", "w_branch_attn", "w_branch_ret", "w_merge_gate", "w_out"):
        shared[nm] = f(inputs[nm])
    shared["b_ada_l"] = f(np.transpose(np.asarray(inputs["b_ada"]).reshape(DEPTH, 144, 128), (2, 0, 1)))
    g = np.stack([inputs["ffn1_norm"][0], inputs["mix_norm"][0], inputs["ffn2_norm"][0],
                  inputs["ffn1_norm"][1], inputs["mix_norm"][1], inputs["ffn2_norm"][1], inputs["final_norm"]])
    shared["gains"] = f(np.transpose(np.asarray(g).reshape(7, KC, 128), (2, 0, 1)))
    shared["qkn"] = f(np.stack([np.asarray(inputs["q_norm"]).T, np.asarray(inputs["k_norm"]).T], axis=2))
    shared["rdec"] = f(np.broadcast_to(np.asarray(inputs["ret_decay"]).reshape(1, DEPTH * 8), (128, DEPTH * 8)))
    shared["bmg"] = f(np.transpose(np.asarray(inputs["b_merge_gate"]).reshape(DEPTH, 48, 128), (2, 0, 1)))
    shared.update(cst)
    maps = []
    x = np.asarray(inputs["x"])
    c = np.asarray(inputs["c"])
    ctx = np.asarray(inputs["ctx"])
    cc = np.asarray(inputs["c_ctx"])
    for i in range(NCORES):
        m = dict(shared)
        m["xin"] = f(x[2 * i:2 * i + 2])
        m["ctxin"] = f(ctx[2 * i:2 * i + 2])
        cat = np.stack([c[2 * i], c[2 * i + 1], cc], axis=1)
        m["ccat"] = f(np.transpose(cat.reshape(KC, 128, 3), (1, 0, 2)))
        maps.append(m)
    return maps


def kernel(**inputs):
    nc = build_program()
    maps = make_in_maps(inputs)
    res = run_bass_kernel_spmd(nc, maps, core_ids=list(range(NCORES)))
    return np.concatenate([r["out"] for r in res.results], axis=0).astype(np.float32)
```
